# Optimizing a Trainium2 kernel written in Bass

```python
import math
import jax, jax.numpy as jnp
from jax import lax
import numpy as np

D_MODEL = 1024
BATCH = 16
SEQ = 2048
DEPTH = 1

D_FF = 2816
CHUNK = 128
D_A = 512
SGU_GROUPS = 4
SGU_GROUP_DIM = D_A // SGU_GROUPS
HEAD_DIM = 64
N_Q_HEADS = 8
N_KV_HEADS = 2
Q_PER_KV = N_Q_HEADS // N_KV_HEADS
WINDOW = 128
BLOCK = 128
D_B = N_Q_HEADS * HEAD_DIM
SPLITS = (D_A, D_A, D_B, N_KV_HEADS * HEAD_DIM, N_KV_HEADS * HEAD_DIM, D_MODEL, D_MODEL)
IN_COLS = sum(SPLITS)
N_MOD = 9
EPS = 1e-6
NEG = -1e30

kernel_name = "hybrid_gmlp_swa_sink_macaron_adaln"


def rms_norm(x, g):
    xf = x.astype(jnp.float32)
    y = xf * lax.rsqrt(jnp.mean(xf * xf, axis=-1, keepdims=True) + EPS)
    return (y * g.astype(jnp.float32)).astype(x.dtype)


def layer_norm(x, g, b):
    xf = x.astype(jnp.float32)
    mu = jnp.mean(xf, axis=-1, keepdims=True)
    var = jnp.mean(jnp.square(xf - mu), axis=-1, keepdims=True)
    y = (xf - mu) * lax.rsqrt(var + EPS)
    return (y * g.astype(jnp.float32) + b.astype(jnp.float32)).astype(x.dtype)


def modulate(xn, shift, scale):
    return xn * (1 + scale[:, None, :]) + shift[:, None, :]


def swiglu(x, w_gate, w_up, w_down):
    return (jax.nn.silu(x @ w_gate) * (x @ w_up)) @ w_down


def gmlp_sgu(u, v, g_ln, b_ln, w_s, b_s):
    bsz, seq, _ = v.shape
    n_chunks = seq // CHUNK
    v = layer_norm(v, g_ln, b_ln)
    vg = v.reshape(bsz, n_chunks, CHUNK, SGU_GROUPS, SGU_GROUP_DIM)
    causal = jnp.tril(jnp.ones((CHUNK, CHUNK), dtype=bool))
    ws = jnp.where(causal[None], w_s, jnp.zeros_like(w_s))
    z = jnp.einsum('gts,bnsgc->bntgc', ws, vg)
    z = z + b_s.T[None, None, :, :, None]
    return u * z.reshape(bsz, seq, D_A)


def sliding_window_attention(q, k, v, g_q, g_k, sinks):
    bsz, seq = q.shape[0], q.shape[1]
    nb = seq // BLOCK
    q = rms_norm(q, g_q)
    k = rms_norm(k, g_k)
    pad = ((0, 0), (BLOCK, 0), (0, 0), (0, 0))
    k_prev = jnp.pad(k, pad)[:, :seq]
    v_prev = jnp.pad(v, pad)[:, :seq]
    kb = jnp.concatenate([k_prev.reshape(bsz, nb, BLOCK, N_KV_HEADS, HEAD_DIM),
                          k.reshape(bsz, nb, BLOCK, N_KV_HEADS, HEAD_DIM)], axis=2)
    vb = jnp.concatenate([v_prev.reshape(bsz, nb, BLOCK, N_KV_HEADS, HEAD_DIM),
                          v.reshape(bsz, nb, BLOCK, N_KV_HEADS, HEAD_DIM)], axis=2)
    qb = q.reshape(bsz, nb, BLOCK, N_KV_HEADS, Q_PER_KV, HEAD_DIM)
    scores = jnp.einsum('bnqhgd,bnkhd->bnhgqk', qb, kb).astype(jnp.float32)
    scores = scores * (HEAD_DIM ** -0.5)
    blk = jnp.arange(nb)[:, None, None]
    qpos = blk * BLOCK + jnp.arange(BLOCK)[None, :, None]
    kpos = (blk - 1) * BLOCK + jnp.arange(2 * BLOCK)[None, None, :]
    diff = qpos - kpos
    valid = (diff >= 0) & (diff < WINDOW) & (kpos >= 0)
    scores = jnp.where(valid[None, :, None, None], scores, NEG)
    sink = jnp.broadcast_to(
        sinks.astype(jnp.float32).reshape(1, 1, N_KV_HEADS, Q_PER_KV, 1, 1),
        scores.shape[:-1] + (1,))
    probs = jax.nn.softmax(jnp.concatenate([scores, sink], axis=-1), axis=-1)[..., :-1]
    out = jnp.einsum('bnhgqk,bnkhd->bnqhgd', probs.astype(v.dtype), vb)
    return out.reshape(bsz, seq, D_B)


def setup_inputs(seed: int = 0) -> dict:
    key = jax.random.key(seed)
    ks = jax.random.split(key, 26)
    f32 = jnp.float32
    L = DEPTH

    def nrm(k, shape, fan_in):
        return jax.random.normal(k, shape, f32) * (fan_in ** -0.5)

    def gain(k, shape):
        return 1.0 + 0.02 * jax.random.normal(k, shape, f32)

    return {
        "x": jax.random.normal(ks[0], (BATCH, SEQ, D_MODEL), f32),
        "c": jax.random.normal(ks[1], (BATCH, D_MODEL), f32),
        "w_ada": nrm(ks[2], (L, D_MODEL, N_MOD * D_MODEL), D_MODEL) * 0.5,
        "b_ada": 0.01 * jax.random.normal(ks[3], (L, N_MOD * D_MODEL), f32),
        "g_norm1": gain(ks[4], (L, D_MODEL)),
        "ffn1_w_gate": nrm(ks[5], (L, D_MODEL, D_FF), D_MODEL),
        "ffn1_w_up": nrm(ks[6], (L, D_MODEL, D_FF), D_MODEL),
        "ffn1_w_down": nrm(ks[7], (L, D_FF, D_MODEL), D_FF),
        "g_norm2": gain(ks[8], (L, D_MODEL)),
        "w_in": nrm(ks[9], (L, D_MODEL, IN_COLS), D_MODEL),
        "g_sgu_ln": gain(ks[10], (L, D_A)),
        "b_sgu_ln": 0.01 * jax.random.normal(ks[11], (L, D_A), f32),
        "w_spatial": nrm(ks[12], (L, SGU_GROUPS, CHUNK, CHUNK), CHUNK),
        "b_spatial": 1.0 + 0.02 * jax.random.normal(ks[13], (L, SGU_GROUPS, CHUNK), f32),
        "g_q": gain(ks[14], (L, HEAD_DIM)),
        "g_k": gain(ks[15], (L, HEAD_DIM)),
        "attn_sinks": jax.random.normal(ks[16], (L, N_Q_HEADS), f32),
        "w_branch_a": nrm(ks[17], (L, D_A, D_MODEL), D_A),
        "w_branch_b": nrm(ks[18], (L, D_B, D_MODEL), D_B),
        "w_out": nrm(ks[19], (L, D_MODEL, D_MODEL), D_MODEL),
        "g_norm3": gain(ks[20], (L, D_MODEL)),
        "ffn2_w_gate": nrm(ks[21], (L, D_MODEL, D_FF), D_MODEL),
        "ffn2_w_up": nrm(ks[22], (L, D_MODEL, D_FF), D_MODEL),
        "ffn2_w_down": nrm(ks[23], (L, D_FF, D_MODEL), D_FF),
    }


def reference(x, c, w_ada, b_ada, g_norm1, ffn1_w_gate, ffn1_w_up, ffn1_w_down,
              g_norm2, w_in, g_sgu_ln, b_sgu_ln, w_spatial, b_spatial, g_q, g_k,
              attn_sinks, w_branch_a, w_branch_b, w_out, g_norm3,
              ffn2_w_gate, ffn2_w_up, ffn2_w_down):
    bsz, seq, _ = x.shape
    cond = jax.nn.silu(c)
    split_idx = list(np.cumsum(SPLITS)[:-1])
    h = x
    for l in range(DEPTH):
        mods = cond @ w_ada[l] + b_ada[l]
        sh1, sc1, ga1, sh2, sc2, ga2, sh3, sc3, ga3 = jnp.split(mods, N_MOD, axis=-1)

        xn = modulate(rms_norm(h, g_norm1[l]), sh1, sc1)
        h = h + 0.5 * ga1[:, None, :] * swiglu(xn, ffn1_w_gate[l], ffn1_w_up[l], ffn1_w_down[l])

        xn = modulate(rms_norm(h, g_norm2[l]), sh2, sc2)
        proj = xn @ w_in[l]
        a_u, a_v, q, k, v, gate_a, gate_b = jnp.split(proj, split_idx, axis=-1)

        y_a = gmlp_sgu(jax.nn.gelu(a_u), jax.nn.gelu(a_v), g_sgu_ln[l], b_sgu_ln[l],
                       w_spatial[l], b_spatial[l]) @ w_branch_a[l]

        y_b = sliding_window_attention(
            q.reshape(bsz, seq, N_Q_HEADS, HEAD_DIM),
            k.reshape(bsz, seq, N_KV_HEADS, HEAD_DIM),
            v.reshape(bsz, seq, N_KV_HEADS, HEAD_DIM),
            g_q[l], g_k[l], attn_sinks[l]) @ w_branch_b[l]

        merged = jax.nn.sigmoid(gate_a) * y_a + jax.nn.sigmoid(gate_b) * y_b
        h = h + ga2[:, None, :] * (merged @ w_out[l])

        xn = modulate(rms_norm(h, g_norm3[l]), sh3, sc3)
        h = h + 0.5 * ga3[:, None, :] * swiglu(xn, ffn2_w_gate[l], ffn2_w_up[l], ffn2_w_down[l])
    return h
```

```python
import numpy as np
from contextlib import ExitStack
import concourse.bass as bass
import concourse.mybir as mybir
from concourse.bass_utils import run_bass_kernel_spmd

F32 = mybir.dt.float32
BF16 = mybir.dt.bfloat16
AF = mybir.ActivationFunctionType
ALU = mybir.AluOpType
AX = mybir.AxisListType

NCORES = 8
D = 1024
S = 2048
DFF = 2816
NK = 8
NT = 4
NB = 16
TT = 512
HC = 11
NSLOT = 7
EPS = 1e-6

W8_ADA = 0
W8_F1 = 72
W8_F2 = 116
W8_IN = 160
W8_OUT = 185
W8_N = 193

PV_BADA = 0
PV_G1 = 72
PV_GQ = 96
PV_GK = 97
PV_SINK = 98
PV_COND = 102
PV_N = 118

SC_U = 0
SC_VLN = 8192
SC_Q = 16384
SC_KT = 24576
SC_V = 26624
SC_WTM = 28672
SC_N = 34816
MG_OFF = [8192, 10240, 12288, 14336, 24576, 26624, 28672, 30720]


class _Eng:
    def __init__(self):
        self.ops = []
        self.cnt = 0
        self.seen = {}
        self.sem = None


class Prog:
    def __init__(self, nc, es):
        self.nc = nc
        self.es = es
        self.E = {n: _Eng() for n in ("pe", "act", "dve", "pool", "sp")}
        for n, e in self.E.items():
            e.sem = es.enter_context(nc.semaphore("s_" + n))
        self.lastw = {}
        self.readers = {}
        self.nsem = 0
        self.last_pe = None

    def newsem(self, name):
        return self.es.enter_context(self.nc.semaphore(name))

    def _waits(self, e, deps):
        waits = {}
        for d in deps:
            if d is None:
                continue
            s, v = d
            if v > waits.get(s, 0):
                waits[s] = v
        wl = []
        for s, v in waits.items():
            if e.seen.get(s, 0) >= v:
                continue
            e.seen[s] = v
            wl.append((s, v))
        return wl

    def _deps(self, e, R, W, extra):
        deps = list(extra)
        for k in R:
            t = self.lastw.get(k)
            if t is not None:
                deps.append(t)
        pe_sem = self.E["pe"].sem
        for k in W:
            t = self.lastw.get(k)
            if t is not None and not (t[0] is e.sem and e.sem is pe_sem):
                deps.append(t)
            rd = self.readers.get(k)
            if rd:
                for s, v in rd.items():
                    if not (s is e.sem and e.sem is pe_sem):
                        deps.append((s, v))
        return deps

    def _record(self, tok, R, W):
        for k in R:
            rd = self.readers.setdefault(k, {})
            if rd.get(tok[0], 0) < tok[1]:
                rd[tok[0]] = tok[1]
        for k in W:
            self.lastw[k] = tok
            self.readers[k] = {}

    def op(self, en, fn, R=(), W=(), extra=()):
        e = self.E[en]
        wl = self._waits(e, self._deps(e, R, W, extra))
        e.cnt += 1
        tok = (e.sem, e.cnt)
        sem_e = e.sem

        def thunk(eng, wl=wl, fn=fn):
            for s, v in wl:
                eng.wait_ge(s, v)
            fn(eng).then_inc(sem_e, 1)
        e.ops.append(thunk)
        self._record(tok, R, W)
        if en == "pe":
            self.last_pe = tok
        return tok

    def dma(self, en, out, in_, semc, R=(), W=(), extra=()):
        e = self.E[en]
        wl = self._waits(e, self._deps(e, R, W, extra))
        semc[1] += 16
        tok = (semc[0], semc[1])
        sem = semc[0]

        def thunk(eng, wl=wl):
            for s, v in wl:
                eng.wait_ge(s, v)
            eng.dma_start(out=out, in_=in_).then_inc(sem, 16)
        e.ops.append(thunk)
        self._record(tok, R, W)
        return tok

    def wait(self, en, toks):
        e = self.E[en]
        wl = self._waits(e, toks)

        def thunk(eng, wl=wl):
            for s, v in wl:
                eng.wait_ge(s, v)
        e.ops.append(thunk)

    def run(self):
        nc = self.nc
        E = self.E
        with nc.Block() as block:
            @block.tensor
            def _(eng):
                for t in E["pe"].ops:
                    t(eng)

            @block.scalar
            def _(eng):
                for t in E["act"].ops:
                    t(eng)

            @block.vector
            def _(eng):
                for t in E["dve"].ops:
                    t(eng)

            @block.gpsimd
            def _(eng):
                for t in E["pool"].ops:
                    t(eng)

            @block.sync
            def _(eng):
                for t in E["sp"].ops:
                    t(eng)


class Rot:
    def __init__(self, items):
        self.items = items
        self.i = 0

    def next(self):
        x = self.items[self.i % len(self.items)]
        self.i += 1
        return x


def build_program(nseq=2, stop=99):
    nc = bass.Bass("TRN2", target_bir_lowering=False)
    dt = lambda n, s: nc.dram_tensor(n, s, F32, kind="ExternalInput").ap()
    xT_d = dt("xT", [nseq, 128, NK * S])
    w8_d = dt("w8", [W8_N, 128, 1024])
    w11_d = dt("w11", [32, 128, HC * 128])
    w4_d = dt("w4", [16, 128, 512])
    wtm_d = dt("wtm", [128, 8 * 768])
    pvec_d = dt("pvec", [128, PV_N])
    lngb_d = dt("lngb", [128, 1024])
    wst_d = dt("wst", [128, 512])
    bsr_d = dt("bsr", [1, 512])
    cmat_d = dt("cmat", [128, 512])
    out_d = nc.dram_tensor("out", [nseq, 128, NK * S], F32, kind="ExternalOutput").ap()

    with ExitStack() as es:
        P = Prog(nc, es)
        sb = lambda n, s, d: es.enter_context(nc.sbuf_tensor(n, s, d))
        h = sb("h", [128, NK, S], F32)
        xn = sb("xn", [128, NK, S], BF16)
        scr = sb("scr", [128, SC_N], BF16)
        ringbuf = sb("ringbuf", [128, NSLOT * 1024], BF16)
        ring = [ringbuf[:, i * 1024:(i + 1) * 1024] for i in range(NSLOT)]
        sqr = [sb(f"sq{i}", [128, TT], BF16) for i in range(6)]
        rsr = [sb(f"rs{i}", [128, TT], F32) for i in range(2)]
        tfr = [sb(f"tf{i}", [128, TT], F32) for i in range(4)]
        ksqr = [sb(f"ksq{i}", [128, 128], F32) for i in range(1)]
        cmat = sb("cmat_sb", [128, 512], BF16)
        pvec = sb("pvec_sb", [128, PV_N], F32)
        lngb = sb("lngb_sb", [128, 1024], F32)
        wst = sb("wst_sb", [128, 512], BF16)
        bsr = sb("bsr_sb", [1, 512], BF16)
        cond = sb("cond", [128, 16], BF16)
        mods = sb("mods", [128, 144], F32)
        Asb = sb("Asb", [128, 48], F32)
        Gsb = sb("Gsb", [128, 48], F32)
        gqk = sb("gqk", [128, 1], F32)
        esink = sb("esink", [128, 4], F32)
        epst = sb("epst", [128, 2], F32)
        st6 = sb("st6", [128, NB * 6], F32)
        mv = sb("mv", [128, NB * 2], F32)
        ssk = sb("ssk", [128, NB * 2], F32)
        rstdv = sb("rstdv", [128, NB], F32)
        rk8 = sb("rk8", [128, NB * 2], F32)
        nmr = sb("nmr", [128, NB], F32)
        bank = [es.enter_context(nc.psum_tensor(f"bank{i}", [128, TT], F32)) for i in range(8)]

        ones = cmat[:, 0:128]
        bd = cmat[:, 128:256]
        mprev = cmat[:, 256:384]
        mcur = cmat[:, 384:512]

        hT = scr[:, 0:HC * S].rearrange("p (k t) -> p k t", t=S)
        u3 = scr[:, SC_U:SC_U + 4 * S].rearrange("p (g t) -> p g t", t=S)
        vln = scr[:, SC_VLN:SC_VLN + NB * 512].rearrange("p (n c) -> p n c", c=512)
        q3 = scr[:, SC_Q:SC_Q + 4 * S].rearrange("p (g t) -> p g t", t=S)
        kT = scr[:, SC_KT:SC_KT + S]
        vtm = scr[:, SC_V:SC_V + NB * 128].rearrange("p (n c) -> p n c", c=128)
        wtm = scr[:, SC_WTM:SC_WTM + 8 * 768].rearrange("p (k c) -> p k c", c=768)
        PT = [scr[:, SC_WTM + i * 2048:SC_WTM + (i + 1) * 2048].rearrange("p (j t) -> p j t", t=512) for i in range(2)]
        mg = [scr[:, o:o + S] for o in MG_OFF]

        ring_sem = [[P.newsem(f"ringsem{i}"), 0] for i in range(NSLOT)]
        ld_sem = [[P.newsem(f"ldsem{i}"), 0] for i in range(2 * NK)]
        st_sem = [[P.newsem(f"stsem{i}"), 0] for i in range(2 * NK)]
        misc_sem = [[P.newsem(f"miscsem{i}"), 0] for i in range(7)]
        ring_i = [0]
        ring_gen = [0] * NSLOT

        class RW:
            pass

        def ring_load(src, nk):
            i = ring_i[0] % NSLOT
            ring_i[0] += 1
            ring_gen[i] += 1
            dst = ringbuf[:, i * 1024:i * 1024 + nk * 128]
            tok = P.dma("pool", dst, src, ring_sem[i], W=[("ring", i)])
            r = RW()
            r.tok = tok
            r.v = dst.rearrange("p (k c) -> p k c", c=128)
            r.i = i
            r.gen = ring_gen[i]
            return r

        def rk(r):
            assert ring_gen[r.i] == r.gen, "ring slot reloaded before use"
            return ("ring", r.i)

        def ring_load2(src, nk):
            i = ring_i[0] % NSLOT
            if i == NSLOT - 1:
                ring_i[0] += 1
                i = 0
            ring_i[0] += 2
            ring_gen[i] += 1
            ring_gen[i + 1] += 1
            dst = ringbuf[:, i * 1024:i * 1024 + nk * 128]
            tok = P.dma("pool", dst, src, ring_sem[i], W=[("ring", i), ("ring", i + 1)])
            r = RW()
            r.tok = tok
            r.v = dst.rearrange("p (k c) -> p k c", c=128)
            r.i = i
            r.gen = ring_gen[i]
            r.gen2 = ring_gen[i + 1]
            return r

        def rk2(r):
            assert ring_gen[r.i] == r.gen and ring_gen[r.i + 1] == r.gen2, "ring slot pair reloaded before use"
            return [("ring", r.i), ("ring", r.i + 1)]

        class BG:
            def __init__(self):
                self.steps = []

            def add(self, fns):
                self.steps.extend(fns)

            def step(self, n=1):
                for _ in range(n):
                    if self.steps:
                        self.steps.pop(0)()

            def flush(self):
                while self.steps:
                    self.steps.pop(0)()
        bg = BG()

        sq_rot = Rot([0, 1, 2, 3])
        qsq_rot = Rot([4, 5])
        rs_rot = Rot(list(range(2)))
        tf_rot = Rot(list(range(4)))
        ksq_rot = Rot([0])

        def sl(tt):
            return slice(tt * TT, (tt + 1) * TT)

        def bl(n):
            return slice(n * 128, (n + 1) * 128)

        def gemm8(B, w, tt, src=None):
            def f(e):
                for k in range(NK):
                    ins = e.matmul(bank[B][:], w.v[:, k, :], xn[:, k, sl(tt)], start=(k == 0), stop=(k == NK - 1))
                return ins
            return P.op("pe", f, R=[rk(w)] + [("xn", k, tt) for k in range(NK)], W=[("ps", B)])

        P.dma("sp", pvec[:], pvec_d, misc_sem[0], W=[("pvec",)])
        P.dma("sp", lngb[:], lngb_d, misc_sem[1], W=[("lngb",)])
        P.dma("sp", tfr[3][:], wst_d, misc_sem[2], W=[("tf", 3)])
        P.dma("pool", cmat[:], cmat_d, misc_sem[3], W=[("cmat",)])
        P.dma("pool", bsr[:], bsr_d, misc_sem[4], W=[("bsr",)])
        P.op("dve", lambda e: e.memset(epst[:, 0:1], EPS), W=[("eps0",)])
        P.op("dve", lambda e: e.memset(epst[:, 1:2], 64 * EPS), W=[("eps1",)])
        P.op("dve", lambda e: e.tensor_tensor(wst[:].rearrange("p (g t) -> p g t", g=4), tfr[3][:].rearrange("p (g t) -> p g t", g=4),
                                              mcur.unsqueeze(1).broadcast_to([128, 4, 128]), ALU.mult),
             R=[("tf", 3), ("cmat",)], W=[("wst",)])
        P.op("dve", lambda e: e.tensor_tensor(gqk[:], pvec[:, PV_GQ:PV_GQ + 1], pvec[:, PV_GK:PV_GK + 1], ALU.mult),
             R=[("pvec",)], W=[("gqk",)])
        P.op("act", lambda e: e.activation(esink[:], pvec[:, PV_SINK:PV_SINK + 4], AF.Exp), R=[("pvec",)], W=[("esink",)])
        P.op("act", lambda e: e.activation(cond[:], pvec[:, PV_COND:PV_COND + 16], AF.Silu), R=[("pvec",)], W=[("cond",)])

        cond3 = cond[:].rearrange("p (k b) -> p k b", b=2)
        mods3 = mods[:].rearrange("p (j b) -> p j b", b=2)

        def mods_chunk(j):
            w = ring_load(w8_d[W8_ADA + j], 8)

            def f(e):
                for k in range(8):
                    ins = e.matmul(bank[7][:, 2 * j:2 * j + 2], w.v[:, k, :], cond3[:, k, :], start=(k == 0), stop=(k == 7))
                return ins
            P.op("pe", f, R=[rk(w), ("cond",)], W=[("ps7", j)])
            return w.tok

        def mods_finish(j0, j1, As, Gs):
            P.op("dve", lambda e: e.tensor_tensor(mods3[:, j0:j1, :],
                                                  bank[7][:, 2 * j0:2 * j1].rearrange("p (j b) -> p j b", b=2),
                                                  pvec[:, PV_BADA + j0:PV_BADA + j1].unsqueeze(2).broadcast_to([128, j1 - j0, 2]), ALU.add),
                 R=[("ps7", j) for j in range(j0, j1)] + [("pvec",)], W=[("mods", j) for j in range(j0, j1)])
            for s3 in As:
                for b in range(2):
                    o = (s3 * 2 + b) * 8
                    P.op("dve", lambda e, s3=s3, b=b, o=o: e.scalar_tensor_tensor(
                        Asb[:, o:o + 8], mods3[:, (3 * s3 + 1) * 8:(3 * s3 + 2) * 8, b], 1.0,
                        pvec[:, PV_G1 + 8 * s3:PV_G1 + 8 * s3 + 8], ALU.add, ALU.mult),
                        R=[("mods", j) for j in range((3 * s3 + 1) * 8, (3 * s3 + 2) * 8)] + [("pvec",)], W=[("A", s3, b)])
            for s3 in Gs:
                for b in range(2):
                    o = (s3 * 2 + b) * 8
                    P.op("dve", lambda e, s3=s3, b=b, o=o: e.tensor_scalar(
                        Gsb[:, o:o + 8], mods3[:, (3 * s3 + 2) * 8:(3 * s3 + 3) * 8, b], (1.0 if s3 == 1 else 0.5), None, ALU.mult),
                        R=[("mods", j) for j in range((3 * s3 + 2) * 8, (3 * s3 + 3) * 8)], W=[("G", s3, b)])

        mods_bg = ([(lambda j=j: mods_chunk(j)) for j in range(16, 24)] + [lambda: mods_finish(16, 24, [], [0])] +
                   [(lambda j=j: mods_chunk(j)) for j in range(24, 72)] + [lambda: mods_finish(24, 72, [1, 2], [1, 2])])

        def Acol(s3, b, k):
            o = (s3 * 2 + b) * 8 + k
            return Asb[:, o:o + 1]

        def Gcol(s3, b, k):
            o = (s3 * 2 + b) * 8 + k
            return Gsb[:, o:o + 1]

        def Bcol(s3, b, k):
            return mods3[:, 3 * s3 * 8 + k, b:b + 1]

        def mk_ln_step(tt, box, bk):
            def f():
                r = rs_rot.next()
                box["r"] = r
                P.op("act", lambda e: e.activation(rsr[r][:], bank[bk][:], AF.Ln, bias=epst[:, 0:1], scale=1.0 / D),
                     R=[("ps", bk), ("eps0",)], W=[("rs", r)])
                P.op("act", lambda e: e.activation(rsr[r][:], rsr[r][:], AF.Exp, scale=-0.5), R=[("rs", r)], W=[("rs", r)])
            return f

        def mk_aff_step(s3, b, k0, tt, box, dve_only=False):
            def f():
                r = box["r"]
                for k in (k0, k0 + 1):
                    t = tf_rot.next()
                    P.op("dve", lambda e, t=t, k=k: e.scalar_tensor_tensor(tfr[t][:], h[:, k, sl(tt)], Acol(s3, b, k), rsr[r][:],
                                                                       ALU.mult, ALU.mult),
                         R=[("h", k, tt), ("rs", r), ("A", s3, b)], W=[("tf", t)])
                    if dve_only:
                        P.op("dve", lambda e, t=t, k=k: e.tensor_scalar(xn[:, k, sl(tt)], tfr[t][:], Bcol(s3, b, k), None, ALU.add),
                             R=[("tf", t), ("mods", 3 * s3 * 8 + k)], W=[("xn", k, tt)])
                    else:
                        P.op("act", lambda e, t=t, k=k: e.activation(xn[:, k, sl(tt)], tfr[t][:], AF.Identity, bias=Bcol(s3, b, k), scale=1.0),
                             R=[("tf", t), ("mods", 3 * s3 * 8 + k)], W=[("xn", k, tt)])
            return f

        def emit_square(i, k, tt):
            if k % 2 == 0:
                P.op("act", lambda e: e.activation(sqr[i][:], h[:, k, sl(tt)], AF.Square), R=[("h", k, tt)], W=[("sq", i)])
            else:
                P.op("dve", lambda e: e.tensor_tensor(sqr[i][:], h[:, k, sl(tt)], h[:, k, sl(tt)], ALU.mult), R=[("h", k, tt)], W=[("sq", i)])

        def norm_steps(s3, b, tiles, bk=6):
            steps = []
            boxes = {tt: {} for tt in tiles}
            for tt in tiles:
                pend = []

                def emit_mm1(tt=tt, pend=pend):
                    i, k = pend.pop(0)
                    P.op("pe", lambda e: e.matmul(bank[bk][:], ones, sqr[i][:], start=(k == 0), stop=(k == NK - 1)),
                         R=[("sq", i), ("cmat",)], W=[("ps", bk)])

                def sq_step(k, tt=tt, pend=pend, emit_mm1=emit_mm1):
                    def f():
                        if len(pend) >= 2:
                            emit_mm1()
                        i = sq_rot.next()
                        emit_square(i, k, tt)
                        pend.append((i, k))
                    return f
                for k in range(NK):
                    steps.append(sq_step(k))
                steps.append(emit_mm1)
                steps.append(emit_mm1)
                steps.append(mk_ln_step(tt, boxes[tt], bk))
            for tt in tiles:
                for k0 in range(0, NK, 2):
                    steps.append(mk_aff_step(s3, b, k0, tt, boxes[tt]))
            return steps

        class NormInc:
            def __init__(self, s3, b, tiles, banks):
                self.s3, self.b, self.tiles = s3, b, tiles
                self.B = dict(zip(tiles, banks))
                self.pend = []
                self.cnt = {tt: 0 for tt in tiles}

            def _mm(self):
                while self.pend:
                    i, k, tt = self.pend.pop(0)
                    n = self.cnt[tt]
                    self.cnt[tt] += 1
                    bk = self.B[tt]
                    P.op("pe", lambda e, i=i, n=n, bk=bk: e.matmul(bank[bk][:], ones, sqr[i][:], start=(n == 0), stop=(n == NK - 1)),
                         R=[("sq", i), ("cmat",)], W=[("ps", bk)])

            def feed(self, k, tt):
                self._mm()
                i = qsq_rot.next()
                emit_square(i, k, tt)
                self.pend.append((i, k, tt))

            def finish_steps(self):
                steps = [self._mm]
                boxes = {tt: {} for tt in self.tiles}
                for tt in self.tiles:
                    steps.append(mk_ln_step(tt, boxes[tt], self.B[tt]))
                for tt in self.tiles:
                    for k0 in range(0, NK, 2):
                        steps.append(mk_aff_step(self.s3, self.b, k0, tt, boxes[tt], dve_only=(self.s3 == 1)))
                return steps

        TA = (0, 1)
        TB = (2, 3)

        def ffn(f, b, tailA, tailB, extra_bg=None, inc=None, after_dc=None):
            s3 = 0 if f == 0 else 2
            gb = Rot([0, 1])
            ub = Rot([2, 3])
            db = Rot([4, 5, 0, 1, 2])
            base8 = W8_F1 if f == 0 else W8_F2

            def g1_iter(wg, wu, hc, tt):
                G = gb.next()
                U = ub.next()
                gemm8(G, wg, tt)
                gemm8(U, wu, tt)
                t = tf_rot.next()
                P.op("act", lambda e: e.activation(tfr[t][:], bank[G][:], AF.Silu), R=[("ps", G)], W=[("tf", t)])
                P.op("dve", lambda e: e.tensor_tensor(hT[:, hc, sl(tt)], tfr[t][:], bank[U][:], ALU.mult),
                     R=[("tf", t), ("ps", U)], W=[("hT", hc, tt)])

            def wd_load(half, dc):
                return ring_load2(w11_d[(f * 2 + half) * 8 + dc], HC)

            def g2_iter(wd, dc, tt):
                assert ("G", s3, b) in P.lastw
                Dk = db.next()

                def fd(e):
                    for kk in range(HC):
                        ins = e.matmul(bank[Dk][:], wd.v[:, kk, :], hT[:, kk, sl(tt)], start=(kk == 0), stop=(kk == HC - 1))
                    return ins
                P.op("pe", fd, R=rk2(wd) + [("hT", kk, tt) for kk in range(HC)], W=[("ps", Dk)])
                P.op("dve", lambda e: e.scalar_tensor_tensor(
                    h[:, dc, sl(tt)], bank[Dk][:], Gcol(s3, b, dc), h[:, dc, sl(tt)], ALU.mult, ALU.add),
                    R=[("ps", Dk), ("h", dc, tt), ("G", s3, b)], W=[("h", dc, tt)])

            for half in range(2):
                hcs = list(range(HC))
                if half == 0:
                    NST = 3
                    ws = []
                    for hc in range(NST):
                        ws.append((ring_load(w8_d[base8 + hc * 2], 8), ring_load(w8_d[base8 + hc * 2 + 1], 8)))
                    for tt in TA:
                        for hc in range(NST):
                            bg.step(5)
                            g1_iter(ws[hc][0], ws[hc][1], hc, tt)
                    bg.flush()
                    for tt in TB:
                        for hc in range(NST):
                            g1_iter(ws[hc][0], ws[hc][1], hc, tt)
                    hcs = list(range(NST, HC))
                    if extra_bg:
                        bg.add(extra_bg)
                for hc in hcs:
                    ghc = half * HC + hc
                    wg = ring_load(w8_d[base8 + ghc * 2], 8)
                    wu = ring_load(w8_d[base8 + ghc * 2 + 1], 8)
                    for tt in range(NT):
                        g1_iter(wg, wu, hc, tt)
                        if tt % 2 == 1 or (tt == 2 and len(bg.steps) > 24):
                            bg.step(1)
                if half == 0:
                    for dc in range(NK):
                        wd = wd_load(half, dc)
                        for tt in range(NT):
                            g2_iter(wd, dc, tt)
                            if tt % 2 == 1:
                                bg.step(1)
                else:
                    bg.flush()
                    for tiles, tail in ((TA, tailA), (TB, tailB)):
                        ninc = NormInc(inc, b, TB, (6, 7)) if (inc is not None and tiles is TB) else None
                        for dc in range(NK):
                            wd = wd_load(half, dc)
                            for tt in tiles:
                                g2_iter(wd, dc, tt)
                                if ninc:
                                    ninc.feed(dc, tt)
                                bg.step(2)
                            if after_dc:
                                after_dc(dc, 0 if tiles is TA else 1)
                        if after_dc:
                            after_dc(NK, 0 if tiles is TA else 1)
                            after_dc(NK + 1, 0 if tiles is TA else 1)
                        bg.flush()
                        if ninc:
                            bg.add(ninc.finish_steps())
                        elif tail:
                            bg.add(tail())

        def mixer(b, tailA, tailB, inc=None):
            ob = Rot([0, 1, 2, 4, 5])
            ob3 = Rot([0, 1, 4, 5])
            sb2 = Rot([2, 3])
            pend = []

            def q_stage2(g, tt, B, i):
                Sq = sb2.next()
                P.op("pe", lambda e: e.matmul(bank[Sq][:], bd, sqr[i][:], start=True, stop=True),
                     R=[("sq", i), ("cmat",)], W=[("ps", Sq)])
                r = tf_rot.next()
                P.op("act", lambda e: e.activation(tfr[r][:], bank[Sq][:], AF.Ln, bias=epst[:, 0:1], scale=1.0 / 64),
                     R=[("ps", Sq), ("eps0",)], W=[("tf", r)])
                P.op("act", lambda e: e.activation(tfr[r][:], tfr[r][:], AF.Exp, scale=-0.5), R=[("tf", r)], W=[("tf", r)])
                P.op("dve", lambda e: e.scalar_tensor_tensor(q3[:, g, sl(tt)], bank[B][:], gqk[:, 0:1], tfr[r][:], ALU.mult, ALU.mult),
                     R=[("ps", B), ("tf", r), ("gqk",)], W=[("q", g, n) for n in range(tt * 4, tt * 4 + 4)])

            wq = [ring_load(w8_d[W8_IN + g], 8) for g in range(5)]

            def m2a(tiles, nstep):
                it = [0]

                def bstep():
                    it[0] += 1
                    if nstep >= 1:
                        bg.step(nstep)
                    elif nstep > 0 and it[0] % 2 == 1:
                        bg.step(1)
                for tt in tiles:
                    for g in range(5):
                        B = ob3.next()
                        gemm8(B, wq[g], tt)
                        if g < 4:
                            i = qsq_rot.next()
                            P.op("act", lambda e, i=i, B=B: e.activation(sqr[i][:], bank[B][:], AF.Square), R=[("ps", B)], W=[("sq", i)])
                            bstep()
                            if pend:
                                q_stage2(*pend.pop())
                            pend.append((g, tt, B, i))
                        else:
                            P.op("act", lambda e, B=B, tt=tt: e.activation(kT[:, sl(tt)], bank[B][:], AF.Copy), R=[("ps", B)],
                                 W=[("kT", n) for n in range(tt * 4, tt * 4 + 4)])
                            bstep()
                q_stage2(*pend.pop())

            avb = Rot([0, 1])
            kvb = Rot([2, 3])
            st3 = st6[:].rearrange("p (n c) -> p n c", c=6)
            mv3 = mv[:].rearrange("p (n c) -> p n c", c=2)
            ssk3 = ssk[:].rearrange("p (n c) -> p n c", c=2)
            rk83 = rk8[:].rearrange("p (n c) -> p n c", c=2)
            def m2b(blocks, nstep):
              for n in blocks:
                tt = n // 4
                AV = avb.next()
                KV = kvb.next()

                def ftm(e, AV=AV, KV=KV, n=n):
                    for k in range(NK):
                        e.matmul(bank[AV][:], xn[:, k, bl(n)], wtm[:, k, 0:512], start=(k == 0), stop=(k == NK - 1))
                        ins = e.matmul(bank[KV][:, 0:256], xn[:, k, bl(n)], wtm[:, k, 512:768], start=(k == 0), stop=(k == NK - 1))
                    return ins
                P.op("pe", ftm, R=[("wtm",)] + [("xn", k, tt) for k in range(NK)], W=[("ps", AV), ("ps", KV)])
                P.op("act", lambda e, AV=AV, n=n: e.activation(vln[:, n, :], bank[AV][:], AF.Gelu_apprx_tanh), R=[("ps", AV)], W=[("vln", n)])
                ki = ksq_rot.next()
                P.op("act", lambda e, KV=KV, ki=ki: e.activation(ksqr[ki][:], bank[KV][:, 0:128], AF.Square), R=[("ps", KV)], W=[("ksq", ki)])
                P.op("act", lambda e, KV=KV, n=n: e.activation(vtm[:, n, :], bank[KV][:, 128:256], AF.Copy), R=[("ps", KV)], W=[("v", n)])
                P.op("dve", lambda e, n=n: e.bn_stats(st3[:, n, :], vln[:, n, :]), R=[("vln", n)], W=[("st", n)])
                P.op("dve", lambda e, n=n: e.bn_aggr(mv3[:, n, :], st3[:, n, :]), R=[("st", n)], W=[("mv", n)])
                P.op("dve", lambda e, n=n, ki=ki: e.reduce_sum(ssk3[:, n, :], ksqr[ki][:].rearrange("p (h d) -> p h d", h=2), AX.X),
                     R=[("ksq", ki)], W=[("ssk", n)])
                bg.step(nstep)

            m2a(TA, 0.5)
            m2b(range(0, NB // 2), 1)
            bg.flush()
            m2a(TB, 0)
            m2b(range(NB // 2, NB), 0)

            P.op("act", lambda e: e.activation(rstdv[:], mv3[:, :, 1], AF.Ln, bias=epst[:, 0:1], scale=1.0),
                 R=[("mv", n) for n in range(NB)] + [("eps0",)], W=[("rstdv",)])
            P.op("act", lambda e: e.activation(rstdv[:], rstdv[:], AF.Exp, scale=-0.5), R=[("rstdv",)], W=[("rstdv",)])
            P.op("act", lambda e: e.activation(rk8[:], ssk[:], AF.Ln, bias=epst[:, 1:2], scale=1.0),
                 R=[("ssk", n) for n in range(NB)] + [("eps1",)], W=[("rk8",)])
            P.op("act", lambda e: e.activation(rk8[:], rk8[:], AF.Exp, scale=-0.5), R=[("rk8",)], W=[("rk8",)])

            P.op("dve", lambda e: e.scalar_tensor_tensor(nmr[:], mv3[:, :, 0], -1.0, rstdv[:], ALU.mult, ALU.mult),
                 R=[("mv", n) for n in range(NB)] + [("rstdv",)], W=[("nmr",)])

            def ln_fin(n):
                def f():
                    P.op("act", lambda e: e.activation(vln[:, n, :], vln[:, n, :], AF.Identity, bias=nmr[:, n:n + 1], scale=rstdv[:, n:n + 1]),
                         R=[("vln", n), ("nmr",), ("rstdv",)], W=[("vln", n)])
                    P.op("dve", lambda e: e.tensor_tensor(vln[:, n, :], vln[:, n, :], lngb[:, 0:512], ALU.mult),
                         R=[("vln", n), ("lngb",)], W=[("vln", n)])
                    P.op("dve", lambda e: e.tensor_tensor(vln[:, n, :], vln[:, n, :], lngb[:, 512:1024], ALU.add),
                         R=[("vln", n), ("lngb",)], W=[("vln", n)])
                return f
            bg.add([ln_fin(n) for n in range(NB)])

            zb = Rot([4, 5])
            ub4 = Rot([0, 1, 2, 3])
            wst3 = wst[:].rearrange("p (g t) -> p g t", g=4)

            def sgu(n):
                assert ("vln", n) in P.lastw
                Z = zb.next()

                def fz(e):
                    e.matmul(bank[Z][:, 0:512], cmat[0:1, 0:128], bsr[0:1, 0:512], start=True, stop=False)
                    for g in range(4):
                        ins = e.matmul(bank[Z][:, g * 128:(g + 1) * 128], vln[:, n, g * 128:(g + 1) * 128], wst3[:, g, :], start=False, stop=(g == 3))
                    return ins
                P.op("pe", fz, R=[("vln", n), ("wst",), ("bsr",), ("cmat",)], W=[("ps", Z)])
                P.op("dve", lambda e: e.tensor_tensor(u3[:, :, bl(n)], bank[Z][:].rearrange("p (g t) -> p g t", g=4), u3[:, :, bl(n)], ALU.mult),
                     R=[("ps", Z)] + [("u", g, n) for g in range(4)], W=[("u", g, n) for g in range(4)])

            wu4 = [ring_load(w8_d[W8_IN + 5 + g], 8) for g in range(4)]
            for tt in range(NT):
                for g in range(4):
                    B = ub4.next()
                    gemm8(B, wu4[g], tt)
                    P.op("act", lambda e, B=B, g=g, tt=tt: e.activation(u3[:, g, sl(tt)], bank[B][:], AF.Gelu_apprx_tanh), R=[("ps", B)],
                         W=[("u", g, n) for n in range(tt * 4, tt * 4 + 4)])
                    bg.step(1)
                    if tt >= 1:
                        sgu((tt - 1) * 4 + g)
            bg.flush()
            for n in range(NB - 4, NB):
                sgu(n)

            def scores(n):
                kbs = [n - 1, n] if n > 0 else [n]
                slot = n % 2
                for hh in range(2):
                    for kb in kbs:
                        c = 1 if kb == n else 0
                        Sb = hh * 2 + c
                        j = hh * 2 + c

                        def fs(e, hh=hh, kb=kb, Sb=Sb):
                            for g in range(4):
                                ins = e.matmul(bank[Sb][:, g * 128:(g + 1) * 128], kT[hh * 64:(hh + 1) * 64, bl(kb)],
                                               q3[hh * 64:(hh + 1) * 64, g, bl(n)], start=True, stop=True)
                            return ins
                        P.op("pe", fs, R=[("kT", kb)] + [("q", g, n) for g in range(4)], W=[("ps", Sb)])
                        P.op("act", lambda e, Sb=Sb, j=j, kb=kb, hh=hh: e.activation(PT[slot][:, j, :], bank[Sb][:], AF.Exp, scale=rk83[:, kb, hh:hh + 1]),
                             R=[("ps", Sb), ("rk8",)], W=[("PT", slot, j)])
                        m = mcur if kb == n else mprev
                        P.op("dve", lambda e, j=j, m=m: e.tensor_tensor(PT[slot][:, j, :].rearrange("p (g t) -> p g t", g=4),
                                                                    PT[slot][:, j, :].rearrange("p (g t) -> p g t", g=4),
                                                                    m.unsqueeze(1).broadcast_to([128, 4, 128]), ALU.mult),
                             R=[("PT", slot, j), ("cmat",)], W=[("PT", slot, j)])

            def pv(n):
                kbs = [n - 1, n] if n > 0 else [n]
                slot = n % 2
                O = 4 + (n % 2)
                Dn = 6 + (n % 2)
                for hh in range(2):
                    def fo(e, hh=hh):
                        for ci, kb in enumerate(kbs):
                            j = hh * 2 + (1 if kb == n else 0)
                            ins = e.matmul(bank[O][hh * 64:(hh + 1) * 64, :], vtm[:, kb, hh * 64:(hh + 1) * 64], PT[slot][:, j, :],
                                           start=(ci == 0), stop=(ci == len(kbs) - 1))
                        return ins
                    rkeys = [("v", kb) for kb in kbs] + [("PT", slot, hh * 2 + (1 if kb == n else 0)) for kb in kbs]
                    P.op("pe", fo, R=rkeys, W=[("ps", O)])

                    def fdn(e, hh=hh):
                        for ci, kb in enumerate(kbs):
                            j = hh * 2 + (1 if kb == n else 0)
                            ins = e.matmul(bank[Dn][hh * 64:(hh + 1) * 64, :], cmat[:, 0:64], PT[slot][:, j, :],
                                           start=(ci == 0), stop=(ci == len(kbs) - 1))
                        return ins
                    P.op("pe", fdn, R=rkeys + [("cmat",)], W=[("ps", Dn)])
            nbox = {}

            def pv_add(n):
                Dn = 6 + (n % 2)
                t = tf_rot.next()
                nbox[n] = t
                t3 = tfr[t][:].rearrange("p (g t) -> p g t", g=4)
                P.op("dve", lambda e: e.tensor_tensor(t3, bank[Dn][:].rearrange("p (g t) -> p g t", g=4),
                                                      esink[:].unsqueeze(2).broadcast_to([128, 4, 128]), ALU.add),
                     R=[("ps", Dn), ("esink",)], W=[("tf", t)])

            def pv_lnexp(n):
                t = nbox[n]
                P.op("act", lambda e: e.activation(tfr[t][:], tfr[t][:], AF.Ln), R=[("tf", t)], W=[("tf", t)])
                P.op("act", lambda e: e.activation(tfr[t][:], tfr[t][:], AF.Exp, scale=-1.0), R=[("tf", t)], W=[("tf", t)])

            def pv_mult(n):
                t = nbox[n]
                O = 4 + (n % 2)
                t3 = tfr[t][:].rearrange("p (g t) -> p g t", g=4)
                P.op("dve", lambda e: e.tensor_tensor(q3[:, :, bl(n)], bank[O][:].rearrange("p (g t) -> p g t", g=4), t3, ALU.mult),
                     R=[("ps", O), ("tf", t)] + [("q", g, n) for g in range(4)], W=[("q", g, n) for g in range(4)])

            scores(0)
            for n in range(NB):
                if n >= 1:
                    pv_add(n - 1)
                if n + 1 < NB:
                    scores(n + 1)
                if n >= 1:
                    pv_lnexp(n - 1)
                    pv_mult(n - 1)
                pv(n)
            pv_add(NB - 1)
            pv_lnexp(NB - 1)
            pv_mult(NB - 1)

            for dc in range(NK):
                wa = ring_load(w4_d[dc], 4)
                wb = ring_load(w4_d[8 + dc], 4)
                wga = ring_load(w8_d[W8_IN + 9 + dc], 8)
                wgb = ring_load(w8_d[W8_IN + 17 + dc], 8)
                for tt in range(NT):
                    base = 4 * ((dc * NT + tt) % 2)
                    Ya, Yb, Ga, Gb = base, base + 1, base + 2, base + 3
                    nr = range(tt * 4, tt * 4 + 4)

                    def fya(e, w=wa, B=Ya, tt=tt):
                        for k in range(4):
                            ins = e.matmul(bank[B][:], w.v[:, k, :], u3[:, k, sl(tt)], start=(k == 0), stop=(k == 3))
                        return ins
                    P.op("pe", fya, R=[rk(wa)] + [("u", g, n) for g in range(4) for n in nr], W=[("ps", Ya)])

                    def fyb(e, w=wb, B=Yb, tt=tt):
                        for k in range(4):
                            ins = e.matmul(bank[B][:], w.v[:, k, :], q3[:, k, sl(tt)], start=(k == 0), stop=(k == 3))
                        return ins
                    P.op("pe", fyb, R=[rk(wb)] + [("q", g, n) for g in range(4) for n in nr], W=[("ps", Yb)])
                    gemm8(Ga, wga, tt)
                    gemm8(Gb, wgb, tt)
                    ta = tf_rot.next()
                    tb = tf_rot.next()
                    P.op("act", lambda e, ta=ta, Ga=Ga: e.activation(tfr[ta][:], bank[Ga][:], AF.Sigmoid), R=[("ps", Ga)], W=[("tf", ta)])
                    P.op("act", lambda e, tb=tb, Gb=Gb: e.activation(tfr[tb][:], bank[Gb][:], AF.Sigmoid), R=[("ps", Gb)], W=[("tf", tb)])
                    P.op("dve", lambda e, ta=ta, Ya=Ya: e.tensor_tensor(tfr[ta][:], bank[Ya][:], tfr[ta][:], ALU.mult),
                         R=[("ps", Ya), ("tf", ta)], W=[("tf", ta)])
                    P.op("dve", lambda e, tb=tb, Yb=Yb: e.tensor_tensor(tfr[tb][:], bank[Yb][:], tfr[tb][:], ALU.mult),
                         R=[("ps", Yb), ("tf", tb)], W=[("tf", tb)])
                    P.op("dve", lambda e, ta=ta, tb=tb, dc=dc, tt=tt: e.tensor_tensor(mg[dc][:, sl(tt)], tfr[ta][:], tfr[tb][:], ALU.add),
                         R=[("tf", ta), ("tf", tb)], W=[("mg", dc, tt)])

            for tiles, tail in ((TA, tailA), (TB, tailB)):
                ninc = NormInc(inc, b, TB, (6, 7)) if (inc is not None and tiles is TB) else None
                for dc in range(NK):
                    wo = ring_load(w8_d[W8_OUT + dc], 8)
                    for tt in tiles:
                        B = ob.next()

                        def fo2(e, w=wo, B=B, tt=tt):
                            for k in range(NK):
                                ins = e.matmul(bank[B][:], w.v[:, k, :], mg[k][:, sl(tt)], start=(k == 0), stop=(k == NK - 1))
                            return ins
                        P.op("pe", fo2, R=[rk(wo)] + [("mg", k, tt) for k in range(NK)], W=[("ps", B)])
                        P.op("dve", lambda e, B=B, dc=dc, tt=tt: e.scalar_tensor_tensor(
                            h[:, dc, sl(tt)], bank[B][:], Gcol(1, b, dc), h[:, dc, sl(tt)], ALU.mult, ALU.add),
                            R=[("ps", B), ("h", dc, tt), ("G", 1, b)], W=[("h", dc, tt)])
                        if ninc:
                            ninc.feed(dc, tt)
                        bg.step(2)
                bg.flush()
                if ninc:
                    bg.add(ninc.finish_steps())
                elif tail:
                    bg.add(tail())

        st_toks = []
        HS = S // 2

        def load_k(b, k, hf, extra=()):
            P.dma("sp", h[:, k, hf * HS:(hf + 1) * HS], xT_d[b, :, k * S + hf * HS:k * S + (hf + 1) * HS], ld_sem[hf * NK + k],
                  W=[("h", k, tt) for tt in (TA if hf == 0 else TB)], extra=extra)

        def store_k(b, k, hf):
            st_toks.append(P.dma("sp", out_d[b, :, k * S + hf * HS:k * S + (hf + 1) * HS], h[:, k, hf * HS:(hf + 1) * HS],
                                 st_sem[hf * NK + k], R=[("h", k, tt) for tt in (TA if hf == 0 else TB)]))

        def swap_dc(b):
            def f(dc, hf):
                if dc < NK:
                    store_k(b, dc, hf)
                if b + 1 < nseq and 0 <= dc - 2 < NK:
                    load_k(b + 1, dc - 2, hf)
            return f

        for k in range(NK):
            load_k(0, k, 0)
        ns = norm_steps(0, 0, TA)
        for fstep in ns[0:22]:
            fstep()
        mtok = None
        for j in range(16):
            mtok = mods_chunk(j)
        mods_finish(0, 16, [0], [])
        for k in range(NK):
            load_k(0, k, 1, extra=[mtok])
        for fstep in ns[22:30]:
            fstep()
        bg.add(norm_steps(0, 0, TB))
        mix_last = [P.last_pe]

        def wtm_load():
            P.dma("pool", scr[:, SC_WTM:SC_WTM + 8 * 768], wtm_d, misc_sem[5], W=[("wtm",)], extra=[mix_last[0]])
        for b in range(nseq):
            ffn(0, b, lambda b=b: norm_steps(1, b, TA, bk=3), None, extra_bg=(mods_bg[:9] + [wtm_load] + mods_bg[9:] if b == 0 else None), inc=1)
            mixer(b, lambda b=b: norm_steps(2, b, TA, bk=3), None, inc=2)
            mix_last[0] = P.last_pe
            nxt = b + 1 < nseq
            ffn(1, b, (lambda b=b: norm_steps(0, b + 1, TA)) if nxt else None, (lambda b=b: norm_steps(0, b + 1, TB)) if nxt else None,
                extra_bg=([wtm_load] if nxt else None), after_dc=swap_dc(b))
        bg.flush()
        P.wait("sp", st_toks)
        P.run()
    return nc


def _chunks_fm(W):
    K, N = W.shape
    return np.ascontiguousarray(W.reshape(K // 128, 128, N // 128, 128).transpose(2, 1, 0, 3)).reshape(N // 128, 128, K)


def _prep_shared(inp):
    f = lambda a: np.asarray(a, dtype=np.float32)
    w_ada = f(inp["w_ada"])[0]
    w_in = f(inp["w_in"])[0]
    w8 = np.empty((W8_N, 128, 1024), np.float32)
    w8[W8_ADA:W8_ADA + 72] = _chunks_fm(w_ada)
    for fi, (g, u) in enumerate((("ffn1_w_gate", "ffn1_w_up"), ("ffn2_w_gate", "ffn2_w_up"))):
        base = W8_F1 if fi == 0 else W8_F2
        w8[base:base + 44:2] = _chunks_fm(f(inp[g])[0])
        w8[base + 1:base + 44:2] = _chunks_fm(f(inp[u])[0])
    qcols = []
    for j in range(4):
        qcols += list(range(1024 + j * 64, 1024 + (j + 1) * 64)) + list(range(1024 + (4 + j) * 64, 1024 + (5 + j) * 64))
    w8[W8_IN:W8_IN + 4] = _chunks_fm(w_in[:, qcols])
    w8[W8_IN + 4:W8_IN + 5] = _chunks_fm(w_in[:, 1536:1664])
    w8[W8_IN + 5:W8_IN + 9] = _chunks_fm(w_in[:, 0:512])
    w8[W8_IN + 9:W8_IN + 17] = _chunks_fm(w_in[:, 1792:2816])
    w8[W8_IN + 17:W8_IN + 25] = _chunks_fm(w_in[:, 2816:3840])
    w8[W8_OUT:W8_OUT + 8] = _chunks_fm(f(inp["w_out"])[0])
    w11 = np.empty((32, 128, HC * 128), np.float32)
    for fi, name in enumerate(("ffn1_w_down", "ffn2_w_down")):
        wd = f(inp[name])[0]
        for half in range(2):
            w11[(fi * 2 + half) * 8:(fi * 2 + half) * 8 + 8] = _chunks_fm(wd[half * HC * 128:(half + 1) * HC * 128])
    w4 = np.empty((16, 128, 512), np.float32)
    w4[0:8] = _chunks_fm(f(inp["w_branch_a"])[0])
    wb = f(inp["w_branch_b"])[0]
    rows = []
    for g in range(4):
        for kvh in range(2):
            rows += list(range((kvh * 4 + g) * 64, (kvh * 4 + g + 1) * 64))
    w4[8:16] = _chunks_fm(wb[rows])
    wtm_cols = list(range(512, 1024)) + list(range(1536, 1664)) + list(range(1664, 1792))
    wtm = np.ascontiguousarray(w_in[:, wtm_cols].reshape(8, 128, 768).transpose(1, 0, 2)).reshape(128, 8 * 768)
    lngb = np.empty((128, 1024), np.float32)
    lngb[:, 0:512] = f(inp["g_sgu_ln"])[0][None, :]
    lngb[:, 512:1024] = f(inp["b_sgu_ln"])[0][None, :]
    wsp = f(inp["w_spatial"])[0]
    wst = np.ascontiguousarray(wsp.transpose(2, 0, 1)).reshape(128, 512)
    bsr = np.ascontiguousarray(f(inp["b_spatial"])[0].reshape(1, 512))
    p = np.arange(128)
    cmat = np.zeros((128, 512), np.float32)
    cmat[:, 0:128] = 1.0
    cmat[:, 128:256] = (p[:, None] // 64 == p[None, :] // 64)
    cmat[:, 256:384] = (p[:, None] > p[None, :])
    cmat[:, 384:512] = (p[:, None] <= p[None, :])
    pv = np.zeros((128, PV_N), np.float32)
    pv[:, PV_BADA:PV_BADA + 72] = f(inp["b_ada"])[0].reshape(72, 128).T
    for i, nm in enumerate(("g_norm1", "g_norm2", "g_norm3")):
        pv[:, PV_G1 + 8 * i:PV_G1 + 8 * i + 8] = f(inp[nm])[0].reshape(8, 128).T
    pv[:, PV_GQ] = np.tile(f(inp["g_q"])[0], 2)
    pv[:, PV_GK] = np.tile(f(inp["g_k"])[0], 2)
    sk = f(inp["attn_sinks"])[0]
    for kvh in range(2):
        pv[kvh * 64:(kvh + 1) * 64, PV_SINK:PV_SINK + 4] = sk[kvh * 4:kvh * 4 + 4][None, :]
    return dict(w8=w8, w11=w11, w4=w4, wtm=wtm, lngb=lngb, wst=wst, bsr=bsr, cmat=cmat), pv


def _prep_core(inp, pv, core, nseq=2):
    x = np.asarray(inp["x"], dtype=np.float32)
    c = np.asarray(inp["c"], dtype=np.float32)
    b0 = core * nseq
    xT = np.empty((nseq, 128, NK * S), np.float32)
    pvc = pv.copy()
    for s in range(nseq):
        xT[s] = x[b0 + s].reshape(S, NK, 128).transpose(2, 1, 0).reshape(128, NK * S)
        pvc[:, PV_COND + s:PV_COND + 16:2] = c[b0 + s].reshape(NK, 128).T
    return xT, pvc


_NC_CACHE = {}


def kernel(**inputs):
    shared, pv = _prep_shared(inputs)
    nc = build_program(2)
    in_maps = []
    for core in range(NCORES):
        xT, pvc = _prep_core(inputs, pv, core)
        m = dict(shared)
        m["xT"] = xT
        m["pvec"] = pvc
        in_maps.append(m)
    res = run_bass_kernel_spmd(nc, in_maps, core_ids=list(range(NCORES)))
    out = np.empty((16, S, D), np.float32)
    for core in range(NCORES):
        o = res.results[core]["out"]
        for s in range(2):
            out[core * 2 + s] = o[s].reshape(128, NK, S).transpose(2, 1, 0).reshape(S, D)
    return out
```

```python
import numpy as np
from contextlib import ExitStack
import concourse.bass as bass
import concourse.mybir as mybir
from concourse.bass_utils import run_bass_kernel_spmd

F32 = mybir.dt.float32
BF16 = mybir.dt.bfloat16
AF = mybir.ActivationFunctionType
ALU = mybir.AluOpType
AX = mybir.AxisListType

NCORES = 8
D = 1024
S = 2048
DFF = 2816
NK = 8
NT = 4
NB = 16
TT = 512
HC = 11
NSLOT = 7
EPS = 1e-6

W8_ADA = 0
W8_F1 = 72
W8_F2 = 116
W8_IN = 160
W8_OUT = 185
W8_N = 193

PV_BADA = 0
PV_G1 = 72
PV_GQ = 96
PV_GK = 97
PV_SINK = 98
PV_COND = 102
PV_N = 118

SC_U = 0
SC_VLN = 8192
SC_Q = 16384
SC_KT = 24576
SC_V = 26624
SC_WTM = 28672
SC_N = 34816
MG_OFF = [8192, 10240, 12288, 14336, 24576, 26624, 28672, 30720]


class _Eng:
    def __init__(self):
        self.ops = []
        self.cnt = 0
        self.seen = {}
        self.sem = None


class Prog:
    def __init__(self, nc, es):
        self.nc = nc
        self.es = es
        self.E = {n: _Eng() for n in ("pe", "act", "dve", "pool", "sp")}
        for n, e in self.E.items():
            e.sem = es.enter_context(nc.semaphore("s_" + n))
        self.lastw = {}
        self.readers = {}
        self.nsem = 0
        self.last_pe = None

    def newsem(self, name):
        return self.es.enter_context(self.nc.semaphore(name))

    def _waits(self, e, deps):
        waits = {}
        for d in deps:
            if d is None:
                continue
            s, v = d
            if v > waits.get(s, 0):
                waits[s] = v
        wl = []
        for s, v in waits.items():
            if e.seen.get(s, 0) >= v:
                continue
            e.seen[s] = v
            wl.append((s, v))
        return wl

    def _deps(self, e, R, W, extra):
        deps = list(extra)
        for k in R:
            t = self.lastw.get(k)
            if t is not None:
                deps.append(t)
        pe_sem = self.E["pe"].sem
        for k in W:
            t = self.lastw.get(k)
            if t is not None and not (t[0] is e.sem and e.sem is pe_sem):
                deps.append(t)
            rd = self.readers.get(k)
            if rd:
                for s, v in rd.items():
                    if not (s is e.sem and e.sem is pe_sem):
                        deps.append((s, v))
        return deps

    def _record(self, tok, R, W):
        for k in R:
            rd = self.readers.setdefault(k, {})
            if rd.get(tok[0], 0) < tok[1]:
                rd[tok[0]] = tok[1]
        for k in W:
            self.lastw[k] = tok
            self.readers[k] = {}

    def op(self, en, fn, R=(), W=(), extra=()):
        e = self.E[en]
        wl = self._waits(e, self._deps(e, R, W, extra))
        e.cnt += 1
        tok = (e.sem, e.cnt)
        sem_e = e.sem

        def thunk(eng, wl=wl, fn=fn):
            for s, v in wl:
                eng.wait_ge(s, v)
            fn(eng).then_inc(sem_e, 1)
        e.ops.append(thunk)
        self._record(tok, R, W)
        if en == "pe":
            self.last_pe = tok
        return tok

    def dma(self, en, out, in_, semc, R=(), W=(), extra=()):
        e = self.E[en]
        wl = self._waits(e, self._deps(e, R, W, extra))
        semc[1] += 16
        tok = (semc[0], semc[1])
        sem = semc[0]

        def thunk(eng, wl=wl):
            for s, v in wl:
                eng.wait_ge(s, v)
            eng.dma_start(out=out, in_=in_).then_inc(sem, 16)
        e.ops.append(thunk)
        self._record(tok, R, W)
        return tok

    def wait(self, en, toks):
        e = self.E[en]
        wl = self._waits(e, toks)

        def thunk(eng, wl=wl):
            for s, v in wl:
                eng.wait_ge(s, v)
        e.ops.append(thunk)

    def run(self):
        nc = self.nc
        E = self.E
        with nc.Block() as block:
            @block.tensor
            def _(eng):
                for t in E["pe"].ops:
                    t(eng)

            @block.scalar
            def _(eng):
                for t in E["act"].ops:
                    t(eng)

            @block.vector
            def _(eng):
                for t in E["dve"].ops:
                    t(eng)

            @block.gpsimd
            def _(eng):
                for t in E["pool"].ops:
                    t(eng)

            @block.sync
            def _(eng):
                for t in E["sp"].ops:
                    t(eng)


class Rot:
    def __init__(self, items):
        self.items = items
        self.i = 0

    def next(self):
        x = self.items[self.i % len(self.items)]
        self.i += 1
        return x


def build_program(nseq=2, stop=99):
    nc = bass.Bass("TRN2", target_bir_lowering=False)
    dt = lambda n, s: nc.dram_tensor(n, s, F32, kind="ExternalInput").ap()
    xT_d = dt("xT", [nseq, 128, NK * S])
    w8_d = dt("w8", [W8_N, 128, 1024])
    w11_d = dt("w11", [32, 128, HC * 128])
    w4_d = dt("w4", [16, 128, 512])
    wtm_d = dt("wtm", [128, 8 * 768])
    pvec_d = dt("pvec", [128, PV_N])
    lngb_d = dt("lngb", [128, 1024])
    wst_d = dt("wst", [128, 512])
    bsr_d = dt("bsr", [1, 512])
    cmat_d = dt("cmat", [128, 512])
    out_d = nc.dram_tensor("out", [nseq, 128, NK * S], F32, kind="ExternalOutput").ap()

    with ExitStack() as es:
        P = Prog(nc, es)
        sb = lambda n, s, d: es.enter_context(nc.sbuf_tensor(n, s, d))
        h = sb("h", [128, NK, S], F32)
        xn = sb("xn", [128, NK, S], BF16)
        scr = sb("scr", [128, SC_N], BF16)
        ringbuf = sb("ringbuf", [128, NSLOT * 1024], BF16)
        ring = [ringbuf[:, i * 1024:(i + 1) * 1024] for i in range(NSLOT)]
        sqr = [sb(f"sq{i}", [128, TT], BF16) for i in range(6)]
        rsr = [sb(f"rs{i}", [128, TT], F32) for i in range(2)]
        tfr = [sb(f"tf{i}", [128, TT], F32) for i in range(4)]
        ksqr = [sb(f"ksq{i}", [128, 128], F32) for i in range(1)]
        cmat = sb("cmat_sb", [128, 512], BF16)
        pvec = sb("pvec_sb", [128, PV_N], F32)
        lngb = sb("lngb_sb", [128, 1024], F32)
        wst = sb("wst_sb", [128, 512], BF16)
        bsr = sb("bsr_sb", [1, 512], BF16)
        cond = sb("cond", [128, 16], BF16)
        mods = sb("mods", [128, 144], F32)
        Asb = sb("Asb", [128, 48], F32)
        Gsb = sb("Gsb", [128, 48], F32)
        gqk = sb("gqk", [128, 1], F32)
        esink = sb("esink", [128, 4], F32)
        epst = sb("epst", [128, 2], F32)
        st6 = sb("st6", [128, NB * 6], F32)
        mv = sb("mv", [128, NB * 2], F32)
        ssk = sb("ssk", [128, NB * 2], F32)
        rstdv = sb("rstdv", [128, NB], F32)
        rk8 = sb("rk8", [128, NB * 2], F32)
        nmr = sb("nmr", [128, NB], F32)
        bank = [es.enter_context(nc.psum_tensor(f"bank{i}", [128, TT], F32)) for i in range(8)]

        ones = cmat[:, 0:128]
        bd = cmat[:, 128:256]
        mprev = cmat[:, 256:384]
        mcur = cmat[:, 384:512]

        hT = scr[:, 0:HC * S].rearrange("p (k t) -> p k t", t=S)
        u3 = scr[:, SC_U:SC_U + 4 * S].rearrange("p (g t) -> p g t", t=S)
        vln = scr[:, SC_VLN:SC_VLN + NB * 512].rearrange("p (n c) -> p n c", c=512)
        q3 = scr[:, SC_Q:SC_Q + 4 * S].rearrange("p (g t) -> p g t", t=S)
        kT = scr[:, SC_KT:SC_KT + S]
        vtm = scr[:, SC_V:SC_V + NB * 128].rearrange("p (n c) -> p n c", c=128)
        wtm = scr[:, SC_WTM:SC_WTM + 8 * 768].rearrange("p (k c) -> p k c", c=768)
        PT = [scr[:, SC_WTM + i * 2048:SC_WTM + (i + 1) * 2048].rearrange("p (j t) -> p j t", t=512) for i in range(2)]
        mg = [scr[:, o:o + S] for o in MG_OFF]

        ring_sem = [[P.newsem(f"ringsem{i}"), 0] for i in range(NSLOT)]
        ld_sem = [[P.newsem(f"ldsem{i}"), 0] for i in range(2 * NK)]
        st_sem = [[P.newsem(f"stsem{i}"), 0] for i in range(2 * NK)]
        misc_sem = [[P.newsem(f"miscsem{i}"), 0] for i in range(7)]
        ring_i = [0]
        ring_gen = [0] * NSLOT

        class RW:
            pass

        def ring_load(src, nk):
            i = ring_i[0] % NSLOT
            ring_i[0] += 1
            ring_gen[i] += 1
            dst = ringbuf[:, i * 1024:i * 1024 + nk * 128]
            tok = P.dma("pool", dst, src, ring_sem[i], W=[("ring", i)])
            r = RW()
            r.tok = tok
            r.v = dst.rearrange("p (k c) -> p k c", c=128)
            r.i = i
            r.gen = ring_gen[i]
            return r

        def rk(r):
            assert ring_gen[r.i] == r.gen, "ring slot reloaded before use"
            return ("ring", r.i)

        def ring_load2(src, nk):
            i = ring_i[0] % NSLOT
            if i == NSLOT - 1:
                ring_i[0] += 1
                i = 0
            ring_i[0] += 2
            ring_gen[i] += 1
            ring_gen[i + 1] += 1
            dst = ringbuf[:, i * 1024:i * 1024 + nk * 128]
            tok = P.dma("pool", dst, src, ring_sem[i], W=[("ring", i), ("ring", i + 1)])
            r = RW()
            r.tok = tok
            r.v = dst.rearrange("p (k c) -> p k c", c=128)
            r.i = i
            r.gen = ring_gen[i]
            r.gen2 = ring_gen[i + 1]
            return r

        def rk2(r):
            assert ring_gen[r.i] == r.gen and ring_gen[r.i + 1] == r.gen2, "ring slot pair reloaded before use"
            return [("ring", r.i), ("ring", r.i + 1)]

        class BG:
            def __init__(self):
                self.steps = []

            def add(self, fns):
                self.steps.extend(fns)

            def step(self, n=1):
                for _ in range(n):
                    if self.steps:
                        self.steps.pop(0)()

            def flush(self):
                while self.steps:
                    self.steps.pop(0)()
        bg = BG()

        sq_rot = Rot([0, 1, 2, 3])
        qsq_rot = Rot([4, 5])
        rs_rot = Rot(list(range(2)))
        tf_rot = Rot(list(range(4)))
        ksq_rot = Rot([0])

        def sl(tt):
            return slice(tt * TT, (tt + 1) * TT)

        def bl(n):
            return slice(n * 128, (n + 1) * 128)

        def gemm8(B, w, tt, src=None):
            def f(e):
                for k in range(NK):
                    ins = e.matmul(bank[B][:], w.v[:, k, :], xn[:, k, sl(tt)], start=(k == 0), stop=(k == NK - 1))
                return ins
            return P.op("pe", f, R=[rk(w)] + [("xn", k, tt) for k in range(NK)], W=[("ps", B)])

        P.dma("sp", pvec[:], pvec_d, misc_sem[0], W=[("pvec",)])
        P.dma("sp", lngb[:], lngb_d, misc_sem[1], W=[("lngb",)])
        P.dma("sp", tfr[3][:], wst_d, misc_sem[2], W=[("tf", 3)])
        P.dma("pool", cmat[:], cmat_d, misc_sem[3], W=[("cmat",)])
        P.dma("pool", bsr[:], bsr_d, misc_sem[4], W=[("bsr",)])
        P.op("dve", lambda e: e.memset(epst[:, 0:1], EPS), W=[("eps0",)])
        P.op("dve", lambda e: e.memset(epst[:, 1:2], 64 * EPS), W=[("eps1",)])
        P.op("dve", lambda e: e.tensor_tensor(wst[:].rearrange("p (g t) -> p g t", g=4), tfr[3][:].rearrange("p (g t) -> p g t", g=4),
                                              mcur.unsqueeze(1).broadcast_to([128, 4, 128]), ALU.mult),
             R=[("tf", 3), ("cmat",)], W=[("wst",)])
        P.op("dve", lambda e: e.tensor_tensor(gqk[:], pvec[:, PV_GQ:PV_GQ + 1], pvec[:, PV_GK:PV_GK + 1], ALU.mult),
             R=[("pvec",)], W=[("gqk",)])
        P.op("act", lambda e: e.activation(esink[:], pvec[:, PV_SINK:PV_SINK + 4], AF.Exp), R=[("pvec",)], W=[("esink",)])
        P.op("act", lambda e: e.activation(cond[:], pvec[:, PV_COND:PV_COND + 16], AF.Silu), R=[("pvec",)], W=[("cond",)])

        cond3 = cond[:].rearrange("p (k b) -> p k b", b=2)
        mods3 = mods[:].rearrange("p (j b) -> p j b", b=2)

        def mods_chunk(j):
            w = ring_load(w8_d[W8_ADA + j], 8)

            def f(e):
                for k in range(8):
                    ins = e.matmul(bank[7][:, 2 * j:2 * j + 2], w.v[:, k, :], cond3[:, k, :], start=(k == 0), stop=(k == 7))
                return ins
            P.op("pe", f, R=[rk(w), ("cond",)], W=[("ps7", j)])
            return w.tok

        def mods_finish(j0, j1, As, Gs):
            P.op("dve", lambda e: e.tensor_tensor(mods3[:, j0:j1, :],
                                                  bank[7][:, 2 * j0:2 * j1].rearrange("p (j b) -> p j b", b=2),
                                                  pvec[:, PV_BADA + j0:PV_BADA + j1].unsqueeze(2).broadcast_to([128, j1 - j0, 2]), ALU.add),
                 R=[("ps7", j) for j in range(j0, j1)] + [("pvec",)], W=[("mods", j) for j in range(j0, j1)])
            for s3 in As:
                for b in range(2):
                    o = (s3 * 2 + b) * 8
                    P.op("dve", lambda e, s3=s3, b=b, o=o: e.scalar_tensor_tensor(
                        Asb[:, o:o + 8], mods3[:, (3 * s3 + 1) * 8:(3 * s3 + 2) * 8, b], 1.0,
                        pvec[:, PV_G1 + 8 * s3:PV_G1 + 8 * s3 + 8], ALU.add, ALU.mult),
                        R=[("mods", j) for j in range((3 * s3 + 1) * 8, (3 * s3 + 2) * 8)] + [("pvec",)], W=[("A", s3, b)])
            for s3 in Gs:
                for b in range(2):
                    o = (s3 * 2 + b) * 8
                    P.op("dve", lambda e, s3=s3, b=b, o=o: e.tensor_scalar(
                        Gsb[:, o:o + 8], mods3[:, (3 * s3 + 2) * 8:(3 * s3 + 3) * 8, b], (1.0 if s3 == 1 else 0.5), None, ALU.mult),
                        R=[("mods", j) for j in range((3 * s3 + 2) * 8, (3 * s3 + 3) * 8)], W=[("G", s3, b)])

        mods_bg = ([(lambda j=j: mods_chunk(j)) for j in range(16, 24)] + [lambda: mods_finish(16, 24, [], [0])] +
                   [(lambda j=j: mods_chunk(j)) for j in range(24, 72)] + [lambda: mods_finish(24, 72, [1, 2], [1, 2])])

        def Acol(s3, b, k):
            o = (s3 * 2 + b) * 8 + k
            return Asb[:, o:o + 1]

        def Gcol(s3, b, k):
            o = (s3 * 2 + b) * 8 + k
            return Gsb[:, o:o + 1]

        def Bcol(s3, b, k):
            return mods3[:, 3 * s3 * 8 + k, b:b + 1]

        def mk_ln_step(tt, box, bk):
            def f():
                r = rs_rot.next()
                box["r"] = r
                P.op("act", lambda e: e.activation(rsr[r][:], bank[bk][:], AF.Ln, bias=epst[:, 0:1], scale=1.0 / D),
                     R=[("ps", bk), ("eps0",)], W=[("rs", r)])
                P.op("act", lambda e: e.activation(rsr[r][:], rsr[r][:], AF.Exp, scale=-0.5), R=[("rs", r)], W=[("rs", r)])
            return f

        def mk_aff_step(s3, b, k0, tt, box, dve_only=False):
            def f():
                r = box["r"]
                for k in (k0, k0 + 1):
                    t = tf_rot.next()
                    P.op("dve", lambda e, t=t, k=k: e.scalar_tensor_tensor(tfr[t][:], h[:, k, sl(tt)], Acol(s3, b, k), rsr[r][:],
                                                                       ALU.mult, ALU.mult),
                         R=[("h", k, tt), ("rs", r), ("A", s3, b)], W=[("tf", t)])
                    if dve_only:
                        P.op("dve", lambda e, t=t, k=k: e.tensor_scalar(xn[:, k, sl(tt)], tfr[t][:], Bcol(s3, b, k), None, ALU.add),
                             R=[("tf", t), ("mods", 3 * s3 * 8 + k)], W=[("xn", k, tt)])
                    else:
                        P.op("act", lambda e, t=t, k=k: e.activation(xn[:, k, sl(tt)], tfr[t][:], AF.Identity, bias=Bcol(s3, b, k), scale=1.0),
                             R=[("tf", t), ("mods", 3 * s3 * 8 + k)], W=[("xn", k, tt)])
            return f

        def emit_square(i, k, tt):
            if k % 2 == 0:
                P.op("act", lambda e: e.activation(sqr[i][:], h[:, k, sl(tt)], AF.Square), R=[("h", k, tt)], W=[("sq", i)])
            else:
                P.op("dve", lambda e: e.tensor_tensor(sqr[i][:], h[:, k, sl(tt)], h[:, k, sl(tt)], ALU.mult), R=[("h", k, tt)], W=[("sq", i)])

        def norm_steps(s3, b, tiles, bk=6):
            steps = []
            boxes = {tt: {} for tt in tiles}
            for tt in tiles:
                pend = []

                def emit_mm1(tt=tt, pend=pend):
                    i, k = pend.pop(0)
                    P.op("pe", lambda e: e.matmul(bank[bk][:], ones, sqr[i][:], start=(k == 0), stop=(k == NK - 1)),
                         R=[("sq", i), ("cmat",)], W=[("ps", bk)])

                def sq_step(k, tt=tt, pend=pend, emit_mm1=emit_mm1):
                    def f():
                        if len(pend) >= 2:
                            emit_mm1()
                        i = sq_rot.next()
                        emit_square(i, k, tt)
                        pend.append((i, k))
                    return f
                for k in range(NK):
                    steps.append(sq_step(k))
                steps.append(emit_mm1)
                steps.append(emit_mm1)
                steps.append(mk_ln_step(tt, boxes[tt], bk))
            for tt in tiles:
                for k0 in range(0, NK, 2):
                    steps.append(mk_aff_step(s3, b, k0, tt, boxes[tt]))
            return steps

        class NormInc:
            def __init__(self, s3, b, tiles, banks):
                self.s3, self.b, self.tiles = s3, b, tiles
                self.B = dict(zip(tiles, banks))
                self.pend = []
                self.cnt = {tt: 0 for tt in tiles}

            def _mm(self):
                while self.pend:
                    i, k, tt = self.pend.pop(0)
                    n = self.cnt[tt]
                    self.cnt[tt] += 1
                    bk = self.B[tt]
                    P.op("pe", lambda e, i=i, n=n, bk=bk: e.matmul(bank[bk][:], ones, sqr[i][:], start=(n == 0), stop=(n == NK - 1)),
                         R=[("sq", i), ("cmat",)], W=[("ps", bk)])

            def feed(self, k, tt):
                self._mm()
                i = qsq_rot.next()
                emit_square(i, k, tt)
                self.pend.append((i, k, tt))

            def finish_steps(self):
                steps = [self._mm]
                boxes = {tt: {} for tt in self.tiles}
                for tt in self.tiles:
                    steps.append(mk_ln_step(tt, boxes[tt], self.B[tt]))
                na = 0
                for tt in self.tiles:
                    for k0 in range(0, NK, 2):
                        steps.append(mk_aff_step(self.s3, self.b, k0, tt, boxes[tt], dve_only=(self.s3 == 1 and na < 2)))
                        na += 1
                return steps

        TA = (0, 1)
        TB = (2, 3)

        def ffn(f, b, tailA, tailB, extra_bg=None, inc=None, after_dc=None):
            s3 = 0 if f == 0 else 2
            gb = Rot([0, 1])
            ub = Rot([2, 3])
            db = Rot([4, 5, 0, 1, 2])
            base8 = W8_F1 if f == 0 else W8_F2

            def g1_iter(wg, wu, hc, tt):
                G = gb.next()
                U = ub.next()
                gemm8(G, wg, tt)
                gemm8(U, wu, tt)
                t = tf_rot.next()
                P.op("act", lambda e: e.activation(tfr[t][:], bank[G][:], AF.Silu), R=[("ps", G)], W=[("tf", t)])
                P.op("dve", lambda e: e.tensor_tensor(hT[:, hc, sl(tt)], tfr[t][:], bank[U][:], ALU.mult),
                     R=[("tf", t), ("ps", U)], W=[("hT", hc, tt)])

            def wd_load(half, dc):
                return ring_load2(w11_d[(f * 2 + half) * 8 + dc], HC)

            def g2_iter(wd, dc, tt):
                assert ("G", s3, b) in P.lastw
                Dk = db.next()

                def fd(e):
                    for kk in range(HC):
                        ins = e.matmul(bank[Dk][:], wd.v[:, kk, :], hT[:, kk, sl(tt)], start=(kk == 0), stop=(kk == HC - 1))
                    return ins
                P.op("pe", fd, R=rk2(wd) + [("hT", kk, tt) for kk in range(HC)], W=[("ps", Dk)])
                P.op("dve", lambda e: e.scalar_tensor_tensor(
                    h[:, dc, sl(tt)], bank[Dk][:], Gcol(s3, b, dc), h[:, dc, sl(tt)], ALU.mult, ALU.add),
                    R=[("ps", Dk), ("h", dc, tt), ("G", s3, b)], W=[("h", dc, tt)])

            for half in range(2):
                hcs = list(range(HC))
                if half == 0:
                    NST = 3
                    ws = []
                    for hc in range(NST):
                        ws.append((ring_load(w8_d[base8 + hc * 2], 8), ring_load(w8_d[base8 + hc * 2 + 1], 8)))
                    for tt in TA:
                        for hc in range(NST):
                            bg.step(5)
                            g1_iter(ws[hc][0], ws[hc][1], hc, tt)
                    bg.flush()
                    for tt in TB:
                        for hc in range(NST):
                            g1_iter(ws[hc][0], ws[hc][1], hc, tt)
                    hcs = list(range(NST, HC))
                    if extra_bg:
                        bg.add(extra_bg)
                for hc in hcs:
                    ghc = half * HC + hc
                    wg = ring_load(w8_d[base8 + ghc * 2], 8)
                    wu = ring_load(w8_d[base8 + ghc * 2 + 1], 8)
                    for tt in range(NT):
                        g1_iter(wg, wu, hc, tt)
                        if tt % 2 == 1 or (tt == 2 and len(bg.steps) > 24):
                            bg.step(1)
                if half == 0:
                    for dc in range(NK):
                        wd = wd_load(half, dc)
                        for tt in range(NT):
                            g2_iter(wd, dc, tt)
                            if tt % 2 == 1:
                                bg.step(1)
                else:
                    bg.flush()
                    for tiles, tail in ((TA, tailA), (TB, tailB)):
                        ninc = NormInc(inc, b, TB, (6, 7)) if (inc is not None and tiles is TB) else None
                        for dc in range(NK):
                            wd = wd_load(half, dc)
                            for tt in tiles:
                                g2_iter(wd, dc, tt)
                                if ninc:
                                    ninc.feed(dc, tt)
                                bg.step(2)
                            if after_dc:
                                after_dc(dc, 0 if tiles is TA else 1)
                        if after_dc:
                            after_dc(NK, 0 if tiles is TA else 1)
                            after_dc(NK + 1, 0 if tiles is TA else 1)
                        bg.flush()
                        if ninc:
                            bg.add(ninc.finish_steps())
                        elif tail:
                            bg.add(tail())

        def mixer(b, tailA, tailB, inc=None):
            ob = Rot([0, 1, 2, 4, 5])
            ob3 = Rot([0, 1, 4, 5])
            sb2 = Rot([2, 3])
            pend = []

            def q_stage2(g, tt, B, i):
                Sq = sb2.next()
                P.op("pe", lambda e: e.matmul(bank[Sq][:], bd, sqr[i][:], start=True, stop=True),
                     R=[("sq", i), ("cmat",)], W=[("ps", Sq)])
                r = tf_rot.next()
                P.op("act", lambda e: e.activation(tfr[r][:], bank[Sq][:], AF.Ln, bias=epst[:, 0:1], scale=1.0 / 64),
                     R=[("ps", Sq), ("eps0",)], W=[("tf", r)])
                P.op("act", lambda e: e.activation(tfr[r][:], tfr[r][:], AF.Exp, scale=-0.5), R=[("tf", r)], W=[("tf", r)])
                P.op("dve", lambda e: e.scalar_tensor_tensor(q3[:, g, sl(tt)], bank[B][:], gqk[:, 0:1], tfr[r][:], ALU.mult, ALU.mult),
                     R=[("ps", B), ("tf", r), ("gqk",)], W=[("q", g, n) for n in range(tt * 4, tt * 4 + 4)])

            wq = [ring_load(w8_d[W8_IN + g], 8) for g in range(5)]

            def m2a(tiles, nstep):
                it = [0]

                def bstep():
                    it[0] += 1
                    if nstep >= 1:
                        bg.step(nstep)
                    elif nstep > 0 and it[0] % 2 == 1:
                        bg.step(1)
                for tt in tiles:
                    for g in range(5):
                        B = ob3.next()
                        gemm8(B, wq[g], tt)
                        if g < 4:
                            i = qsq_rot.next()
                            P.op("act", lambda e, i=i, B=B: e.activation(sqr[i][:], bank[B][:], AF.Square), R=[("ps", B)], W=[("sq", i)])
                            bstep()
                            if pend:
                                q_stage2(*pend.pop())
                            pend.append((g, tt, B, i))
                        else:
                            P.op("act", lambda e, B=B, tt=tt: e.activation(kT[:, sl(tt)], bank[B][:], AF.Copy), R=[("ps", B)],
                                 W=[("kT", n) for n in range(tt * 4, tt * 4 + 4)])
                            bstep()
                q_stage2(*pend.pop())

            avb = Rot([0, 1])
            kvb = Rot([2, 3])
            st3 = st6[:].rearrange("p (n c) -> p n c", c=6)
            mv3 = mv[:].rearrange("p (n c) -> p n c", c=2)
            ssk3 = ssk[:].rearrange("p (n c) -> p n c", c=2)
            rk83 = rk8[:].rearrange("p (n c) -> p n c", c=2)
            def m2b(blocks, nstep):
              for n in blocks:
                tt = n // 4
                AV = avb.next()
                KV = kvb.next()

                def ftm(e, AV=AV, KV=KV, n=n):
                    for k in range(NK):
                        e.matmul(bank[AV][:], xn[:, k, bl(n)], wtm[:, k, 0:512], start=(k == 0), stop=(k == NK - 1))
                        ins = e.matmul(bank[KV][:, 0:256], xn[:, k, bl(n)], wtm[:, k, 512:768], start=(k == 0), stop=(k == NK - 1))
                    return ins
                P.op("pe", ftm, R=[("wtm",)] + [("xn", k, tt) for k in range(NK)], W=[("ps", AV), ("ps", KV)])
                P.op("act", lambda e, AV=AV, n=n: e.activation(vln[:, n, :], bank[AV][:], AF.Gelu_apprx_tanh), R=[("ps", AV)], W=[("vln", n)])
                ki = ksq_rot.next()
                P.op("act", lambda e, KV=KV, ki=ki: e.activation(ksqr[ki][:], bank[KV][:, 0:128], AF.Square), R=[("ps", KV)], W=[("ksq", ki)])
                P.op("act", lambda e, KV=KV, n=n: e.activation(vtm[:, n, :], bank[KV][:, 128:256], AF.Copy), R=[("ps", KV)], W=[("v", n)])
                P.op("dve", lambda e, n=n: e.bn_stats(st3[:, n, :], vln[:, n, :]), R=[("vln", n)], W=[("st", n)])
                P.op("dve", lambda e, n=n: e.bn_aggr(mv3[:, n, :], st3[:, n, :]), R=[("st", n)], W=[("mv", n)])
                P.op("dve", lambda e, n=n, ki=ki: e.reduce_sum(ssk3[:, n, :], ksqr[ki][:].rearrange("p (h d) -> p h d", h=2), AX.X),
                     R=[("ksq", ki)], W=[("ssk", n)])
                bg.step(nstep)

            m2a(TA, 0.5)
            m2b(range(0, NB // 2), 1)
            bg.flush()
            m2a(TB, 0)
            m2b(range(NB // 2, NB), 0)

            P.op("act", lambda e: e.activation(rstdv[:], mv3[:, :, 1], AF.Ln, bias=epst[:, 0:1], scale=1.0),
                 R=[("mv", n) for n in range(NB)] + [("eps0",)], W=[("rstdv",)])
            P.op("act", lambda e: e.activation(rstdv[:], rstdv[:], AF.Exp, scale=-0.5), R=[("rstdv",)], W=[("rstdv",)])
            P.op("act", lambda e: e.activation(rk8[:], ssk[:], AF.Ln, bias=epst[:, 1:2], scale=1.0),
                 R=[("ssk", n) for n in range(NB)] + [("eps1",)], W=[("rk8",)])
            P.op("act", lambda e: e.activation(rk8[:], rk8[:], AF.Exp, scale=-0.5), R=[("rk8",)], W=[("rk8",)])

            P.op("dve", lambda e: e.scalar_tensor_tensor(nmr[:], mv3[:, :, 0], -1.0, rstdv[:], ALU.mult, ALU.mult),
                 R=[("mv", n) for n in range(NB)] + [("rstdv",)], W=[("nmr",)])

            def ln_fin(n):
                def f():
                    P.op("act", lambda e: e.activation(vln[:, n, :], vln[:, n, :], AF.Identity, bias=nmr[:, n:n + 1], scale=rstdv[:, n:n + 1]),
                         R=[("vln", n), ("nmr",), ("rstdv",)], W=[("vln", n)])
                    P.op("dve", lambda e: e.tensor_tensor(vln[:, n, :], vln[:, n, :], lngb[:, 0:512], ALU.mult),
                         R=[("vln", n), ("lngb",)], W=[("vln", n)])
                    P.op("dve", lambda e: e.tensor_tensor(vln[:, n, :], vln[:, n, :], lngb[:, 512:1024], ALU.add),
                         R=[("vln", n), ("lngb",)], W=[("vln", n)])
                return f
            bg.add([ln_fin(n) for n in range(NB)])

            zb = Rot([4, 5])
            ub4 = Rot([0, 1, 2, 3])
            wst3 = wst[:].rearrange("p (g t) -> p g t", g=4)

            def sgu(n):
                assert ("vln", n) in P.lastw
                Z = zb.next()

                def fz(e):
                    e.matmul(bank[Z][:, 0:512], cmat[0:1, 0:128], bsr[0:1, 0:512], start=True, stop=False)
                    for g in range(4):
                        ins = e.matmul(bank[Z][:, g * 128:(g + 1) * 128], vln[:, n, g * 128:(g + 1) * 128], wst3[:, g, :], start=False, stop=(g == 3))
                    return ins
                P.op("pe", fz, R=[("vln", n), ("wst",), ("bsr",), ("cmat",)], W=[("ps", Z)])
                P.op("dve", lambda e: e.tensor_tensor(u3[:, :, bl(n)], bank[Z][:].rearrange("p (g t) -> p g t", g=4), u3[:, :, bl(n)], ALU.mult),
                     R=[("ps", Z)] + [("u", g, n) for g in range(4)], W=[("u", g, n) for g in range(4)])

            wu4 = [ring_load(w8_d[W8_IN + 5 + g], 8) for g in range(4)]
            for tt in range(NT):
                for g in range(4):
                    B = ub4.next()
                    gemm8(B, wu4[g], tt)
                    P.op("act", lambda e, B=B, g=g, tt=tt: e.activation(u3[:, g, sl(tt)], bank[B][:], AF.Gelu_apprx_tanh), R=[("ps", B)],
                         W=[("u", g, n) for n in range(tt * 4, tt * 4 + 4)])
                    bg.step(1)
                    if tt >= 1:
                        sgu((tt - 1) * 4 + g)
            bg.flush()
            for n in range(NB - 4, NB):
                sgu(n)

            def scores(n):
                kbs = [n - 1, n] if n > 0 else [n]
                slot = n % 2
                for hh in range(2):
                    for kb in kbs:
                        c = 1 if kb == n else 0
                        Sb = hh * 2 + c
                        j = hh * 2 + c

                        def fs(e, hh=hh, kb=kb, Sb=Sb):
                            for g in range(4):
                                ins = e.matmul(bank[Sb][:, g * 128:(g + 1) * 128], kT[hh * 64:(hh + 1) * 64, bl(kb)],
                                               q3[hh * 64:(hh + 1) * 64, g, bl(n)], start=True, stop=True)
                            return ins
                        P.op("pe", fs, R=[("kT", kb)] + [("q", g, n) for g in range(4)], W=[("ps", Sb)])
                        P.op("act", lambda e, Sb=Sb, j=j, kb=kb, hh=hh: e.activation(PT[slot][:, j, :], bank[Sb][:], AF.Exp, scale=rk83[:, kb, hh:hh + 1]),
                             R=[("ps", Sb), ("rk8",)], W=[("PT", slot, j)])
                        m = mcur if kb == n else mprev
                        P.op("dve", lambda e, j=j, m=m: e.tensor_tensor(PT[slot][:, j, :].rearrange("p (g t) -> p g t", g=4),
                                                                    PT[slot][:, j, :].rearrange("p (g t) -> p g t", g=4),
                                                                    m.unsqueeze(1).broadcast_to([128, 4, 128]), ALU.mult),
                             R=[("PT", slot, j), ("cmat",)], W=[("PT", slot, j)])

            def pv(n):
                kbs = [n - 1, n] if n > 0 else [n]
                slot = n % 2
                O = 4 + (n % 2)
                Dn = 6 + (n % 2)
                for hh in range(2):
                    def fo(e, hh=hh):
                        for ci, kb in enumerate(kbs):
                            j = hh * 2 + (1 if kb == n else 0)
                            ins = e.matmul(bank[O][hh * 64:(hh + 1) * 64, :], vtm[:, kb, hh * 64:(hh + 1) * 64], PT[slot][:, j, :],
                                           start=(ci == 0), stop=(ci == len(kbs) - 1))
                        return ins
                    rkeys = [("v", kb) for kb in kbs] + [("PT", slot, hh * 2 + (1 if kb == n else 0)) for kb in kbs]
                    P.op("pe", fo, R=rkeys, W=[("ps", O)])

                    def fdn(e, hh=hh):
                        for ci, kb in enumerate(kbs):
                            j = hh * 2 + (1 if kb == n else 0)
                            ins = e.matmul(bank[Dn][hh * 64:(hh + 1) * 64, :], cmat[:, 0:64], PT[slot][:, j, :],
                                           start=(ci == 0), stop=(ci == len(kbs) - 1))
                        return ins
                    P.op("pe", fdn, R=rkeys + [("cmat",)], W=[("ps", Dn)])
            nbox = {}

            def pv_add(n):
                Dn = 6 + (n % 2)
                t = tf_rot.next()
                nbox[n] = t
                t3 = tfr[t][:].rearrange("p (g t) -> p g t", g=4)
                P.op("dve", lambda e: e.tensor_tensor(t3, bank[Dn][:].rearrange("p (g t) -> p g t", g=4),
                                                      esink[:].unsqueeze(2).broadcast_to([128, 4, 128]), ALU.add),
                     R=[("ps", Dn), ("esink",)], W=[("tf", t)])

            def pv_lnexp(n):
                t = nbox[n]
                P.op("act", lambda e: e.activation(tfr[t][:], tfr[t][:], AF.Ln), R=[("tf", t)], W=[("tf", t)])
                P.op("act", lambda e: e.activation(tfr[t][:], tfr[t][:], AF.Exp, scale=-1.0), R=[("tf", t)], W=[("tf", t)])

            def pv_mult(n):
                t = nbox[n]
                O = 4 + (n % 2)
                t3 = tfr[t][:].rearrange("p (g t) -> p g t", g=4)
                P.op("dve", lambda e: e.tensor_tensor(q3[:, :, bl(n)], bank[O][:].rearrange("p (g t) -> p g t", g=4), t3, ALU.mult),
                     R=[("ps", O), ("tf", t)] + [("q", g, n) for g in range(4)], W=[("q", g, n) for g in range(4)])

            scores(0)
            for n in range(NB):
                if n >= 1:
                    pv_add(n - 1)
                if n + 1 < NB:
                    scores(n + 1)
                if n >= 1:
                    pv_lnexp(n - 1)
                    pv_mult(n - 1)
                pv(n)
            pv_add(NB - 1)
            pv_lnexp(NB - 1)
            pv_mult(NB - 1)

            for dc in range(NK):
                wa = ring_load(w4_d[dc], 4)
                wb = ring_load(w4_d[8 + dc], 4)
                wga = ring_load(w8_d[W8_IN + 9 + dc], 8)
                wgb = ring_load(w8_d[W8_IN + 17 + dc], 8)
                for tt in range(NT):
                    base = 4 * ((dc * NT + tt) % 2)
                    Ya, Yb, Ga, Gb = base, base + 1, base + 2, base + 3
                    nr = range(tt * 4, tt * 4 + 4)

                    def fya(e, w=wa, B=Ya, tt=tt):
                        for k in range(4):
                            ins = e.matmul(bank[B][:], w.v[:, k, :], u3[:, k, sl(tt)], start=(k == 0), stop=(k == 3))
                        return ins
                    P.op("pe", fya, R=[rk(wa)] + [("u", g, n) for g in range(4) for n in nr], W=[("ps", Ya)])

                    def fyb(e, w=wb, B=Yb, tt=tt):
                        for k in range(4):
                            ins = e.matmul(bank[B][:], w.v[:, k, :], q3[:, k, sl(tt)], start=(k == 0), stop=(k == 3))
                        return ins
                    P.op("pe", fyb, R=[rk(wb)] + [("q", g, n) for g in range(4) for n in nr], W=[("ps", Yb)])
                    gemm8(Ga, wga, tt)
                    gemm8(Gb, wgb, tt)
                    ta = tf_rot.next()
                    tb = tf_rot.next()
                    P.op("act", lambda e, ta=ta, Ga=Ga: e.activation(tfr[ta][:], bank[Ga][:], AF.Sigmoid), R=[("ps", Ga)], W=[("tf", ta)])
                    P.op("act", lambda e, tb=tb, Gb=Gb: e.activation(tfr[tb][:], bank[Gb][:], AF.Sigmoid), R=[("ps", Gb)], W=[("tf", tb)])
                    P.op("dve", lambda e, ta=ta, Ya=Ya: e.tensor_tensor(tfr[ta][:], bank[Ya][:], tfr[ta][:], ALU.mult),
                         R=[("ps", Ya), ("tf", ta)], W=[("tf", ta)])
                    P.op("dve", lambda e, tb=tb, Yb=Yb: e.tensor_tensor(tfr[tb][:], bank[Yb][:], tfr[tb][:], ALU.mult),
                         R=[("ps", Yb), ("tf", tb)], W=[("tf", tb)])
                    P.op("dve", lambda e, ta=ta, tb=tb, dc=dc, tt=tt: e.tensor_tensor(mg[dc][:, sl(tt)], tfr[ta][:], tfr[tb][:], ALU.add),
                         R=[("tf", ta), ("tf", tb)], W=[("mg", dc, tt)])

            for tiles, tail in ((TA, tailA), (TB, tailB)):
                ninc = NormInc(inc, b, TB, (6, 7)) if (inc is not None and tiles is TB) else None
                for dc in range(NK):
                    wo = ring_load(w8_d[W8_OUT + dc], 8)
                    for tt in tiles:
                        B = ob.next()

                        def fo2(e, w=wo, B=B, tt=tt):
                            for k in range(NK):
                                ins = e.matmul(bank[B][:], w.v[:, k, :], mg[k][:, sl(tt)], start=(k == 0), stop=(k == NK - 1))
                            return ins
                        P.op("pe", fo2, R=[rk(wo)] + [("mg", k, tt) for k in range(NK)], W=[("ps", B)])
                        P.op("dve", lambda e, B=B, dc=dc, tt=tt: e.scalar_tensor_tensor(
                            h[:, dc, sl(tt)], bank[B][:], Gcol(1, b, dc), h[:, dc, sl(tt)], ALU.mult, ALU.add),
                            R=[("ps", B), ("h", dc, tt), ("G", 1, b)], W=[("h", dc, tt)])
                        if ninc:
                            ninc.feed(dc, tt)
                        bg.step(2)
                bg.flush()
                if ninc:
                    bg.add(ninc.finish_steps())
                elif tail:
                    bg.add(tail())

        st_toks = []
        HS = S // 2

        def load_k(b, k, hf, extra=()):
            P.dma("sp", h[:, k, hf * HS:(hf + 1) * HS], xT_d[b, :, k * S + hf * HS:k * S + (hf + 1) * HS], ld_sem[hf * NK + k],
                  W=[("h", k, tt) for tt in (TA if hf == 0 else TB)], extra=extra)

        def store_k(b, k, hf):
            st_toks.append(P.dma("sp", out_d[b, :, k * S + hf * HS:k * S + (hf + 1) * HS], h[:, k, hf * HS:(hf + 1) * HS],
                                 st_sem[hf * NK + k], R=[("h", k, tt) for tt in (TA if hf == 0 else TB)]))

        def swap_dc(b):
            def f(dc, hf):
                if dc < NK:
                    store_k(b, dc, hf)
                if b + 1 < nseq and 0 <= dc - 2 < NK:
                    load_k(b + 1, dc - 2, hf)
            return f

        for k in range(NK):
            load_k(0, k, 0)
        ns = norm_steps(0, 0, TA)
        mtok = None
        for j in range(NSLOT):
            mtok = mods_chunk(j)
        for fstep in ns[0:22]:
            fstep()
        for j in range(NSLOT, 16):
            mtok = mods_chunk(j)
        mods_finish(0, 16, [0], [])
        for k in range(NK):
            load_k(0, k, 1, extra=[mtok])
        for fstep in ns[22:30]:
            fstep()
        bg.add(norm_steps(0, 0, TB))
        mix_last = [P.last_pe]

        def wtm_load():
            P.dma("pool", scr[:, SC_WTM:SC_WTM + 8 * 768], wtm_d, misc_sem[5], W=[("wtm",)], extra=[mix_last[0]])
        for b in range(nseq):
            ffn(0, b, lambda b=b: norm_steps(1, b, TA, bk=3), None, extra_bg=(mods_bg[:9] + [wtm_load] + mods_bg[9:] if b == 0 else None), inc=1)
            mixer(b, lambda b=b: norm_steps(2, b, TA, bk=3), None, inc=2)
            mix_last[0] = P.last_pe
            nxt = b + 1 < nseq
            ffn(1, b, (lambda b=b: norm_steps(0, b + 1, TA)) if nxt else None, (lambda b=b: norm_steps(0, b + 1, TB)) if nxt else None,
                extra_bg=([wtm_load] if nxt else None), after_dc=swap_dc(b))
        bg.flush()
        P.wait("sp", st_toks)
        P.run()
    return nc


def _chunks_fm(W):
    K, N = W.shape
    return np.ascontiguousarray(W.reshape(K // 128, 128, N // 128, 128).transpose(2, 1, 0, 3)).reshape(N // 128, 128, K)


def _prep_shared(inp):
    f = lambda a: np.asarray(a, dtype=np.float32)
    w_ada = f(inp["w_ada"])[0]
    w_in = f(inp["w_in"])[0]
    w8 = np.empty((W8_N, 128, 1024), np.float32)
    w8[W8_ADA:W8_ADA + 72] = _chunks_fm(w_ada)
    for fi, (g, u) in enumerate((("ffn1_w_gate", "ffn1_w_up"), ("ffn2_w_gate", "ffn2_w_up"))):
        base = W8_F1 if fi == 0 else W8_F2
        w8[base:base + 44:2] = _chunks_fm(f(inp[g])[0])
        w8[base + 1:base + 44:2] = _chunks_fm(f(inp[u])[0])
    qcols = []
    for j in range(4):
        qcols += list(range(1024 + j * 64, 1024 + (j + 1) * 64)) + list(range(1024 + (4 + j) * 64, 1024 + (5 + j) * 64))
    w8[W8_IN:W8_IN + 4] = _chunks_fm(w_in[:, qcols])
    w8[W8_IN + 4:W8_IN + 5] = _chunks_fm(w_in[:, 1536:1664])
    w8[W8_IN + 5:W8_IN + 9] = _chunks_fm(w_in[:, 0:512])
    w8[W8_IN + 9:W8_IN + 17] = _chunks_fm(w_in[:, 1792:2816])
    w8[W8_IN + 17:W8_IN + 25] = _chunks_fm(w_in[:, 2816:3840])
    w8[W8_OUT:W8_OUT + 8] = _chunks_fm(f(inp["w_out"])[0])
    w11 = np.empty((32, 128, HC * 128), np.float32)
    for fi, name in enumerate(("ffn1_w_down", "ffn2_w_down")):
        wd = f(inp[name])[0]
        for half in range(2):
            w11[(fi * 2 + half) * 8:(fi * 2 + half) * 8 + 8] = _chunks_fm(wd[half * HC * 128:(half + 1) * HC * 128])
    w4 = np.empty((16, 128, 512), np.float32)
    w4[0:8] = _chunks_fm(f(inp["w_branch_a"])[0])
    wb = f(inp["w_branch_b"])[0]
    rows = []
    for g in range(4):
        for kvh in range(2):
            rows += list(range((kvh * 4 + g) * 64, (kvh * 4 + g + 1) * 64))
    w4[8:16] = _chunks_fm(wb[rows])
    wtm_cols = list(range(512, 1024)) + list(range(1536, 1664)) + list(range(1664, 1792))
    wtm = np.ascontiguousarray(w_in[:, wtm_cols].reshape(8, 128, 768).transpose(1, 0, 2)).reshape(128, 8 * 768)
    lngb = np.empty((128, 1024), np.float32)
    lngb[:, 0:512] = f(inp["g_sgu_ln"])[0][None, :]
    lngb[:, 512:1024] = f(inp["b_sgu_ln"])[0][None, :]
    wsp = f(inp["w_spatial"])[0]
    wst = np.ascontiguousarray(wsp.transpose(2, 0, 1)).reshape(128, 512)
    bsr = np.ascontiguousarray(f(inp["b_spatial"])[0].reshape(1, 512))
    p = np.arange(128)
    cmat = np.zeros((128, 512), np.float32)
    cmat[:, 0:128] = 1.0
    cmat[:, 128:256] = (p[:, None] // 64 == p[None, :] // 64)
    cmat[:, 256:384] = (p[:, None] > p[None, :])
    cmat[:, 384:512] = (p[:, None] <= p[None, :])
    pv = np.zeros((128, PV_N), np.float32)
    pv[:, PV_BADA:PV_BADA + 72] = f(inp["b_ada"])[0].reshape(72, 128).T
    for i, nm in enumerate(("g_norm1", "g_norm2", "g_norm3")):
        pv[:, PV_G1 + 8 * i:PV_G1 + 8 * i + 8] = f(inp[nm])[0].reshape(8, 128).T
    pv[:, PV_GQ] = np.tile(f(inp["g_q"])[0], 2)
    pv[:, PV_GK] = np.tile(f(inp["g_k"])[0], 2)
    sk = f(inp["attn_sinks"])[0]
    for kvh in range(2):
        pv[kvh * 64:(kvh + 1) * 64, PV_SINK:PV_SINK + 4] = sk[kvh * 4:kvh * 4 + 4][None, :]
    return dict(w8=w8, w11=w11, w4=w4, wtm=wtm, lngb=lngb, wst=wst, bsr=bsr, cmat=cmat), pv


def _prep_core(inp, pv, core, nseq=2):
    x = np.asarray(inp["x"], dtype=np.float32)
    c = np.asarray(inp["c"], dtype=np.float32)
    b0 = core * nseq
    xT = np.empty((nseq, 128, NK * S), np.float32)
    pvc = pv.copy()
    for s in range(nseq):
        xT[s] = x[b0 + s].reshape(S, NK, 128).transpose(2, 1, 0).reshape(128, NK * S)
        pvc[:, PV_COND + s:PV_COND + 16:2] = c[b0 + s].reshape(NK, 128).T
    return xT, pvc


_NC_CACHE = {}


def kernel(**inputs):
    shared, pv = _prep_shared(inputs)
    nc = build_program(2)
    in_maps = []
    for core in range(NCORES):
        xT, pvc = _prep_core(inputs, pv, core)
        m = dict(shared)
        m["xT"] = xT
        m["pvec"] = pvc
        in_maps.append(m)
    res = run_bass_kernel_spmd(nc, in_maps, core_ids=list(range(NCORES)))
    out = np.empty((16, S, D), np.float32)
    for core in range(NCORES):
        o = res.results[core]["out"]
        for s in range(2):
            out[core * 2 + s] = o[s].reshape(128, NK, S).transpose(2, 1, 0).reshape(S, D)
    return out
```

```python
import numpy as np
from contextlib import ExitStack
import concourse.bass as bass
import concourse.mybir as mybir
from concourse.bass_utils import run_bass_kernel_spmd

F32 = mybir.dt.float32
BF16 = mybir.dt.bfloat16
AF = mybir.ActivationFunctionType
ALU = mybir.AluOpType
AX = mybir.AxisListType

NCORES = 8
D = 1024
S = 2048
DFF = 2816
NK = 8
NT = 4
NB = 16
TT = 512
HC = 11
NSLOT = 7
EPS = 1e-6

W8_ADA = 0
W8_F1 = 72
W8_F2 = 116
W8_IN = 160
W8_OUT = 185
W8_N = 193

PV_BADA = 0
PV_G1 = 72
PV_GQ = 96
PV_GK = 97
PV_SINK = 98
PV_COND = 102
PV_N = 118

SC_U = 0
SC_VLN = 8192
SC_Q = 16384
SC_KT = 24576
SC_V = 26624
SC_WTM = 28672
SC_N = 34816
MG_OFF = [8192, 10240, 12288, 14336, 24576, 26624, 28672, 30720]


class _Eng:
    def __init__(self):
        self.ops = []
        self.cnt = 0
        self.seen = {}
        self.sem = None


class Prog:
    def __init__(self, nc, es):
        self.nc = nc
        self.es = es
        self.E = {n: _Eng() for n in ("pe", "act", "dve", "pool", "sp")}
        for n, e in self.E.items():
            e.sem = es.enter_context(nc.semaphore("s_" + n))
        self.lastw = {}
        self.readers = {}
        self.nsem = 0
        self.last_pe = None

    def newsem(self, name):
        return self.es.enter_context(self.nc.semaphore(name))

    def _waits(self, e, deps):
        waits = {}
        for d in deps:
            if d is None:
                continue
            s, v = d
            if v > waits.get(s, 0):
                waits[s] = v
        wl = []
        for s, v in waits.items():
            if e.seen.get(s, 0) >= v:
                continue
            e.seen[s] = v
            wl.append((s, v))
        return wl

    def _deps(self, e, R, W, extra):
        deps = list(extra)
        for k in R:
            t = self.lastw.get(k)
            if t is not None:
                deps.append(t)
        pe_sem = self.E["pe"].sem
        for k in W:
            t = self.lastw.get(k)
            if t is not None and not (t[0] is e.sem and e.sem is pe_sem):
                deps.append(t)
            rd = self.readers.get(k)
            if rd:
                for s, v in rd.items():
                    if not (s is e.sem and e.sem is pe_sem):
                        deps.append((s, v))
        return deps

    def _record(self, tok, R, W):
        for k in R:
            rd = self.readers.setdefault(k, {})
            if rd.get(tok[0], 0) < tok[1]:
                rd[tok[0]] = tok[1]
        for k in W:
            self.lastw[k] = tok
            self.readers[k] = {}

    def op(self, en, fn, R=(), W=(), extra=()):
        e = self.E[en]
        wl = self._waits(e, self._deps(e, R, W, extra))
        e.cnt += 1
        tok = (e.sem, e.cnt)
        sem_e = e.sem

        def thunk(eng, wl=wl, fn=fn):
            for s, v in wl:
                eng.wait_ge(s, v)
            fn(eng).then_inc(sem_e, 1)
        e.ops.append(thunk)
        self._record(tok, R, W)
        if en == "pe":
            self.last_pe = tok
        return tok

    def dma(self, en, out, in_, semc, R=(), W=(), extra=()):
        e = self.E[en]
        wl = self._waits(e, self._deps(e, R, W, extra))
        semc[1] += 16
        tok = (semc[0], semc[1])
        sem = semc[0]

        def thunk(eng, wl=wl):
            for s, v in wl:
                eng.wait_ge(s, v)
            eng.dma_start(out=out, in_=in_).then_inc(sem, 16)
        e.ops.append(thunk)
        self._record(tok, R, W)
        return tok

    def wait(self, en, toks):
        e = self.E[en]
        wl = self._waits(e, toks)

        def thunk(eng, wl=wl):
            for s, v in wl:
                eng.wait_ge(s, v)
        e.ops.append(thunk)

    def run(self):
        nc = self.nc
        E = self.E
        with nc.Block() as block:
            @block.tensor
            def _(eng):
                for t in E["pe"].ops:
                    t(eng)

            @block.scalar
            def _(eng):
                for t in E["act"].ops:
                    t(eng)

            @block.vector
            def _(eng):
                for t in E["dve"].ops:
                    t(eng)

            @block.gpsimd
            def _(eng):
                for t in E["pool"].ops:
                    t(eng)

            @block.sync
            def _(eng):
                for t in E["sp"].ops:
                    t(eng)


class Rot:
    def __init__(self, items):
        self.items = items
        self.i = 0

    def next(self):
        x = self.items[self.i % len(self.items)]
        self.i += 1
        return x


def build_program(nseq=2, stop=99):
    nc = bass.Bass("TRN2", target_bir_lowering=False)
    dt = lambda n, s: nc.dram_tensor(n, s, F32, kind="ExternalInput").ap()
    xT_d = dt("xT", [nseq, 128, NK * S])
    w8_d = dt("w8", [W8_N, 128, 1024])
    w11_d = dt("w11", [32, 128, HC * 128])
    w4_d = dt("w4", [16, 128, 512])
    wtm_d = dt("wtm", [128, 8 * 768])
    pvec_d = dt("pvec", [128, PV_N])
    lngb_d = dt("lngb", [128, 1024])
    wst_d = dt("wst", [128, 512])
    bsr_d = dt("bsr", [1, 512])
    cmat_d = dt("cmat", [128, 512])
    out_d = nc.dram_tensor("out", [nseq, 128, NK * S], F32, kind="ExternalOutput").ap()

    with ExitStack() as es:
        P = Prog(nc, es)
        sb = lambda n, s, d: es.enter_context(nc.sbuf_tensor(n, s, d))
        h = sb("h", [128, NK, S], F32)
        xn = sb("xn", [128, NK, S], BF16)
        scr = sb("scr", [128, SC_N], BF16)
        ringbuf = sb("ringbuf", [128, NSLOT * 1024], BF16)
        ring = [ringbuf[:, i * 1024:(i + 1) * 1024] for i in range(NSLOT)]
        sqr = [sb(f"sq{i}", [128, TT], BF16) for i in range(6)]
        rsr = [sb(f"rs{i}", [128, TT], F32) for i in range(2)]
        tfr = [sb(f"tf{i}", [128, TT], F32) for i in range(4)]
        ksqr = [sb(f"ksq{i}", [128, 128], F32) for i in range(1)]
        cmat = sb("cmat_sb", [128, 512], BF16)
        pvec = sb("pvec_sb", [128, PV_N], F32)
        lngb = sb("lngb_sb", [128, 1024], F32)
        wst = sb("wst_sb", [128, 512], BF16)
        bsr = sb("bsr_sb", [1, 512], BF16)
        cond = sb("cond", [128, 16], BF16)
        mods = sb("mods", [128, 144], F32)
        Asb = sb("Asb", [128, 48], F32)
        Gsb = sb("Gsb", [128, 48], F32)
        gqk = sb("gqk", [128, 1], F32)
        esink = sb("esink", [128, 4], F32)
        epst = sb("epst", [128, 2], F32)
        st6 = sb("st6", [128, NB * 6], F32)
        mv = sb("mv", [128, NB * 2], F32)
        ssk = sb("ssk", [128, NB * 2], F32)
        rstdv = sb("rstdv", [128, NB], F32)
        rk8 = sb("rk8", [128, NB * 2], F32)
        nmr = sb("nmr", [128, NB], F32)
        bank = [es.enter_context(nc.psum_tensor(f"bank{i}", [128, TT], F32)) for i in range(8)]

        ones = cmat[:, 0:128]
        bd = cmat[:, 128:256]
        mprev = cmat[:, 256:384]
        mcur = cmat[:, 384:512]

        hT = scr[:, 0:HC * S].rearrange("p (k t) -> p k t", t=S)
        u3 = scr[:, SC_U:SC_U + 4 * S].rearrange("p (g t) -> p g t", t=S)
        vln = scr[:, SC_VLN:SC_VLN + NB * 512].rearrange("p (n c) -> p n c", c=512)
        q3 = scr[:, SC_Q:SC_Q + 4 * S].rearrange("p (g t) -> p g t", t=S)
        kT = scr[:, SC_KT:SC_KT + S]
        vtm = scr[:, SC_V:SC_V + NB * 128].rearrange("p (n c) -> p n c", c=128)
        wtm = scr[:, SC_WTM:SC_WTM + 8 * 768].rearrange("p (k c) -> p k c", c=768)
        PT = [scr[:, SC_WTM + i * 2048:SC_WTM + (i + 1) * 2048].rearrange("p (j t) -> p j t", t=512) for i in range(2)]
        mg = [scr[:, o:o + S] for o in MG_OFF]

        ring_sem = [[P.newsem(f"ringsem{i}"), 0] for i in range(NSLOT)]
        ld_sem = [[P.newsem(f"ldsem{i}"), 0] for i in range(2 * NK)]
        st_sem = [[P.newsem(f"stsem{i}"), 0] for i in range(2 * NK)]
        misc_sem = [[P.newsem(f"miscsem{i}"), 0] for i in range(7)]
        ring_i = [0]
        ring_gen = [0] * NSLOT

        class RW:
            pass

        def ring_load(src, nk):
            i = ring_i[0] % NSLOT
            ring_i[0] += 1
            ring_gen[i] += 1
            dst = ringbuf[:, i * 1024:i * 1024 + nk * 128]
            tok = P.dma("pool", dst, src, ring_sem[i], W=[("ring", i)])
            r = RW()
            r.tok = tok
            r.v = dst.rearrange("p (k c) -> p k c", c=128)
            r.i = i
            r.gen = ring_gen[i]
            return r

        def rk(r):
            assert ring_gen[r.i] == r.gen, "ring slot reloaded before use"
            return ("ring", r.i)

        def ring_load2(src, nk):
            i = ring_i[0] % NSLOT
            if i == NSLOT - 1:
                ring_i[0] += 1
                i = 0
            ring_i[0] += 2
            ring_gen[i] += 1
            ring_gen[i + 1] += 1
            dst = ringbuf[:, i * 1024:i * 1024 + nk * 128]
            tok = P.dma("pool", dst, src, ring_sem[i], W=[("ring", i), ("ring", i + 1)])
            r = RW()
            r.tok = tok
            r.v = dst.rearrange("p (k c) -> p k c", c=128)
            r.i = i
            r.gen = ring_gen[i]
            r.gen2 = ring_gen[i + 1]
            return r

        def rk2(r):
            assert ring_gen[r.i] == r.gen and ring_gen[r.i + 1] == r.gen2, "ring slot pair reloaded before use"
            return [("ring", r.i), ("ring", r.i + 1)]

        class BG:
            def __init__(self):
                self.steps = []

            def add(self, fns):
                self.steps.extend(fns)

            def step(self, n=1):
                for _ in range(n):
                    if self.steps:
                        self.steps.pop(0)()

            def flush(self):
                while self.steps:
                    self.steps.pop(0)()
        bg = BG()

        sq_rot = Rot([0, 1, 2, 3])
        qsq_rot = Rot([4, 5])
        rs_rot = Rot(list(range(2)))
        tf_rot = Rot(list(range(4)))
        ksq_rot = Rot([0])

        def sl(tt):
            return slice(tt * TT, (tt + 1) * TT)

        def bl(n):
            return slice(n * 128, (n + 1) * 128)

        def gemm8(B, w, tt, src=None):
            def f(e):
                for k in range(NK):
                    ins = e.matmul(bank[B][:], w.v[:, k, :], xn[:, k, sl(tt)], start=(k == 0), stop=(k == NK - 1))
                return ins
            return P.op("pe", f, R=[rk(w)] + [("xn", k, tt) for k in range(NK)], W=[("ps", B)])

        P.dma("sp", pvec[:], pvec_d, misc_sem[0], W=[("pvec",)])
        P.dma("sp", lngb[:], lngb_d, misc_sem[1], W=[("lngb",)])
        P.dma("sp", tfr[3][:], wst_d, misc_sem[2], W=[("tf", 3)])
        P.dma("pool", cmat[:], cmat_d, misc_sem[3], W=[("cmat",)])
        P.dma("pool", bsr[:], bsr_d, misc_sem[4], W=[("bsr",)])
        P.op("dve", lambda e: e.memset(epst[:, 0:1], EPS), W=[("eps0",)])
        P.op("dve", lambda e: e.memset(epst[:, 1:2], 64 * EPS), W=[("eps1",)])
        P.op("dve", lambda e: e.tensor_tensor(wst[:].rearrange("p (g t) -> p g t", g=4), tfr[3][:].rearrange("p (g t) -> p g t", g=4),
                                              mcur.unsqueeze(1).broadcast_to([128, 4, 128]), ALU.mult),
             R=[("tf", 3), ("cmat",)], W=[("wst",)])
        P.op("dve", lambda e: e.tensor_tensor(gqk[:], pvec[:, PV_GQ:PV_GQ + 1], pvec[:, PV_GK:PV_GK + 1], ALU.mult),
             R=[("pvec",)], W=[("gqk",)])
        P.op("act", lambda e: e.activation(esink[:], pvec[:, PV_SINK:PV_SINK + 4], AF.Exp), R=[("pvec",)], W=[("esink",)])
        P.op("act", lambda e: e.activation(cond[:], pvec[:, PV_COND:PV_COND + 16], AF.Silu), R=[("pvec",)], W=[("cond",)])

        cond3 = cond[:].rearrange("p (k b) -> p k b", b=2)
        mods3 = mods[:].rearrange("p (j b) -> p j b", b=2)

        def mods_chunk(j):
            w = ring_load(w8_d[W8_ADA + j], 8)

            def f(e):
                for k in range(8):
                    ins = e.matmul(bank[7][:, 2 * j:2 * j + 2], w.v[:, k, :], cond3[:, k, :], start=(k == 0), stop=(k == 7))
                return ins
            P.op("pe", f, R=[rk(w), ("cond",)], W=[("ps7", j)])
            return w.tok

        def mods_finish(j0, j1, As, Gs):
            P.op("dve", lambda e: e.tensor_tensor(mods3[:, j0:j1, :],
                                                  bank[7][:, 2 * j0:2 * j1].rearrange("p (j b) -> p j b", b=2),
                                                  pvec[:, PV_BADA + j0:PV_BADA + j1].unsqueeze(2).broadcast_to([128, j1 - j0, 2]), ALU.add),
                 R=[("ps7", j) for j in range(j0, j1)] + [("pvec",)], W=[("mods", j) for j in range(j0, j1)])
            for s3 in As:
                for b in range(2):
                    o = (s3 * 2 + b) * 8
                    P.op("dve", lambda e, s3=s3, b=b, o=o: e.scalar_tensor_tensor(
                        Asb[:, o:o + 8], mods3[:, (3 * s3 + 1) * 8:(3 * s3 + 2) * 8, b], 1.0,
                        pvec[:, PV_G1 + 8 * s3:PV_G1 + 8 * s3 + 8], ALU.add, ALU.mult),
                        R=[("mods", j) for j in range((3 * s3 + 1) * 8, (3 * s3 + 2) * 8)] + [("pvec",)], W=[("A", s3, b)])
            for s3 in Gs:
                for b in range(2):
                    o = (s3 * 2 + b) * 8
                    P.op("dve", lambda e, s3=s3, b=b, o=o: e.tensor_scalar(
                        Gsb[:, o:o + 8], mods3[:, (3 * s3 + 2) * 8:(3 * s3 + 3) * 8, b], (1.0 if s3 == 1 else 0.5), None, ALU.mult),
                        R=[("mods", j) for j in range((3 * s3 + 2) * 8, (3 * s3 + 3) * 8)], W=[("G", s3, b)])

        mods_bg = ([(lambda j=j: mods_chunk(j)) for j in range(16, 24)] + [lambda: mods_finish(16, 24, [], [0])] +
                   [(lambda j=j: mods_chunk(j)) for j in range(24, 72)] + [lambda: mods_finish(24, 72, [1, 2], [1, 2])])

        def Acol(s3, b, k):
            o = (s3 * 2 + b) * 8 + k
            return Asb[:, o:o + 1]

        def Gcol(s3, b, k):
            o = (s3 * 2 + b) * 8 + k
            return Gsb[:, o:o + 1]

        def Bcol(s3, b, k):
            return mods3[:, 3 * s3 * 8 + k, b:b + 1]

        def mk_ln_step(tt, box, bk):
            def f():
                r = rs_rot.next()
                box["r"] = r
                P.op("act", lambda e: e.activation(rsr[r][:], bank[bk][:], AF.Ln, bias=epst[:, 0:1], scale=1.0 / D),
                     R=[("ps", bk), ("eps0",)], W=[("rs", r)])
                P.op("act", lambda e: e.activation(rsr[r][:], rsr[r][:], AF.Exp, scale=-0.5), R=[("rs", r)], W=[("rs", r)])
            return f

        def mk_aff_step(s3, b, k0, tt, box, dve_only=False):
            def f():
                r = box["r"]
                for k in (k0, k0 + 1):
                    t = tf_rot.next()
                    P.op("dve", lambda e, t=t, k=k: e.scalar_tensor_tensor(tfr[t][:], h[:, k, sl(tt)], Acol(s3, b, k), rsr[r][:],
                                                                       ALU.mult, ALU.mult),
                         R=[("h", k, tt), ("rs", r), ("A", s3, b)], W=[("tf", t)])
                    if dve_only:
                        P.op("dve", lambda e, t=t, k=k: e.tensor_scalar(xn[:, k, sl(tt)], tfr[t][:], Bcol(s3, b, k), None, ALU.add),
                             R=[("tf", t), ("mods", 3 * s3 * 8 + k)], W=[("xn", k, tt)])
                    else:
                        P.op("act", lambda e, t=t, k=k: e.activation(xn[:, k, sl(tt)], tfr[t][:], AF.Identity, bias=Bcol(s3, b, k), scale=1.0),
                             R=[("tf", t), ("mods", 3 * s3 * 8 + k)], W=[("xn", k, tt)])
            return f

        def emit_square(i, k, tt):
            if k % 2 == 0:
                P.op("act", lambda e: e.activation(sqr[i][:], h[:, k, sl(tt)], AF.Square), R=[("h", k, tt)], W=[("sq", i)])
            else:
                P.op("dve", lambda e: e.tensor_tensor(sqr[i][:], h[:, k, sl(tt)], h[:, k, sl(tt)], ALU.mult), R=[("h", k, tt)], W=[("sq", i)])

        def norm_steps(s3, b, tiles, bk=6):
            steps = []
            boxes = {tt: {} for tt in tiles}
            for tt in tiles:
                pend = []

                def emit_mm1(tt=tt, pend=pend):
                    i, k = pend.pop(0)
                    P.op("pe", lambda e: e.matmul(bank[bk][:], ones, sqr[i][:], start=(k == 0), stop=(k == NK - 1)),
                         R=[("sq", i), ("cmat",)], W=[("ps", bk)])

                def sq_step(k, tt=tt, pend=pend, emit_mm1=emit_mm1):
                    def f():
                        if len(pend) >= 2:
                            emit_mm1()
                        i = sq_rot.next()
                        emit_square(i, k, tt)
                        pend.append((i, k))
                    return f
                for k in range(NK):
                    steps.append(sq_step(k))
                steps.append(emit_mm1)
                steps.append(emit_mm1)
                steps.append(mk_ln_step(tt, boxes[tt], bk))
            for tt in tiles:
                for k0 in range(0, NK, 2):
                    steps.append(mk_aff_step(s3, b, k0, tt, boxes[tt]))
            return steps

        class NormInc:
            def __init__(self, s3, b, tiles, banks):
                self.s3, self.b, self.tiles = s3, b, tiles
                self.B = dict(zip(tiles, banks))
                self.pend = []
                self.cnt = {tt: 0 for tt in tiles}

            def _mm(self):
                while self.pend:
                    i, k, tt = self.pend.pop(0)
                    n = self.cnt[tt]
                    self.cnt[tt] += 1
                    bk = self.B[tt]
                    P.op("pe", lambda e, i=i, n=n, bk=bk: e.matmul(bank[bk][:], ones, sqr[i][:], start=(n == 0), stop=(n == NK - 1)),
                         R=[("sq", i), ("cmat",)], W=[("ps", bk)])

            def feed(self, k, tt):
                self._mm()
                i = qsq_rot.next()
                emit_square(i, k, tt)
                self.pend.append((i, k, tt))

            def finish_steps(self):
                steps = [self._mm]
                boxes = {tt: {} for tt in self.tiles}
                for tt in self.tiles:
                    steps.append(mk_ln_step(tt, boxes[tt], self.B[tt]))
                na = 0
                for tt in self.tiles:
                    for k0 in range(0, NK, 2):
                        steps.append(mk_aff_step(self.s3, self.b, k0, tt, boxes[tt], dve_only=(self.s3 == 1 and na < 2)))
                        na += 1
                return steps

        TA = (0, 1)
        TB = (2, 3)

        def ffn(f, b, tailA, tailB, extra_bg=None, inc=None, after_dc=None):
            s3 = 0 if f == 0 else 2
            gb = Rot([0, 1, 4])
            ub = Rot([2, 3, 5])
            db = Rot([4, 5, 0, 1, 2])
            base8 = W8_F1 if f == 0 else W8_F2

            def g1_iter(wg, wu, hc, tt):
                G = gb.next()
                U = ub.next()
                gemm8(G, wg, tt)
                gemm8(U, wu, tt)
                t = tf_rot.next()
                P.op("act", lambda e: e.activation(tfr[t][:], bank[G][:], AF.Silu), R=[("ps", G)], W=[("tf", t)])
                P.op("dve", lambda e: e.tensor_tensor(hT[:, hc, sl(tt)], tfr[t][:], bank[U][:], ALU.mult),
                     R=[("tf", t), ("ps", U)], W=[("hT", hc, tt)])

            def wd_load(half, dc):
                return ring_load2(w11_d[(f * 2 + half) * 8 + dc], HC)

            def g2_iter(wd, dc, tt):
                assert ("G", s3, b) in P.lastw
                Dk = db.next()

                def fd(e):
                    for kk in range(HC):
                        ins = e.matmul(bank[Dk][:], wd.v[:, kk, :], hT[:, kk, sl(tt)], start=(kk == 0), stop=(kk == HC - 1))
                    return ins
                P.op("pe", fd, R=rk2(wd) + [("hT", kk, tt) for kk in range(HC)], W=[("ps", Dk)])
                P.op("dve", lambda e: e.scalar_tensor_tensor(
                    h[:, dc, sl(tt)], bank[Dk][:], Gcol(s3, b, dc), h[:, dc, sl(tt)], ALU.mult, ALU.add),
                    R=[("ps", Dk), ("h", dc, tt), ("G", s3, b)], W=[("h", dc, tt)])

            for half in range(2):
                hcs = list(range(HC))
                if half == 0:
                    NST = 3
                    ws = []
                    for hc in range(NST):
                        ws.append((ring_load(w8_d[base8 + hc * 2], 8), ring_load(w8_d[base8 + hc * 2 + 1], 8)))
                    for tt in TA:
                        for hc in range(NST):
                            bg.step(5)
                            g1_iter(ws[hc][0], ws[hc][1], hc, tt)
                    bg.flush()
                    for tt in TB:
                        for hc in range(NST):
                            g1_iter(ws[hc][0], ws[hc][1], hc, tt)
                    hcs = list(range(NST, HC))
                    if extra_bg:
                        bg.add(extra_bg)
                for hc in hcs:
                    ghc = half * HC + hc
                    wg = ring_load(w8_d[base8 + ghc * 2], 8)
                    wu = ring_load(w8_d[base8 + ghc * 2 + 1], 8)
                    for tt in range(NT):
                        g1_iter(wg, wu, hc, tt)
                        if tt % 2 == 1 or (tt == 2 and len(bg.steps) > 24):
                            bg.step(1)
                if half == 0:
                    for dc in range(NK):
                        wd = wd_load(half, dc)
                        for tt in range(NT):
                            g2_iter(wd, dc, tt)
                            if tt % 2 == 1:
                                bg.step(1)
                else:
                    bg.flush()
                    for tiles, tail in ((TA, tailA), (TB, tailB)):
                        ninc = NormInc(inc, b, TB, (6, 7)) if (inc is not None and tiles is TB) else None
                        for dc in range(NK):
                            wd = wd_load(half, dc)
                            for tt in tiles:
                                g2_iter(wd, dc, tt)
                                if ninc:
                                    ninc.feed(dc, tt)
                                bg.step(2)
                            if after_dc:
                                after_dc(dc, 0 if tiles is TA else 1)
                        if after_dc:
                            after_dc(NK, 0 if tiles is TA else 1)
                            after_dc(NK + 1, 0 if tiles is TA else 1)
                        bg.flush()
                        if ninc:
                            bg.add(ninc.finish_steps())
                        elif tail:
                            bg.add(tail())

        def mixer(b, tailA, tailB, inc=None):
            ob = Rot([0, 1, 2, 4, 5])
            ob3 = Rot([0, 1, 4, 5])
            sb2 = Rot([2, 3])
            pend = []

            def q_stage2(g, tt, B, i):
                Sq = sb2.next()
                P.op("pe", lambda e: e.matmul(bank[Sq][:], bd, sqr[i][:], start=True, stop=True),
                     R=[("sq", i), ("cmat",)], W=[("ps", Sq)])
                r = tf_rot.next()
                P.op("act", lambda e: e.activation(tfr[r][:], bank[Sq][:], AF.Ln, bias=epst[:, 0:1], scale=1.0 / 64),
                     R=[("ps", Sq), ("eps0",)], W=[("tf", r)])
                P.op("act", lambda e: e.activation(tfr[r][:], tfr[r][:], AF.Exp, scale=-0.5), R=[("tf", r)], W=[("tf", r)])
                P.op("dve", lambda e: e.scalar_tensor_tensor(q3[:, g, sl(tt)], bank[B][:], gqk[:, 0:1], tfr[r][:], ALU.mult, ALU.mult),
                     R=[("ps", B), ("tf", r), ("gqk",)], W=[("q", g, n) for n in range(tt * 4, tt * 4 + 4)])

            wq = [ring_load(w8_d[W8_IN + g], 8) for g in range(5)]

            def m2a(tiles, nstep):
                it = [0]

                def bstep():
                    it[0] += 1
                    if nstep >= 1:
                        bg.step(nstep)
                    elif nstep > 0 and it[0] % 2 == 1:
                        bg.step(1)
                for tt in tiles:
                    for g in range(5):
                        B = ob3.next()
                        gemm8(B, wq[g], tt)
                        if g < 4:
                            i = qsq_rot.next()
                            P.op("act", lambda e, i=i, B=B: e.activation(sqr[i][:], bank[B][:], AF.Square), R=[("ps", B)], W=[("sq", i)])
                            bstep()
                            if pend:
                                q_stage2(*pend.pop())
                            pend.append((g, tt, B, i))
                        else:
                            P.op("act", lambda e, B=B, tt=tt: e.activation(kT[:, sl(tt)], bank[B][:], AF.Copy), R=[("ps", B)],
                                 W=[("kT", n) for n in range(tt * 4, tt * 4 + 4)])
                            bstep()
                q_stage2(*pend.pop())

            avb = Rot([0, 1])
            kvb = Rot([2, 3])
            st3 = st6[:].rearrange("p (n c) -> p n c", c=6)
            mv3 = mv[:].rearrange("p (n c) -> p n c", c=2)
            ssk3 = ssk[:].rearrange("p (n c) -> p n c", c=2)
            rk83 = rk8[:].rearrange("p (n c) -> p n c", c=2)
            def m2b(blocks, nstep):
              for n in blocks:
                tt = n // 4
                AV = avb.next()
                KV = kvb.next()

                def ftm(e, AV=AV, KV=KV, n=n):
                    for k in range(NK):
                        e.matmul(bank[AV][:], xn[:, k, bl(n)], wtm[:, k, 0:512], start=(k == 0), stop=(k == NK - 1))
                        ins = e.matmul(bank[KV][:, 0:256], xn[:, k, bl(n)], wtm[:, k, 512:768], start=(k == 0), stop=(k == NK - 1))
                    return ins
                P.op("pe", ftm, R=[("wtm",)] + [("xn", k, tt) for k in range(NK)], W=[("ps", AV), ("ps", KV)])
                P.op("act", lambda e, AV=AV, n=n: e.activation(vln[:, n, :], bank[AV][:], AF.Gelu_apprx_tanh), R=[("ps", AV)], W=[("vln", n)])
                ki = ksq_rot.next()
                P.op("act", lambda e, KV=KV, ki=ki: e.activation(ksqr[ki][:], bank[KV][:, 0:128], AF.Square), R=[("ps", KV)], W=[("ksq", ki)])
                P.op("act", lambda e, KV=KV, n=n: e.activation(vtm[:, n, :], bank[KV][:, 128:256], AF.Copy), R=[("ps", KV)], W=[("v", n)])
                P.op("dve", lambda e, n=n: e.bn_stats(st3[:, n, :], vln[:, n, :]), R=[("vln", n)], W=[("st", n)])
                P.op("dve", lambda e, n=n: e.bn_aggr(mv3[:, n, :], st3[:, n, :]), R=[("st", n)], W=[("mv", n)])
                P.op("dve", lambda e, n=n, ki=ki: e.reduce_sum(ssk3[:, n, :], ksqr[ki][:].rearrange("p (h d) -> p h d", h=2), AX.X),
                     R=[("ksq", ki)], W=[("ssk", n)])
                bg.step(nstep)

            m2a(TA, 0.5)
            m2b(range(0, NB // 2), 1)
            bg.flush()
            m2a(TB, 0)
            m2b(range(NB // 2, NB), 0)

            P.op("act", lambda e: e.activation(rstdv[:], mv3[:, :, 1], AF.Ln, bias=epst[:, 0:1], scale=1.0),
                 R=[("mv", n) for n in range(NB)] + [("eps0",)], W=[("rstdv",)])
            P.op("act", lambda e: e.activation(rstdv[:], rstdv[:], AF.Exp, scale=-0.5), R=[("rstdv",)], W=[("rstdv",)])
            P.op("act", lambda e: e.activation(rk8[:], ssk[:], AF.Ln, bias=epst[:, 1:2], scale=1.0),
                 R=[("ssk", n) for n in range(NB)] + [("eps1",)], W=[("rk8",)])
            P.op("act", lambda e: e.activation(rk8[:], rk8[:], AF.Exp, scale=-0.5), R=[("rk8",)], W=[("rk8",)])

            P.op("dve", lambda e: e.scalar_tensor_tensor(nmr[:], mv3[:, :, 0], -1.0, rstdv[:], ALU.mult, ALU.mult),
                 R=[("mv", n) for n in range(NB)] + [("rstdv",)], W=[("nmr",)])

            def ln_fin(n):
                def f():
                    P.op("act", lambda e: e.activation(vln[:, n, :], vln[:, n, :], AF.Identity, bias=nmr[:, n:n + 1], scale=rstdv[:, n:n + 1]),
                         R=[("vln", n), ("nmr",), ("rstdv",)], W=[("vln", n)])
                    P.op("dve", lambda e: e.tensor_tensor(vln[:, n, :], vln[:, n, :], lngb[:, 0:512], ALU.mult),
                         R=[("vln", n), ("lngb",)], W=[("vln", n)])
                    P.op("dve", lambda e: e.tensor_tensor(vln[:, n, :], vln[:, n, :], lngb[:, 512:1024], ALU.add),
                         R=[("vln", n), ("lngb",)], W=[("vln", n)])
                return f
            bg.add([ln_fin(n) for n in range(NB)])

            zb = Rot([4, 5])
            ub4 = Rot([0, 1, 2, 3])
            wst3 = wst[:].rearrange("p (g t) -> p g t", g=4)

            def sgu(n):
                assert ("vln", n) in P.lastw
                Z = zb.next()

                def fz(e):
                    e.matmul(bank[Z][:, 0:512], cmat[0:1, 0:128], bsr[0:1, 0:512], start=True, stop=False)
                    for g in range(4):
                        ins = e.matmul(bank[Z][:, g * 128:(g + 1) * 128], vln[:, n, g * 128:(g + 1) * 128], wst3[:, g, :], start=False, stop=(g == 3))
                    return ins
                P.op("pe", fz, R=[("vln", n), ("wst",), ("bsr",), ("cmat",)], W=[("ps", Z)])
                P.op("dve", lambda e: e.tensor_tensor(u3[:, :, bl(n)], bank[Z][:].rearrange("p (g t) -> p g t", g=4), u3[:, :, bl(n)], ALU.mult),
                     R=[("ps", Z)] + [("u", g, n) for g in range(4)], W=[("u", g, n) for g in range(4)])

            wu4 = [ring_load(w8_d[W8_IN + 5 + g], 8) for g in range(4)]
            for tt in range(NT):
                for g in range(4):
                    B = ub4.next()
                    gemm8(B, wu4[g], tt)
                    P.op("act", lambda e, B=B, g=g, tt=tt: e.activation(u3[:, g, sl(tt)], bank[B][:], AF.Gelu_apprx_tanh), R=[("ps", B)],
                         W=[("u", g, n) for n in range(tt * 4, tt * 4 + 4)])
                    bg.step(1)
                    if tt >= 1:
                        sgu((tt - 1) * 4 + g)
            bg.flush()
            for n in range(NB - 4, NB):
                sgu(n)

            def scores(n):
                kbs = [n - 1, n] if n > 0 else [n]
                slot = n % 2
                for hh in range(2):
                    for kb in kbs:
                        c = 1 if kb == n else 0
                        Sb = hh * 2 + c
                        j = hh * 2 + c

                        def fs(e, hh=hh, kb=kb, Sb=Sb):
                            for g in range(4):
                                ins = e.matmul(bank[Sb][:, g * 128:(g + 1) * 128], kT[hh * 64:(hh + 1) * 64, bl(kb)],
                                               q3[hh * 64:(hh + 1) * 64, g, bl(n)], start=True, stop=True)
                            return ins
                        P.op("pe", fs, R=[("kT", kb)] + [("q", g, n) for g in range(4)], W=[("ps", Sb)])
                        P.op("act", lambda e, Sb=Sb, j=j, kb=kb, hh=hh: e.activation(PT[slot][:, j, :], bank[Sb][:], AF.Exp, scale=rk83[:, kb, hh:hh + 1]),
                             R=[("ps", Sb), ("rk8",)], W=[("PT", slot, j)])
                        m = mcur if kb == n else mprev
                        P.op("dve", lambda e, j=j, m=m: e.tensor_tensor(PT[slot][:, j, :].rearrange("p (g t) -> p g t", g=4),
                                                                    PT[slot][:, j, :].rearrange("p (g t) -> p g t", g=4),
                                                                    m.unsqueeze(1).broadcast_to([128, 4, 128]), ALU.mult),
                             R=[("PT", slot, j), ("cmat",)], W=[("PT", slot, j)])

            def pv(n):
                kbs = [n - 1, n] if n > 0 else [n]
                slot = n % 2
                O = 4 + (n % 2)
                Dn = 6 + (n % 2)
                for hh in range(2):
                    def fo(e, hh=hh):
                        for ci, kb in enumerate(kbs):
                            j = hh * 2 + (1 if kb == n else 0)
                            ins = e.matmul(bank[O][hh * 64:(hh + 1) * 64, :], vtm[:, kb, hh * 64:(hh + 1) * 64], PT[slot][:, j, :],
                                           start=(ci == 0), stop=(ci == len(kbs) - 1))
                        return ins
                    rkeys = [("v", kb) for kb in kbs] + [("PT", slot, hh * 2 + (1 if kb == n else 0)) for kb in kbs]
                    P.op("pe", fo, R=rkeys, W=[("ps", O)])

                    def fdn(e, hh=hh):
                        for ci, kb in enumerate(kbs):
                            j = hh * 2 + (1 if kb == n else 0)
                            ins = e.matmul(bank[Dn][hh * 64:(hh + 1) * 64, :], cmat[:, 0:64], PT[slot][:, j, :],
                                           start=(ci == 0), stop=(ci == len(kbs) - 1))
                        return ins
                    P.op("pe", fdn, R=rkeys + [("cmat",)], W=[("ps", Dn)])
            nbox = {}

            def pv_add(n):
                Dn = 6 + (n % 2)
                t = tf_rot.next()
                nbox[n] = t
                t3 = tfr[t][:].rearrange("p (g t) -> p g t", g=4)
                P.op("dve", lambda e: e.tensor_tensor(t3, bank[Dn][:].rearrange("p (g t) -> p g t", g=4),
                                                      esink[:].unsqueeze(2).broadcast_to([128, 4, 128]), ALU.add),
                     R=[("ps", Dn), ("esink",)], W=[("tf", t)])

            def pv_lnexp(n):
                t = nbox[n]
                P.op("act", lambda e: e.activation(tfr[t][:], tfr[t][:], AF.Ln), R=[("tf", t)], W=[("tf", t)])
                P.op("act", lambda e: e.activation(tfr[t][:], tfr[t][:], AF.Exp, scale=-1.0), R=[("tf", t)], W=[("tf", t)])

            def pv_mult(n):
                t = nbox[n]
                O = 4 + (n % 2)
                t3 = tfr[t][:].rearrange("p (g t) -> p g t", g=4)
                P.op("dve", lambda e: e.tensor_tensor(q3[:, :, bl(n)], bank[O][:].rearrange("p (g t) -> p g t", g=4), t3, ALU.mult),
                     R=[("ps", O), ("tf", t)] + [("q", g, n) for g in range(4)], W=[("q", g, n) for g in range(4)])

            scores(0)
            for n in range(NB):
                if n >= 1:
                    pv_add(n - 1)
                if n + 1 < NB:
                    scores(n + 1)
                if n >= 1:
                    pv_lnexp(n - 1)
                    pv_mult(n - 1)
                pv(n)
            pv_add(NB - 1)
            pv_lnexp(NB - 1)
            pv_mult(NB - 1)

            for dc in range(NK):
                wa = ring_load(w4_d[dc], 4)
                wb = ring_load(w4_d[8 + dc], 4)
                wga = ring_load(w8_d[W8_IN + 9 + dc], 8)
                wgb = ring_load(w8_d[W8_IN + 17 + dc], 8)
                for tt in range(NT):
                    base = 4 * ((dc * NT + tt) % 2)
                    Ya, Yb, Ga, Gb = base, base + 1, base + 2, base + 3
                    nr = range(tt * 4, tt * 4 + 4)

                    def fya(e, w=wa, B=Ya, tt=tt):
                        for k in range(4):
                            ins = e.matmul(bank[B][:], w.v[:, k, :], u3[:, k, sl(tt)], start=(k == 0), stop=(k == 3))
                        return ins
                    P.op("pe", fya, R=[rk(wa)] + [("u", g, n) for g in range(4) for n in nr], W=[("ps", Ya)])

                    def fyb(e, w=wb, B=Yb, tt=tt):
                        for k in range(4):
                            ins = e.matmul(bank[B][:], w.v[:, k, :], q3[:, k, sl(tt)], start=(k == 0), stop=(k == 3))
                        return ins
                    P.op("pe", fyb, R=[rk(wb)] + [("q", g, n) for g in range(4) for n in nr], W=[("ps", Yb)])
                    gemm8(Ga, wga, tt)
                    gemm8(Gb, wgb, tt)
                    ta = tf_rot.next()
                    tb = tf_rot.next()
                    P.op("act", lambda e, ta=ta, Ga=Ga: e.activation(tfr[ta][:], bank[Ga][:], AF.Sigmoid), R=[("ps", Ga)], W=[("tf", ta)])
                    P.op("act", lambda e, tb=tb, Gb=Gb: e.activation(tfr[tb][:], bank[Gb][:], AF.Sigmoid), R=[("ps", Gb)], W=[("tf", tb)])
                    P.op("dve", lambda e, ta=ta, Ya=Ya: e.tensor_tensor(tfr[ta][:], bank[Ya][:], tfr[ta][:], ALU.mult),
                         R=[("ps", Ya), ("tf", ta)], W=[("tf", ta)])
                    P.op("dve", lambda e, tb=tb, Yb=Yb: e.tensor_tensor(tfr[tb][:], bank[Yb][:], tfr[tb][:], ALU.mult),
                         R=[("ps", Yb), ("tf", tb)], W=[("tf", tb)])
                    P.op("dve", lambda e, ta=ta, tb=tb, dc=dc, tt=tt: e.tensor_tensor(mg[dc][:, sl(tt)], tfr[ta][:], tfr[tb][:], ALU.add),
                         R=[("tf", ta), ("tf", tb)], W=[("mg", dc, tt)])

            for tiles, tail in ((TA, tailA), (TB, tailB)):
                ninc = NormInc(inc, b, TB, (6, 7)) if (inc is not None and tiles is TB) else None
                for dc in range(NK):
                    wo = ring_load(w8_d[W8_OUT + dc], 8)
                    for tt in tiles:
                        B = ob.next()

                        def fo2(e, w=wo, B=B, tt=tt):
                            for k in range(NK):
                                ins = e.matmul(bank[B][:], w.v[:, k, :], mg[k][:, sl(tt)], start=(k == 0), stop=(k == NK - 1))
                            return ins
                        P.op("pe", fo2, R=[rk(wo)] + [("mg", k, tt) for k in range(NK)], W=[("ps", B)])
                        P.op("dve", lambda e, B=B, dc=dc, tt=tt: e.scalar_tensor_tensor(
                            h[:, dc, sl(tt)], bank[B][:], Gcol(1, b, dc), h[:, dc, sl(tt)], ALU.mult, ALU.add),
                            R=[("ps", B), ("h", dc, tt), ("G", 1, b)], W=[("h", dc, tt)])
                        if ninc:
                            ninc.feed(dc, tt)
                        bg.step(2)
                bg.flush()
                if ninc:
                    bg.add(ninc.finish_steps())
                elif tail:
                    bg.add(tail())

        st_toks = []
        HS = S // 2

        def load_k(b, k, hf, extra=()):
            P.dma("sp", h[:, k, hf * HS:(hf + 1) * HS], xT_d[b, :, k * S + hf * HS:k * S + (hf + 1) * HS], ld_sem[hf * NK + k],
                  W=[("h", k, tt) for tt in (TA if hf == 0 else TB)], extra=extra)

        def store_k(b, k, hf):
            st_toks.append(P.dma("sp", out_d[b, :, k * S + hf * HS:k * S + (hf + 1) * HS], h[:, k, hf * HS:(hf + 1) * HS],
                                 st_sem[hf * NK + k], R=[("h", k, tt) for tt in (TA if hf == 0 else TB)]))

        def swap_dc(b):
            def f(dc, hf):
                if dc < NK:
                    store_k(b, dc, hf)
                if b + 1 < nseq and 0 <= dc - 2 < NK:
                    load_k(b + 1, dc - 2, hf)
            return f

        for k in range(NK):
            load_k(0, k, 0)
        ns = norm_steps(0, 0, TA)
        mtok = None
        for j in range(NSLOT):
            mtok = mods_chunk(j)
        for fstep in ns[0:22]:
            fstep()
        for j in range(NSLOT, 16):
            mtok = mods_chunk(j)
        mods_finish(0, 16, [0], [])
        for k in range(NK):
            load_k(0, k, 1, extra=[mtok])
        for fstep in ns[22:30]:
            fstep()
        bg.add(norm_steps(0, 0, TB))
        mix_last = [P.last_pe]

        def wtm_load():
            P.dma("pool", scr[:, SC_WTM:SC_WTM + 8 * 768], wtm_d, misc_sem[5], W=[("wtm",)], extra=[mix_last[0]])
        for b in range(nseq):
            ffn(0, b, lambda b=b: norm_steps(1, b, TA, bk=3), None, extra_bg=(mods_bg[:9] + [wtm_load] + mods_bg[9:] if b == 0 else None), inc=1)
            mixer(b, lambda b=b: norm_steps(2, b, TA, bk=3), None, inc=2)
            mix_last[0] = P.last_pe
            nxt = b + 1 < nseq
            ffn(1, b, (lambda b=b: norm_steps(0, b + 1, TA)) if nxt else None, (lambda b=b: norm_steps(0, b + 1, TB)) if nxt else None,
                extra_bg=([wtm_load] if nxt else None), after_dc=swap_dc(b))
        bg.flush()
        P.wait("sp", st_toks)
        P.run()
    return nc


def _chunks_fm(W):
    K, N = W.shape
    return np.ascontiguousarray(W.reshape(K // 128, 128, N // 128, 128).transpose(2, 1, 0, 3)).reshape(N // 128, 128, K)


def _prep_shared(inp):
    f = lambda a: np.asarray(a, dtype=np.float32)
    w_ada = f(inp["w_ada"])[0]
    w_in = f(inp["w_in"])[0]
    w8 = np.empty((W8_N, 128, 1024), np.float32)
    w8[W8_ADA:W8_ADA + 72] = _chunks_fm(w_ada)
    for fi, (g, u) in enumerate((("ffn1_w_gate", "ffn1_w_up"), ("ffn2_w_gate", "ffn2_w_up"))):
        base = W8_F1 if fi == 0 else W8_F2
        w8[base:base + 44:2] = _chunks_fm(f(inp[g])[0])
        w8[base + 1:base + 44:2] = _chunks_fm(f(inp[u])[0])
    qcols = []
    for j in range(4):
        qcols += list(range(1024 + j * 64, 1024 + (j + 1) * 64)) + list(range(1024 + (4 + j) * 64, 1024 + (5 + j) * 64))
    w8[W8_IN:W8_IN + 4] = _chunks_fm(w_in[:, qcols])
    w8[W8_IN + 4:W8_IN + 5] = _chunks_fm(w_in[:, 1536:1664])
    w8[W8_IN + 5:W8_IN + 9] = _chunks_fm(w_in[:, 0:512])
    w8[W8_IN + 9:W8_IN + 17] = _chunks_fm(w_in[:, 1792:2816])
    w8[W8_IN + 17:W8_IN + 25] = _chunks_fm(w_in[:, 2816:3840])
    w8[W8_OUT:W8_OUT + 8] = _chunks_fm(f(inp["w_out"])[0])
    w11 = np.empty((32, 128, HC * 128), np.float32)
    for fi, name in enumerate(("ffn1_w_down", "ffn2_w_down")):
        wd = f(inp[name])[0]
        for half in range(2):
            w11[(fi * 2 + half) * 8:(fi * 2 + half) * 8 + 8] = _chunks_fm(wd[half * HC * 128:(half + 1) * HC * 128])
    w4 = np.empty((16, 128, 512), np.float32)
    w4[0:8] = _chunks_fm(f(inp["w_branch_a"])[0])
    wb = f(inp["w_branch_b"])[0]
    rows = []
    for g in range(4):
        for kvh in range(2):
            rows += list(range((kvh * 4 + g) * 64, (kvh * 4 + g + 1) * 64))
    w4[8:16] = _chunks_fm(wb[rows])
    wtm_cols = list(range(512, 1024)) + list(range(1536, 1664)) + list(range(1664, 1792))
    wtm = np.ascontiguousarray(w_in[:, wtm_cols].reshape(8, 128, 768).transpose(1, 0, 2)).reshape(128, 8 * 768)
    lngb = np.empty((128, 1024), np.float32)
    lngb[:, 0:512] = f(inp["g_sgu_ln"])[0][None, :]
    lngb[:, 512:1024] = f(inp["b_sgu_ln"])[0][None, :]
    wsp = f(inp["w_spatial"])[0]
    wst = np.ascontiguousarray(wsp.transpose(2, 0, 1)).reshape(128, 512)
    bsr = np.ascontiguousarray(f(inp["b_spatial"])[0].reshape(1, 512))
    p = np.arange(128)
    cmat = np.zeros((128, 512), np.float32)
    cmat[:, 0:128] = 1.0
    cmat[:, 128:256] = (p[:, None] // 64 == p[None, :] // 64)
    cmat[:, 256:384] = (p[:, None] > p[None, :])
    cmat[:, 384:512] = (p[:, None] <= p[None, :])
    pv = np.zeros((128, PV_N), np.float32)
    pv[:, PV_BADA:PV_BADA + 72] = f(inp["b_ada"])[0].reshape(72, 128).T
    for i, nm in enumerate(("g_norm1", "g_norm2", "g_norm3")):
        pv[:, PV_G1 + 8 * i:PV_G1 + 8 * i + 8] = f(inp[nm])[0].reshape(8, 128).T
    pv[:, PV_GQ] = np.tile(f(inp["g_q"])[0], 2)
    pv[:, PV_GK] = np.tile(f(inp["g_k"])[0], 2)
    sk = f(inp["attn_sinks"])[0]
    for kvh in range(2):
        pv[kvh * 64:(kvh + 1) * 64, PV_SINK:PV_SINK + 4] = sk[kvh * 4:kvh * 4 + 4][None, :]
    return dict(w8=w8, w11=w11, w4=w4, wtm=wtm, lngb=lngb, wst=wst, bsr=bsr, cmat=cmat), pv


def _prep_core(inp, pv, core, nseq=2):
    x = np.asarray(inp["x"], dtype=np.float32)
    c = np.asarray(inp["c"], dtype=np.float32)
    b0 = core * nseq
    xT = np.empty((nseq, 128, NK * S), np.float32)
    pvc = pv.copy()
    for s in range(nseq):
        xT[s] = x[b0 + s].reshape(S, NK, 128).transpose(2, 1, 0).reshape(128, NK * S)
        pvc[:, PV_COND + s:PV_COND + 16:2] = c[b0 + s].reshape(NK, 128).T
    return xT, pvc


_NC_CACHE = {}


def kernel(**inputs):
    shared, pv = _prep_shared(inputs)
    nc = build_program(2)
    in_maps = []
    for core in range(NCORES):
        xT, pvc = _prep_core(inputs, pv, core)
        m = dict(shared)
        m["xT"] = xT
        m["pvec"] = pvc
        in_maps.append(m)
    res = run_bass_kernel_spmd(nc, in_maps, core_ids=list(range(NCORES)))
    out = np.empty((16, S, D), np.float32)
    for core in range(NCORES):
        o = res.results[core]["out"]
        for s in range(2):
            out[core * 2 + s] = o[s].reshape(128, NK, S).transpose(2, 1, 0).reshape(S, D)
    return out
```

```python
import numpy as np
from contextlib import ExitStack
import concourse.bass as bass
import concourse.mybir as mybir
from concourse.bass_utils import run_bass_kernel_spmd

F32 = mybir.dt.float32
BF16 = mybir.dt.bfloat16
AF = mybir.ActivationFunctionType
ALU = mybir.AluOpType
AX = mybir.AxisListType

NCORES = 8
D = 1024
S = 2048
DFF = 2816
NK = 8
NT = 4
NB = 16
TT = 512
HC = 11
NSLOT = 7
EPS = 1e-6

W8_ADA = 0
W8_F1 = 72
W8_F2 = 116
W8_IN = 160
W8_OUT = 185
W8_N = 193

PV_BADA = 0
PV_G1 = 72
PV_GQ = 96
PV_GK = 97
PV_SINK = 98
PV_COND = 102
PV_N = 118

SC_U = 0
SC_VLN = 8192
SC_Q = 16384
SC_KT = 24576
SC_V = 26624
SC_WTM = 28672
SC_N = 34816
MG_OFF = [8192, 10240, 12288, 14336, 24576, 26624, 28672, 30720]


class _Eng:
    def __init__(self):
        self.ops = []
        self.cnt = 0
        self.seen = {}
        self.sem = None


class Prog:
    def __init__(self, nc, es):
        self.nc = nc
        self.es = es
        self.E = {n: _Eng() for n in ("pe", "act", "dve", "pool", "sp")}
        for n, e in self.E.items():
            e.sem = es.enter_context(nc.semaphore("s_" + n))
        self.lastw = {}
        self.readers = {}
        self.nsem = 0
        self.last_pe = None

    def newsem(self, name):
        return self.es.enter_context(self.nc.semaphore(name))

    def _waits(self, e, deps):
        waits = {}
        for d in deps:
            if d is None:
                continue
            s, v = d
            if v > waits.get(s, 0):
                waits[s] = v
        wl = []
        for s, v in waits.items():
            if e.seen.get(s, 0) >= v:
                continue
            e.seen[s] = v
            wl.append((s, v))
        return wl

    def _deps(self, e, R, W, extra):
        deps = list(extra)
        for k in R:
            t = self.lastw.get(k)
            if t is not None:
                deps.append(t)
        pe_sem = self.E["pe"].sem
        for k in W:
            t = self.lastw.get(k)
            if t is not None and not (t[0] is e.sem and e.sem is pe_sem):
                deps.append(t)
            rd = self.readers.get(k)
            if rd:
                for s, v in rd.items():
                    if not (s is e.sem and e.sem is pe_sem):
                        deps.append((s, v))
        return deps

    def _record(self, tok, R, W):
        for k in R:
            rd = self.readers.setdefault(k, {})
            if rd.get(tok[0], 0) < tok[1]:
                rd[tok[0]] = tok[1]
        for k in W:
            self.lastw[k] = tok
            self.readers[k] = {}

    def op(self, en, fn, R=(), W=(), extra=()):
        e = self.E[en]
        wl = self._waits(e, self._deps(e, R, W, extra))
        e.cnt += 1
        tok = (e.sem, e.cnt)
        sem_e = e.sem

        def thunk(eng, wl=wl, fn=fn):
            for s, v in wl:
                eng.wait_ge(s, v)
            fn(eng).then_inc(sem_e, 1)
        e.ops.append(thunk)
        self._record(tok, R, W)
        if en == "pe":
            self.last_pe = tok
        return tok

    def dma(self, en, out, in_, semc, R=(), W=(), extra=()):
        e = self.E[en]
        wl = self._waits(e, self._deps(e, R, W, extra))
        semc[1] += 16
        tok = (semc[0], semc[1])
        sem = semc[0]

        def thunk(eng, wl=wl):
            for s, v in wl:
                eng.wait_ge(s, v)
            eng.dma_start(out=out, in_=in_).then_inc(sem, 16)
        e.ops.append(thunk)
        self._record(tok, R, W)
        return tok

    def wait(self, en, toks):
        e = self.E[en]
        wl = self._waits(e, toks)

        def thunk(eng, wl=wl):
            for s, v in wl:
                eng.wait_ge(s, v)
        e.ops.append(thunk)

    def run(self):
        nc = self.nc
        E = self.E
        with nc.Block() as block:
            @block.tensor
            def _(eng):
                for t in E["pe"].ops:
                    t(eng)

            @block.scalar
            def _(eng):
                for t in E["act"].ops:
                    t(eng)

            @block.vector
            def _(eng):
                for t in E["dve"].ops:
                    t(eng)

            @block.gpsimd
            def _(eng):
                for t in E["pool"].ops:
                    t(eng)

            @block.sync
            def _(eng):
                for t in E["sp"].ops:
                    t(eng)


class Rot:
    def __init__(self, items):
        self.items = items
        self.i = 0

    def next(self):
        x = self.items[self.i % len(self.items)]
        self.i += 1
        return x


def build_program(nseq=2, stop=99):
    nc = bass.Bass("TRN2", target_bir_lowering=False)
    dt = lambda n, s: nc.dram_tensor(n, s, F32, kind="ExternalInput").ap()
    xT_d = dt("xT", [nseq, 128, NK * S])
    w8_d = dt("w8", [W8_N, 128, 1024])
    w11_d = dt("w11", [32, 128, HC * 128])
    w4_d = dt("w4", [16, 128, 512])
    wtm_d = dt("wtm", [128, 8 * 768])
    pvec_d = dt("pvec", [128, PV_N])
    lngb_d = dt("lngb", [128, 1024])
    wst_d = dt("wst", [128, 512])
    bsr_d = dt("bsr", [1, 512])
    cmat_d = dt("cmat", [128, 512])
    out_d = nc.dram_tensor("out", [nseq, 128, NK * S], F32, kind="ExternalOutput").ap()

    with ExitStack() as es:
        P = Prog(nc, es)
        sb = lambda n, s, d: es.enter_context(nc.sbuf_tensor(n, s, d))
        h = sb("h", [128, NK, S], F32)
        xn = sb("xn", [128, NK, S], BF16)
        scr = sb("scr", [128, SC_N], BF16)
        ringbuf = sb("ringbuf", [128, NSLOT * 1024], BF16)
        ring = [ringbuf[:, i * 1024:(i + 1) * 1024] for i in range(NSLOT)]
        sqr = [sb(f"sq{i}", [128, TT], BF16) for i in range(6)]
        rsr = [sb(f"rs{i}", [128, TT], F32) for i in range(2)]
        tfr = [sb(f"tf{i}", [128, TT], F32) for i in range(4)]
        ksqr = [sb(f"ksq{i}", [128, 128], F32) for i in range(1)]
        cmat = sb("cmat_sb", [128, 512], BF16)
        pvec = sb("pvec_sb", [128, PV_N], F32)
        lngb = sb("lngb_sb", [128, 1024], F32)
        wst = sb("wst_sb", [128, 512], BF16)
        bsr = sb("bsr_sb", [1, 512], BF16)
        cond = sb("cond", [128, 16], BF16)
        mods = sb("mods", [128, 144], F32)
        Asb = sb("Asb", [128, 48], F32)
        Gsb = sb("Gsb", [128, 48], F32)
        gqk = sb("gqk", [128, 1], F32)
        esink = sb("esink", [128, 4], F32)
        epst = sb("epst", [128, 2], F32)
        st6 = sb("st6", [128, NB * 6], F32)
        mv = sb("mv", [128, NB * 2], F32)
        ssk = sb("ssk", [128, NB * 2], F32)
        rstdv = sb("rstdv", [128, NB], F32)
        rk8 = sb("rk8", [128, NB * 2], F32)
        nmr = sb("nmr", [128, NB], F32)
        bank = [es.enter_context(nc.psum_tensor(f"bank{i}", [128, TT], F32)) for i in range(8)]

        ones = cmat[:, 0:128]
        bd = cmat[:, 128:256]
        mprev = cmat[:, 256:384]
        mcur = cmat[:, 384:512]

        hT = scr[:, 0:HC * S].rearrange("p (k t) -> p k t", t=S)
        u3 = scr[:, SC_U:SC_U + 4 * S].rearrange("p (g t) -> p g t", t=S)
        vln = scr[:, SC_VLN:SC_VLN + NB * 512].rearrange("p (n c) -> p n c", c=512)
        q3 = scr[:, SC_Q:SC_Q + 4 * S].rearrange("p (g t) -> p g t", t=S)
        kT = scr[:, SC_KT:SC_KT + S]
        vtm = scr[:, SC_V:SC_V + NB * 128].rearrange("p (n c) -> p n c", c=128)
        wtm = scr[:, SC_WTM:SC_WTM + 8 * 768].rearrange("p (k c) -> p k c", c=768)
        PT = [scr[:, SC_WTM + i * 2048:SC_WTM + (i + 1) * 2048].rearrange("p (j t) -> p j t", t=512) for i in range(2)]
        mg = [scr[:, o:o + S] for o in MG_OFF]

        ring_sem = [[P.newsem(f"ringsem{i}"), 0] for i in range(NSLOT)]
        ld_sem = [[P.newsem(f"ldsem{i}"), 0] for i in range(2 * NK)]
        st_sem = [[P.newsem(f"stsem{i}"), 0] for i in range(2 * NK)]
        misc_sem = [[P.newsem(f"miscsem{i}"), 0] for i in range(7)]
        ring_i = [0]
        ring_gen = [0] * NSLOT

        class RW:
            pass

        def ring_load(src, nk):
            i = ring_i[0] % NSLOT
            ring_i[0] += 1
            ring_gen[i] += 1
            dst = ringbuf[:, i * 1024:i * 1024 + nk * 128]
            tok = P.dma("pool", dst, src, ring_sem[i], W=[("ring", i)])
            r = RW()
            r.tok = tok
            r.v = dst.rearrange("p (k c) -> p k c", c=128)
            r.i = i
            r.gen = ring_gen[i]
            return r

        def rk(r):
            assert ring_gen[r.i] == r.gen, "ring slot reloaded before use"
            return ("ring", r.i)

        def ring_load2(src, nk):
            i = ring_i[0] % NSLOT
            if i == NSLOT - 1:
                ring_i[0] += 1
                i = 0
            ring_i[0] += 2
            ring_gen[i] += 1
            ring_gen[i + 1] += 1
            dst = ringbuf[:, i * 1024:i * 1024 + nk * 128]
            tok = P.dma("pool", dst, src, ring_sem[i], W=[("ring", i), ("ring", i + 1)])
            r = RW()
            r.tok = tok
            r.v = dst.rearrange("p (k c) -> p k c", c=128)
            r.i = i
            r.gen = ring_gen[i]
            r.gen2 = ring_gen[i + 1]
            return r

        def rk2(r):
            assert ring_gen[r.i] == r.gen and ring_gen[r.i + 1] == r.gen2, "ring slot pair reloaded before use"
            return [("ring", r.i), ("ring", r.i + 1)]

        class BG:
            def __init__(self):
                self.steps = []

            def add(self, fns):
                self.steps.extend(fns)

            def step(self, n=1):
                for _ in range(n):
                    if self.steps:
                        self.steps.pop(0)()

            def flush(self):
                while self.steps:
                    self.steps.pop(0)()
        bg = BG()

        sq_rot = Rot([0, 1, 2, 3])
        qsq_rot = Rot([4, 5])
        rs_rot = Rot(list(range(2)))
        tf_rot = Rot(list(range(4)))
        ksq_rot = Rot([0])

        def sl(tt):
            return slice(tt * TT, (tt + 1) * TT)

        def bl(n):
            return slice(n * 128, (n + 1) * 128)

        def gemm8(B, w, tt, src=None):
            def f(e):
                for k in range(NK):
                    ins = e.matmul(bank[B][:], w.v[:, k, :], xn[:, k, sl(tt)], start=(k == 0), stop=(k == NK - 1))
                return ins
            return P.op("pe", f, R=[rk(w)] + [("xn", k, tt) for k in range(NK)], W=[("ps", B)])

        P.dma("sp", pvec[:], pvec_d, misc_sem[0], W=[("pvec",)])
        P.dma("sp", lngb[:], lngb_d, misc_sem[1], W=[("lngb",)])
        P.dma("sp", tfr[3][:], wst_d, misc_sem[2], W=[("tf", 3)])
        P.dma("pool", cmat[:], cmat_d, misc_sem[3], W=[("cmat",)])
        P.dma("pool", bsr[:], bsr_d, misc_sem[4], W=[("bsr",)])
        P.op("dve", lambda e: e.memset(epst[:, 0:1], EPS), W=[("eps0",)])
        P.op("dve", lambda e: e.memset(epst[:, 1:2], 64 * EPS), W=[("eps1",)])
        P.op("dve", lambda e: e.tensor_tensor(wst[:].rearrange("p (g t) -> p g t", g=4), tfr[3][:].rearrange("p (g t) -> p g t", g=4),
                                              mcur.unsqueeze(1).broadcast_to([128, 4, 128]), ALU.mult),
             R=[("tf", 3), ("cmat",)], W=[("wst",)])
        P.op("dve", lambda e: e.tensor_tensor(gqk[:], pvec[:, PV_GQ:PV_GQ + 1], pvec[:, PV_GK:PV_GK + 1], ALU.mult),
             R=[("pvec",)], W=[("gqk",)])
        P.op("act", lambda e: e.activation(esink[:], pvec[:, PV_SINK:PV_SINK + 4], AF.Exp), R=[("pvec",)], W=[("esink",)])
        P.op("act", lambda e: e.activation(cond[:], pvec[:, PV_COND:PV_COND + 16], AF.Silu), R=[("pvec",)], W=[("cond",)])

        cond3 = cond[:].rearrange("p (k b) -> p k b", b=2)
        mods3 = mods[:].rearrange("p (j b) -> p j b", b=2)

        def mods_chunk(j):
            w = ring_load(w8_d[W8_ADA + j], 8)

            def f(e):
                for k in range(8):
                    ins = e.matmul(bank[7][:, 2 * j:2 * j + 2], w.v[:, k, :], cond3[:, k, :], start=(k == 0), stop=(k == 7))
                return ins
            P.op("pe", f, R=[rk(w), ("cond",)], W=[("ps7", j)])
            return w.tok

        def mods_finish(j0, j1, As, Gs):
            P.op("dve", lambda e: e.tensor_tensor(mods3[:, j0:j1, :],
                                                  bank[7][:, 2 * j0:2 * j1].rearrange("p (j b) -> p j b", b=2),
                                                  pvec[:, PV_BADA + j0:PV_BADA + j1].unsqueeze(2).broadcast_to([128, j1 - j0, 2]), ALU.add),
                 R=[("ps7", j) for j in range(j0, j1)] + [("pvec",)], W=[("mods", j) for j in range(j0, j1)])
            for s3 in As:
                for b in range(2):
                    o = (s3 * 2 + b) * 8
                    P.op("dve", lambda e, s3=s3, b=b, o=o: e.scalar_tensor_tensor(
                        Asb[:, o:o + 8], mods3[:, (3 * s3 + 1) * 8:(3 * s3 + 2) * 8, b], 1.0,
                        pvec[:, PV_G1 + 8 * s3:PV_G1 + 8 * s3 + 8], ALU.add, ALU.mult),
                        R=[("mods", j) for j in range((3 * s3 + 1) * 8, (3 * s3 + 2) * 8)] + [("pvec",)], W=[("A", s3, b)])
            for s3 in Gs:
                for b in range(2):
                    o = (s3 * 2 + b) * 8
                    P.op("dve", lambda e, s3=s3, b=b, o=o: e.tensor_scalar(
                        Gsb[:, o:o + 8], mods3[:, (3 * s3 + 2) * 8:(3 * s3 + 3) * 8, b], (1.0 if s3 == 1 else 0.5), None, ALU.mult),
                        R=[("mods", j) for j in range((3 * s3 + 2) * 8, (3 * s3 + 3) * 8)], W=[("G", s3, b)])

        mods_bg = ([(lambda j=j: mods_chunk(j)) for j in range(16, 24)] + [lambda: mods_finish(16, 24, [], [0])] +
                   [(lambda j=j: mods_chunk(j)) for j in range(24, 72)] + [lambda: mods_finish(24, 72, [1, 2], [1, 2])])

        def Acol(s3, b, k):
            o = (s3 * 2 + b) * 8 + k
            return Asb[:, o:o + 1]

        def Gcol(s3, b, k):
            o = (s3 * 2 + b) * 8 + k
            return Gsb[:, o:o + 1]

        def Bcol(s3, b, k):
            return mods3[:, 3 * s3 * 8 + k, b:b + 1]

        def mk_ln_step(tt, box, bk):
            def f():
                r = rs_rot.next()
                box["r"] = r
                P.op("act", lambda e: e.activation(rsr[r][:], bank[bk][:], AF.Ln, bias=epst[:, 0:1], scale=1.0 / D),
                     R=[("ps", bk), ("eps0",)], W=[("rs", r)])
                P.op("act", lambda e: e.activation(rsr[r][:], rsr[r][:], AF.Exp, scale=-0.5), R=[("rs", r)], W=[("rs", r)])
            return f

        def mk_aff_step(s3, b, k0, tt, box, dve_only=False):
            def f():
                r = box["r"]
                for k in (k0, k0 + 1):
                    t = tf_rot.next()
                    P.op("dve", lambda e, t=t, k=k: e.scalar_tensor_tensor(tfr[t][:], h[:, k, sl(tt)], Acol(s3, b, k), rsr[r][:],
                                                                       ALU.mult, ALU.mult),
                         R=[("h", k, tt), ("rs", r), ("A", s3, b)], W=[("tf", t)])
                    if dve_only:
                        P.op("dve", lambda e, t=t, k=k: e.tensor_scalar(xn[:, k, sl(tt)], tfr[t][:], Bcol(s3, b, k), None, ALU.add),
                             R=[("tf", t), ("mods", 3 * s3 * 8 + k)], W=[("xn", k, tt)])
                    else:
                        P.op("act", lambda e, t=t, k=k: e.activation(xn[:, k, sl(tt)], tfr[t][:], AF.Identity, bias=Bcol(s3, b, k), scale=1.0),
                             R=[("tf", t), ("mods", 3 * s3 * 8 + k)], W=[("xn", k, tt)])
            return f

        def emit_square(i, k, tt):
            if k % 2 == 0:
                P.op("act", lambda e: e.activation(sqr[i][:], h[:, k, sl(tt)], AF.Square), R=[("h", k, tt)], W=[("sq", i)])
            else:
                P.op("dve", lambda e: e.tensor_tensor(sqr[i][:], h[:, k, sl(tt)], h[:, k, sl(tt)], ALU.mult), R=[("h", k, tt)], W=[("sq", i)])

        def norm_steps(s3, b, tiles, bk=6):
            steps = []
            boxes = {tt: {} for tt in tiles}
            for tt in tiles:
                pend = []

                def emit_mm1(tt=tt, pend=pend):
                    i, k = pend.pop(0)
                    P.op("pe", lambda e: e.matmul(bank[bk][:], ones, sqr[i][:], start=(k == 0), stop=(k == NK - 1)),
                         R=[("sq", i), ("cmat",)], W=[("ps", bk)])

                def sq_step(k, tt=tt, pend=pend, emit_mm1=emit_mm1):
                    def f():
                        if len(pend) >= 2:
                            emit_mm1()
                        i = sq_rot.next()
                        emit_square(i, k, tt)
                        pend.append((i, k))
                    return f
                for k in range(NK):
                    steps.append(sq_step(k))
                steps.append(emit_mm1)
                steps.append(emit_mm1)
                steps.append(mk_ln_step(tt, boxes[tt], bk))
            for tt in tiles:
                for k0 in range(0, NK, 2):
                    steps.append(mk_aff_step(s3, b, k0, tt, boxes[tt]))
            return steps

        class NormInc:
            def __init__(self, s3, b, tiles, banks):
                self.s3, self.b, self.tiles = s3, b, tiles
                self.B = dict(zip(tiles, banks))
                self.pend = []
                self.cnt = {tt: 0 for tt in tiles}

            def _mm(self):
                while self.pend:
                    i, k, tt = self.pend.pop(0)
                    n = self.cnt[tt]
                    self.cnt[tt] += 1
                    bk = self.B[tt]
                    P.op("pe", lambda e, i=i, n=n, bk=bk: e.matmul(bank[bk][:], ones, sqr[i][:], start=(n == 0), stop=(n == NK - 1)),
                         R=[("sq", i), ("cmat",)], W=[("ps", bk)])

            def feed(self, k, tt):
                self._mm()
                i = qsq_rot.next()
                emit_square(i, k, tt)
                self.pend.append((i, k, tt))

            def finish_steps(self):
                steps = [self._mm]
                boxes = {tt: {} for tt in self.tiles}
                for tt in self.tiles:
                    steps.append(mk_ln_step(tt, boxes[tt], self.B[tt]))
                na = 0
                for tt in self.tiles:
                    for k0 in range(0, NK, 2):
                        steps.append(mk_aff_step(self.s3, self.b, k0, tt, boxes[tt], dve_only=(self.s3 == 1 and na < 2)))
                        na += 1
                return steps

        TA = (0, 1)
        TB = (2, 3)

        def ffn(f, b, tailA, tailB, extra_bg=None, inc=None, after_dc=None):
            s3 = 0 if f == 0 else 2
            gb = Rot([0, 1, 4])
            ub = Rot([2, 3, 5])
            db = Rot([4, 5, 0, 1, 2])
            base8 = W8_F1 if f == 0 else W8_F2

            def g1_iter(wg, wu, hc, tt):
                G = gb.next()
                U = ub.next()
                gemm8(G, wg, tt)
                gemm8(U, wu, tt)
                t = tf_rot.next()
                P.op("act", lambda e: e.activation(tfr[t][:], bank[G][:], AF.Silu), R=[("ps", G)], W=[("tf", t)])
                P.op("dve", lambda e: e.tensor_tensor(hT[:, hc, sl(tt)], tfr[t][:], bank[U][:], ALU.mult),
                     R=[("tf", t), ("ps", U)], W=[("hT", hc, tt)])

            def wd_load(half, dc):
                return ring_load2(w11_d[(f * 2 + half) * 8 + dc], HC)

            def g2_iter(wd, dc, tt):
                assert ("G", s3, b) in P.lastw
                Dk = db.next()

                def fd(e):
                    for kk in range(HC):
                        ins = e.matmul(bank[Dk][:], wd.v[:, kk, :], hT[:, kk, sl(tt)], start=(kk == 0), stop=(kk == HC - 1))
                    return ins
                P.op("pe", fd, R=rk2(wd) + [("hT", kk, tt) for kk in range(HC)], W=[("ps", Dk)])
                P.op("dve", lambda e: e.scalar_tensor_tensor(
                    h[:, dc, sl(tt)], bank[Dk][:], Gcol(s3, b, dc), h[:, dc, sl(tt)], ALU.mult, ALU.add),
                    R=[("ps", Dk), ("h", dc, tt), ("G", s3, b)], W=[("h", dc, tt)])

            for half in range(2):
                hcs = list(range(HC))
                if half == 0:
                    NST = 3
                    ws = []
                    for hc in range(NST):
                        ws.append((ring_load(w8_d[base8 + hc * 2], 8), ring_load(w8_d[base8 + hc * 2 + 1], 8)))
                    for tt in TA:
                        for hc in range(NST):
                            bg.step(5)
                            g1_iter(ws[hc][0], ws[hc][1], hc, tt)
                    bg.flush()
                    for tt in TB:
                        for hc in range(NST):
                            g1_iter(ws[hc][0], ws[hc][1], hc, tt)
                    hcs = list(range(NST, HC))
                    if extra_bg:
                        bg.add(extra_bg)
                for hc in hcs:
                    ghc = half * HC + hc
                    wg = ring_load(w8_d[base8 + ghc * 2], 8)
                    wu = ring_load(w8_d[base8 + ghc * 2 + 1], 8)
                    for tt in range(NT):
                        g1_iter(wg, wu, hc, tt)
                        if tt % 2 == 1 or (tt == 2 and len(bg.steps) > 24):
                            bg.step(1)
                if half == 0:
                    for dc in range(NK):
                        wd = wd_load(half, dc)
                        for tt in range(NT):
                            g2_iter(wd, dc, tt)
                            if tt % 2 == 1:
                                bg.step(1)
                else:
                    bg.flush()
                    for tiles, tail in ((TA, tailA), (TB, tailB)):
                        ninc = NormInc(inc, b, TB, (6, 7)) if (inc is not None and tiles is TB) else None
                        for dc in range(NK):
                            wd = wd_load(half, dc)
                            for tt in tiles:
                                g2_iter(wd, dc, tt)
                                if ninc:
                                    ninc.feed(dc, tt)
                                bg.step(2)
                            if after_dc:
                                after_dc(dc, 0 if tiles is TA else 1)
                        if after_dc:
                            after_dc(NK, 0 if tiles is TA else 1)
                            after_dc(NK + 1, 0 if tiles is TA else 1)
                        bg.flush()
                        if ninc:
                            bg.add(ninc.finish_steps())
                        elif tail:
                            bg.add(tail())

        def mixer(b, tailA, tailB, inc=None):
            ob = Rot([0, 1, 2, 4, 5])
            ob3 = Rot([0, 1, 4, 5])
            sb2 = Rot([2, 3])
            pend = []

            def q_stage2(g, tt, B, i):
                Sq = sb2.next()
                P.op("pe", lambda e: e.matmul(bank[Sq][:], bd, sqr[i][:], start=True, stop=True),
                     R=[("sq", i), ("cmat",)], W=[("ps", Sq)])
                r = tf_rot.next()
                P.op("act", lambda e: e.activation(tfr[r][:], bank[Sq][:], AF.Ln, bias=epst[:, 0:1], scale=1.0 / 64),
                     R=[("ps", Sq), ("eps0",)], W=[("tf", r)])
                P.op("act", lambda e: e.activation(tfr[r][:], tfr[r][:], AF.Exp, scale=-0.5), R=[("tf", r)], W=[("tf", r)])
                P.op("dve", lambda e: e.scalar_tensor_tensor(q3[:, g, sl(tt)], bank[B][:], gqk[:, 0:1], tfr[r][:], ALU.mult, ALU.mult),
                     R=[("ps", B), ("tf", r), ("gqk",)], W=[("q", g, n) for n in range(tt * 4, tt * 4 + 4)])

            wq = [ring_load(w8_d[W8_IN + g], 8) for g in range(5)]

            def m2a(tiles, nstep):
                it = [0]

                def bstep():
                    it[0] += 1
                    if nstep >= 1:
                        bg.step(nstep)
                    elif nstep > 0 and it[0] % 2 == 1:
                        bg.step(1)
                for tt in tiles:
                    for g in range(5):
                        B = ob3.next()
                        gemm8(B, wq[g], tt)
                        if g < 4:
                            i = qsq_rot.next()
                            P.op("act", lambda e, i=i, B=B: e.activation(sqr[i][:], bank[B][:], AF.Square), R=[("ps", B)], W=[("sq", i)])
                            bstep()
                            if pend:
                                q_stage2(*pend.pop())
                            pend.append((g, tt, B, i))
                        else:
                            P.op("act", lambda e, B=B, tt=tt: e.activation(kT[:, sl(tt)], bank[B][:], AF.Copy), R=[("ps", B)],
                                 W=[("kT", n) for n in range(tt * 4, tt * 4 + 4)])
                            bstep()
                q_stage2(*pend.pop())

            avb = Rot([0, 1])
            kvb = Rot([2, 3])
            st3 = st6[:].rearrange("p (n c) -> p n c", c=6)
            mv3 = mv[:].rearrange("p (n c) -> p n c", c=2)
            ssk3 = ssk[:].rearrange("p (n c) -> p n c", c=2)
            rk83 = rk8[:].rearrange("p (n c) -> p n c", c=2)
            def m2b(blocks, nstep):
              for n in blocks:
                tt = n // 4
                AV = avb.next()
                KV = kvb.next()

                def ftm(e, AV=AV, KV=KV, n=n):
                    for k in range(NK):
                        e.matmul(bank[AV][:], xn[:, k, bl(n)], wtm[:, k, 0:512], start=(k == 0), stop=(k == NK - 1))
                        ins = e.matmul(bank[KV][:, 0:256], xn[:, k, bl(n)], wtm[:, k, 512:768], start=(k == 0), stop=(k == NK - 1))
                    return ins
                P.op("pe", ftm, R=[("wtm",)] + [("xn", k, tt) for k in range(NK)], W=[("ps", AV), ("ps", KV)])
                P.op("act", lambda e, AV=AV, n=n: e.activation(vln[:, n, :], bank[AV][:], AF.Gelu_apprx_tanh), R=[("ps", AV)], W=[("vln", n)])
                ki = ksq_rot.next()
                P.op("act", lambda e, KV=KV, ki=ki: e.activation(ksqr[ki][:], bank[KV][:, 0:128], AF.Square), R=[("ps", KV)], W=[("ksq", ki)])
                P.op("act", lambda e, KV=KV, n=n: e.activation(vtm[:, n, :], bank[KV][:, 128:256], AF.Copy), R=[("ps", KV)], W=[("v", n)])
                P.op("dve", lambda e, n=n: e.bn_stats(st3[:, n, :], vln[:, n, :]), R=[("vln", n)], W=[("st", n)])
                P.op("dve", lambda e, n=n: e.bn_aggr(mv3[:, n, :], st3[:, n, :]), R=[("st", n)], W=[("mv", n)])
                P.op("dve", lambda e, n=n, ki=ki: e.reduce_sum(ssk3[:, n, :], ksqr[ki][:].rearrange("p (h d) -> p h d", h=2), AX.X),
                     R=[("ksq", ki)], W=[("ssk", n)])
                bg.step(nstep)

            m2a(TA, 0.5)
            m2b(range(0, NB // 2), 1)
            bg.flush()
            m2a(TB, 0)
            m2b(range(NB // 2, NB), 0)

            P.op("act", lambda e: e.activation(rstdv[:], mv3[:, :, 1], AF.Ln, bias=epst[:, 0:1], scale=1.0),
                 R=[("mv", n) for n in range(NB)] + [("eps0",)], W=[("rstdv",)])
            P.op("act", lambda e: e.activation(rstdv[:], rstdv[:], AF.Exp, scale=-0.5), R=[("rstdv",)], W=[("rstdv",)])
            P.op("act", lambda e: e.activation(rk8[:], ssk[:], AF.Ln, bias=epst[:, 1:2], scale=1.0),
                 R=[("ssk", n) for n in range(NB)] + [("eps1",)], W=[("rk8",)])
            P.op("act", lambda e: e.activation(rk8[:], rk8[:], AF.Exp, scale=-0.5), R=[("rk8",)], W=[("rk8",)])

            P.op("dve", lambda e: e.scalar_tensor_tensor(nmr[:], mv3[:, :, 0], -1.0, rstdv[:], ALU.mult, ALU.mult),
                 R=[("mv", n) for n in range(NB)] + [("rstdv",)], W=[("nmr",)])

            def ln_fin(n):
                def f():
                    P.op("act", lambda e: e.activation(vln[:, n, :], vln[:, n, :], AF.Identity, bias=nmr[:, n:n + 1], scale=rstdv[:, n:n + 1]),
                         R=[("vln", n), ("nmr",), ("rstdv",)], W=[("vln", n)])
                    P.op("dve", lambda e: e.tensor_tensor(vln[:, n, :], vln[:, n, :], lngb[:, 0:512], ALU.mult),
                         R=[("vln", n), ("lngb",)], W=[("vln", n)])
                    P.op("dve", lambda e: e.tensor_tensor(vln[:, n, :], vln[:, n, :], lngb[:, 512:1024], ALU.add),
                         R=[("vln", n), ("lngb",)], W=[("vln", n)])
                return f
            bg.add([ln_fin(n) for n in range(NB)])

            zb = Rot([4, 5])
            ub4 = Rot([0, 1, 2, 3])
            wst3 = wst[:].rearrange("p (g t) -> p g t", g=4)

            def sgu(n):
                assert ("vln", n) in P.lastw
                Z = zb.next()

                def fz(e):
                    e.matmul(bank[Z][:, 0:512], cmat[0:1, 0:128], bsr[0:1, 0:512], start=True, stop=False)
                    for g in range(4):
                        ins = e.matmul(bank[Z][:, g * 128:(g + 1) * 128], vln[:, n, g * 128:(g + 1) * 128], wst3[:, g, :], start=False, stop=(g == 3))
                    return ins
                P.op("pe", fz, R=[("vln", n), ("wst",), ("bsr",), ("cmat",)], W=[("ps", Z)])
                P.op("dve", lambda e: e.tensor_tensor(u3[:, :, bl(n)], bank[Z][:].rearrange("p (g t) -> p g t", g=4), u3[:, :, bl(n)], ALU.mult),
                     R=[("ps", Z)] + [("u", g, n) for g in range(4)], W=[("u", g, n) for g in range(4)])

            wu4 = [ring_load(w8_d[W8_IN + 5 + g], 8) for g in range(4)]
            for tt in range(NT):
                for g in range(4):
                    B = ub4.next()
                    gemm8(B, wu4[g], tt)
                    P.op("act", lambda e, B=B, g=g, tt=tt: e.activation(u3[:, g, sl(tt)], bank[B][:], AF.Gelu_apprx_tanh), R=[("ps", B)],
                         W=[("u", g, n) for n in range(tt * 4, tt * 4 + 4)])
                    bg.step(1)
                    if tt >= 1:
                        sgu((tt - 1) * 4 + g)
            bg.flush()
            for n in range(NB - 4, NB):
                sgu(n)

            def scores(n):
                kbs = [n - 1, n] if n > 0 else [n]
                slot = n % 2
                for hh in range(2):
                    for kb in kbs:
                        c = 1 if kb == n else 0
                        Sb = hh * 2 + c
                        j = hh * 2 + c

                        def fs(e, hh=hh, kb=kb, Sb=Sb):
                            return e.matmul(bank[Sb][:].rearrange("p (g t) -> p g t", g=4), kT[hh * 64:(hh + 1) * 64, bl(kb)],
                                            q3[hh * 64:(hh + 1) * 64, :, bl(n)], start=True, stop=True)
                        P.op("pe", fs, R=[("kT", kb)] + [("q", g, n) for g in range(4)], W=[("ps", Sb)])
                        P.op("act", lambda e, Sb=Sb, j=j, kb=kb, hh=hh: e.activation(PT[slot][:, j, :], bank[Sb][:], AF.Exp, scale=rk83[:, kb, hh:hh + 1]),
                             R=[("ps", Sb), ("rk8",)], W=[("PT", slot, j)])
                        m = mcur if kb == n else mprev
                        P.op("dve", lambda e, j=j, m=m: e.tensor_tensor(PT[slot][:, j, :].rearrange("p (g t) -> p g t", g=4),
                                                                    PT[slot][:, j, :].rearrange("p (g t) -> p g t", g=4),
                                                                    m.unsqueeze(1).broadcast_to([128, 4, 128]), ALU.mult),
                             R=[("PT", slot, j), ("cmat",)], W=[("PT", slot, j)])

            def pv(n):
                kbs = [n - 1, n] if n > 0 else [n]
                slot = n % 2
                O = 4 + (n % 2)
                Dn = 6 + (n % 2)
                for hh in range(2):
                    def fo(e, hh=hh):
                        for ci, kb in enumerate(kbs):
                            j = hh * 2 + (1 if kb == n else 0)
                            ins = e.matmul(bank[O][hh * 64:(hh + 1) * 64, :], vtm[:, kb, hh * 64:(hh + 1) * 64], PT[slot][:, j, :],
                                           start=(ci == 0), stop=(ci == len(kbs) - 1))
                        return ins
                    rkeys = [("v", kb) for kb in kbs] + [("PT", slot, hh * 2 + (1 if kb == n else 0)) for kb in kbs]
                    P.op("pe", fo, R=rkeys, W=[("ps", O)])

                    def fdn(e, hh=hh):
                        for ci, kb in enumerate(kbs):
                            j = hh * 2 + (1 if kb == n else 0)
                            ins = e.matmul(bank[Dn][hh * 64:(hh + 1) * 64, :], cmat[:, 0:64], PT[slot][:, j, :],
                                           start=(ci == 0), stop=(ci == len(kbs) - 1))
                        return ins
                    P.op("pe", fdn, R=rkeys + [("cmat",)], W=[("ps", Dn)])
            nbox = {}

            def pv_add(n):
                Dn = 6 + (n % 2)
                t = tf_rot.next()
                nbox[n] = t
                t3 = tfr[t][:].rearrange("p (g t) -> p g t", g=4)
                P.op("dve", lambda e: e.tensor_tensor(t3, bank[Dn][:].rearrange("p (g t) -> p g t", g=4),
                                                      esink[:].unsqueeze(2).broadcast_to([128, 4, 128]), ALU.add),
                     R=[("ps", Dn), ("esink",)], W=[("tf", t)])

            def pv_lnexp(n):
                t = nbox[n]
                P.op("act", lambda e: e.activation(tfr[t][:], tfr[t][:], AF.Ln), R=[("tf", t)], W=[("tf", t)])
                P.op("act", lambda e: e.activation(tfr[t][:], tfr[t][:], AF.Exp, scale=-1.0), R=[("tf", t)], W=[("tf", t)])

            def pv_mult(n):
                t = nbox[n]
                O = 4 + (n % 2)
                t3 = tfr[t][:].rearrange("p (g t) -> p g t", g=4)
                P.op("dve", lambda e: e.tensor_tensor(q3[:, :, bl(n)], bank[O][:].rearrange("p (g t) -> p g t", g=4), t3, ALU.mult),
                     R=[("ps", O), ("tf", t)] + [("q", g, n) for g in range(4)], W=[("q", g, n) for g in range(4)])

            scores(0)
            for n in range(NB):
                if n >= 1:
                    pv_add(n - 1)
                if n + 1 < NB:
                    scores(n + 1)
                if n >= 1:
                    pv_lnexp(n - 1)
                    pv_mult(n - 1)
                pv(n)
            pv_add(NB - 1)
            pv_lnexp(NB - 1)
            pv_mult(NB - 1)

            for dc in range(NK):
                wa = ring_load(w4_d[dc], 4)
                wb = ring_load(w4_d[8 + dc], 4)
                wga = ring_load(w8_d[W8_IN + 9 + dc], 8)
                wgb = ring_load(w8_d[W8_IN + 17 + dc], 8)
                for tt in range(NT):
                    base = 4 * ((dc * NT + tt) % 2)
                    Ya, Yb, Ga, Gb = base, base + 1, base + 2, base + 3
                    nr = range(tt * 4, tt * 4 + 4)

                    def fya(e, w=wa, B=Ya, tt=tt):
                        for k in range(4):
                            ins = e.matmul(bank[B][:], w.v[:, k, :], u3[:, k, sl(tt)], start=(k == 0), stop=(k == 3))
                        return ins
                    P.op("pe", fya, R=[rk(wa)] + [("u", g, n) for g in range(4) for n in nr], W=[("ps", Ya)])

                    def fyb(e, w=wb, B=Yb, tt=tt):
                        for k in range(4):
                            ins = e.matmul(bank[B][:], w.v[:, k, :], q3[:, k, sl(tt)], start=(k == 0), stop=(k == 3))
                        return ins
                    P.op("pe", fyb, R=[rk(wb)] + [("q", g, n) for g in range(4) for n in nr], W=[("ps", Yb)])
                    gemm8(Ga, wga, tt)
                    gemm8(Gb, wgb, tt)
                    ta = tf_rot.next()
                    tb = tf_rot.next()
                    P.op("act", lambda e, ta=ta, Ga=Ga: e.activation(tfr[ta][:], bank[Ga][:], AF.Sigmoid), R=[("ps", Ga)], W=[("tf", ta)])
                    P.op("act", lambda e, tb=tb, Gb=Gb: e.activation(tfr[tb][:], bank[Gb][:], AF.Sigmoid), R=[("ps", Gb)], W=[("tf", tb)])
                    P.op("dve", lambda e, ta=ta, Ya=Ya: e.tensor_tensor(tfr[ta][:], bank[Ya][:], tfr[ta][:], ALU.mult),
                         R=[("ps", Ya), ("tf", ta)], W=[("tf", ta)])
                    P.op("dve", lambda e, tb=tb, Yb=Yb: e.tensor_tensor(tfr[tb][:], bank[Yb][:], tfr[tb][:], ALU.mult),
                         R=[("ps", Yb), ("tf", tb)], W=[("tf", tb)])
                    P.op("dve", lambda e, ta=ta, tb=tb, dc=dc, tt=tt: e.tensor_tensor(mg[dc][:, sl(tt)], tfr[ta][:], tfr[tb][:], ALU.add),
                         R=[("tf", ta), ("tf", tb)], W=[("mg", dc, tt)])

            for tiles, tail in ((TA, tailA), (TB, tailB)):
                ninc = NormInc(inc, b, TB, (6, 7)) if (inc is not None and tiles is TB) else None
                for dc in range(NK):
                    wo = ring_load(w8_d[W8_OUT + dc], 8)
                    for tt in tiles:
                        B = ob.next()

                        def fo2(e, w=wo, B=B, tt=tt):
                            for k in range(NK):
                                ins = e.matmul(bank[B][:], w.v[:, k, :], mg[k][:, sl(tt)], start=(k == 0), stop=(k == NK - 1))
                            return ins
                        P.op("pe", fo2, R=[rk(wo)] + [("mg", k, tt) for k in range(NK)], W=[("ps", B)])
                        P.op("dve", lambda e, B=B, dc=dc, tt=tt: e.scalar_tensor_tensor(
                            h[:, dc, sl(tt)], bank[B][:], Gcol(1, b, dc), h[:, dc, sl(tt)], ALU.mult, ALU.add),
                            R=[("ps", B), ("h", dc, tt), ("G", 1, b)], W=[("h", dc, tt)])
                        if ninc:
                            ninc.feed(dc, tt)
                        bg.step(2)
                bg.flush()
                if ninc:
                    bg.add(ninc.finish_steps())
                elif tail:
                    bg.add(tail())

        st_toks = []
        HS = S // 2

        def load_k(b, k, hf, extra=()):
            P.dma("sp", h[:, k, hf * HS:(hf + 1) * HS], xT_d[b, :, k * S + hf * HS:k * S + (hf + 1) * HS], ld_sem[hf * NK + k],
                  W=[("h", k, tt) for tt in (TA if hf == 0 else TB)], extra=extra)

        def store_k(b, k, hf):
            st_toks.append(P.dma("sp", out_d[b, :, k * S + hf * HS:k * S + (hf + 1) * HS], h[:, k, hf * HS:(hf + 1) * HS],
                                 st_sem[hf * NK + k], R=[("h", k, tt) for tt in (TA if hf == 0 else TB)]))

        def swap_dc(b):
            def f(dc, hf):
                if dc < NK:
                    store_k(b, dc, hf)
                if b + 1 < nseq and 0 <= dc - 2 < NK:
                    load_k(b + 1, dc - 2, hf)
            return f

        for k in range(NK):
            load_k(0, k, 0)
        ns = norm_steps(0, 0, TA)
        mtok = None
        for j in range(NSLOT):
            mtok = mods_chunk(j)
        for fstep in ns[0:22]:
            fstep()
        for j in range(NSLOT, 16):
            mtok = mods_chunk(j)
        mods_finish(0, 16, [0], [])
        for k in range(NK):
            load_k(0, k, 1, extra=[mtok])
        for fstep in ns[22:30]:
            fstep()
        bg.add(norm_steps(0, 0, TB))
        mix_last = [P.last_pe]

        def wtm_load():
            P.dma("pool", scr[:, SC_WTM:SC_WTM + 8 * 768], wtm_d, misc_sem[5], W=[("wtm",)], extra=[mix_last[0]])
        for b in range(nseq):
            ffn(0, b, lambda b=b: norm_steps(1, b, TA, bk=3), None, extra_bg=(mods_bg[:9] + [wtm_load] + mods_bg[9:] if b == 0 else None), inc=1)
            mixer(b, lambda b=b: norm_steps(2, b, TA, bk=3), None, inc=2)
            mix_last[0] = P.last_pe
            nxt = b + 1 < nseq
            ffn(1, b, (lambda b=b: norm_steps(0, b + 1, TA)) if nxt else None, (lambda b=b: norm_steps(0, b + 1, TB)) if nxt else None,
                extra_bg=([wtm_load] if nxt else None), after_dc=swap_dc(b))
        bg.flush()
        P.wait("sp", st_toks)
        P.run()
    return nc


def _chunks_fm(W):
    K, N = W.shape
    return np.ascontiguousarray(W.reshape(K // 128, 128, N // 128, 128).transpose(2, 1, 0, 3)).reshape(N // 128, 128, K)


def _prep_shared(inp):
    f = lambda a: np.asarray(a, dtype=np.float32)
    w_ada = f(inp["w_ada"])[0]
    w_in = f(inp["w_in"])[0]
    w8 = np.empty((W8_N, 128, 1024), np.float32)
    w8[W8_ADA:W8_ADA + 72] = _chunks_fm(w_ada)
    for fi, (g, u) in enumerate((("ffn1_w_gate", "ffn1_w_up"), ("ffn2_w_gate", "ffn2_w_up"))):
        base = W8_F1 if fi == 0 else W8_F2
        w8[base:base + 44:2] = _chunks_fm(f(inp[g])[0])
        w8[base + 1:base + 44:2] = _chunks_fm(f(inp[u])[0])
    qcols = []
    for j in range(4):
        qcols += list(range(1024 + j * 64, 1024 + (j + 1) * 64)) + list(range(1024 + (4 + j) * 64, 1024 + (5 + j) * 64))
    w8[W8_IN:W8_IN + 4] = _chunks_fm(w_in[:, qcols])
    w8[W8_IN + 4:W8_IN + 5] = _chunks_fm(w_in[:, 1536:1664])
    w8[W8_IN + 5:W8_IN + 9] = _chunks_fm(w_in[:, 0:512])
    w8[W8_IN + 9:W8_IN + 17] = _chunks_fm(w_in[:, 1792:2816])
    w8[W8_IN + 17:W8_IN + 25] = _chunks_fm(w_in[:, 2816:3840])
    w8[W8_OUT:W8_OUT + 8] = _chunks_fm(f(inp["w_out"])[0])
    w11 = np.empty((32, 128, HC * 128), np.float32)
    for fi, name in enumerate(("ffn1_w_down", "ffn2_w_down")):
        wd = f(inp[name])[0]
        for half in range(2):
            w11[(fi * 2 + half) * 8:(fi * 2 + half) * 8 + 8] = _chunks_fm(wd[half * HC * 128:(half + 1) * HC * 128])
    w4 = np.empty((16, 128, 512), np.float32)
    w4[0:8] = _chunks_fm(f(inp["w_branch_a"])[0])
    wb = f(inp["w_branch_b"])[0]
    rows = []
    for g in range(4):
        for kvh in range(2):
            rows += list(range((kvh * 4 + g) * 64, (kvh * 4 + g + 1) * 64))
    w4[8:16] = _chunks_fm(wb[rows])
    wtm_cols = list(range(512, 1024)) + list(range(1536, 1664)) + list(range(1664, 1792))
    wtm = np.ascontiguousarray(w_in[:, wtm_cols].reshape(8, 128, 768).transpose(1, 0, 2)).reshape(128, 8 * 768)
    lngb = np.empty((128, 1024), np.float32)
    lngb[:, 0:512] = f(inp["g_sgu_ln"])[0][None, :]
    lngb[:, 512:1024] = f(inp["b_sgu_ln"])[0][None, :]
    wsp = f(inp["w_spatial"])[0]
    wst = np.ascontiguousarray(wsp.transpose(2, 0, 1)).reshape(128, 512)
    bsr = np.ascontiguousarray(f(inp["b_spatial"])[0].reshape(1, 512))
    p = np.arange(128)
    cmat = np.zeros((128, 512), np.float32)
    cmat[:, 0:128] = 1.0
    cmat[:, 128:256] = (p[:, None] // 64 == p[None, :] // 64)
    cmat[:, 256:384] = (p[:, None] > p[None, :])
    cmat[:, 384:512] = (p[:, None] <= p[None, :])
    pv = np.zeros((128, PV_N), np.float32)
    pv[:, PV_BADA:PV_BADA + 72] = f(inp["b_ada"])[0].reshape(72, 128).T
    for i, nm in enumerate(("g_norm1", "g_norm2", "g_norm3")):
        pv[:, PV_G1 + 8 * i:PV_G1 + 8 * i + 8] = f(inp[nm])[0].reshape(8, 128).T
    pv[:, PV_GQ] = np.tile(f(inp["g_q"])[0], 2)
    pv[:, PV_GK] = np.tile(f(inp["g_k"])[0], 2)
    sk = f(inp["attn_sinks"])[0]
    for kvh in range(2):
        pv[kvh * 64:(kvh + 1) * 64, PV_SINK:PV_SINK + 4] = sk[kvh * 4:kvh * 4 + 4][None, :]
    return dict(w8=w8, w11=w11, w4=w4, wtm=wtm, lngb=lngb, wst=wst, bsr=bsr, cmat=cmat), pv


def _prep_core(inp, pv, core, nseq=2):
    x = np.asarray(inp["x"], dtype=np.float32)
    c = np.asarray(inp["c"], dtype=np.float32)
    b0 = core * nseq
    xT = np.empty((nseq, 128, NK * S), np.float32)
    pvc = pv.copy()
    for s in range(nseq):
        xT[s] = x[b0 + s].reshape(S, NK, 128).transpose(2, 1, 0).reshape(128, NK * S)
        pvc[:, PV_COND + s:PV_COND + 16:2] = c[b0 + s].reshape(NK, 128).T
    return xT, pvc


_NC_CACHE = {}


def kernel(**inputs):
    shared, pv = _prep_shared(inputs)
    nc = build_program(2)
    in_maps = []
    for core in range(NCORES):
        xT, pvc = _prep_core(inputs, pv, core)
        m = dict(shared)
        m["xT"] = xT
        m["pvec"] = pvc
        in_maps.append(m)
    res = run_bass_kernel_spmd(nc, in_maps, core_ids=list(range(NCORES)))
    out = np.empty((16, S, D), np.float32)
    for core in range(NCORES):
        o = res.results[core]["out"]
        for s in range(2):
            out[core * 2 + s] = o[s].reshape(128, NK, S).transpose(2, 1, 0).reshape(S, D)
    return out
```
